# Optimizing a Trainium2 kernel written in Bass

```python
import jax, jax.numpy as jnp
from jax import lax
import numpy as np

D_MODEL = 1024
BATCH = 4
SEQ = 8192
DEPTH = 1

CTX_LEN = 256
GRID_W = 64
MIX_W = D_MODEL
CONV_W = MIX_W // 2
CONV_K = 31
GLA_HEADS = 4
GLA_DV = MIX_W - CONV_W
GLA_HEAD_DV = GLA_DV // GLA_HEADS
GLA_DK = GLA_DV // 2
GLA_HEAD_DK = GLA_DK // GLA_HEADS
GATE_RANK = 16
GATE_NORM = 16.0
CHUNK = 64
FFN_HIDDEN = ((8 * D_MODEL // 3 + 127) // 128) * 128
FFN_K = 3
N_MOD = 6
EPS = 1e-6
IN_SPLITS = (CONV_W, CONV_W, GLA_DK, GLA_DK, GLA_DV, GLA_DV, GATE_RANK, GATE_RANK)
D_IN = 2 * CONV_W + 2 * GLA_DK + 2 * GLA_DV + 2 * GATE_RANK

kernel_name = "hybrid_conformer_gla_dit_layer"


def _rmsnorm(x, g):
    xf = x.astype(jnp.float32)
    y = xf * lax.rsqrt(jnp.mean(xf * xf, axis=-1, keepdims=True) + EPS)
    return (y * g.astype(jnp.float32)).astype(x.dtype)


def _layernorm(x, g, b):
    xf = x.astype(jnp.float32)
    mu = jnp.mean(xf, axis=-1, keepdims=True)
    var = jnp.mean(jnp.square(xf - mu), axis=-1, keepdims=True)
    y = (xf - mu) * lax.rsqrt(var + EPS) * g.astype(jnp.float32) + b.astype(jnp.float32)
    return y.astype(x.dtype)


def _adaln(cvec, w_mod, b_mod):
    m = jax.nn.silu(cvec) @ w_mod + b_mod
    return jnp.split(m[:, None, :], N_MOD, axis=-1)


def _modulate(h, shift, scale):
    return h * (1.0 + scale) + shift


def _dwconv_seq(x, w, b):
    k, ch = w.shape
    y = lax.conv_general_dilated(x, w.reshape(k, 1, ch).astype(x.dtype), (1,), [(k // 2, k // 2)],
                                 dimension_numbers=('NWC', 'WIO', 'NWC'), feature_group_count=ch)
    return y + b


def _dwconv_grid(xg, w, axis):
    k, ch = w.shape
    if axis == 1:
        kern, pad = w.reshape(k, 1, 1, ch), [(k // 2, k // 2), (0, 0)]
    else:
        kern, pad = w.reshape(1, k, 1, ch), [(0, 0), (k // 2, k // 2)]
    return lax.conv_general_dilated(xg, kern.astype(xg.dtype), (1, 1), pad,
                                    dimension_numbers=('NHWC', 'HWIO', 'NHWC'), feature_group_count=ch)


def _split_proj(h, p):
    idx = np.cumsum(IN_SPLITS)[:-1].tolist()
    return jnp.split(h @ p['w_in'], idx, axis=-1)


def _conv_module(u, gate, p, grid):
    glu = u * jax.nn.sigmoid(gate)
    if grid:
        bsz, length, _ = glu.shape
        rows = length // GRID_W
        g = glu.reshape(bsz, rows, GRID_W, CONV_W)
        half = CONV_W // 2
        y = jnp.concatenate([_dwconv_grid(g[..., :half], p['conv_dw'][:, :half], 2),
                             _dwconv_grid(g[..., half:], p['conv_dw'][:, half:], 1)], axis=-1)
        y = y.reshape(bsz, length, CONV_W) + p['conv_b']
    else:
        y = _dwconv_seq(glu, p['conv_dw'], p['conv_b'])
    return jax.nn.silu(_layernorm(y, p['conv_ln_g'], p['conv_ln_b']))


def _gla_inputs(q, k, v, zf, zb, p):
    bsz, length, _ = q.shape
    f32 = jnp.float32
    heads = lambda t, d: t.astype(f32).reshape(bsz, length, GLA_HEADS, d)
    q = heads(q, GLA_HEAD_DK) * (GLA_HEAD_DK ** -0.5)
    k = heads(k, GLA_HEAD_DK)
    v = heads(v, GLA_HEAD_DV)
    gf = heads(jax.nn.log_sigmoid((zf @ p['w_gf'] + p['b_gf']).astype(f32)) / GATE_NORM, GLA_HEAD_DK)
    gb = heads(jax.nn.log_sigmoid((zb @ p['w_gb'] + p['b_gb']).astype(f32)) / GATE_NORM, GLA_HEAD_DK)
    return q, k, v, gf, gb


def _gla_chunked(q, k, v, g, s0):
    bsz, length, nh, dk = q.shape
    dv = v.shape[-1]
    n = length // CHUNK
    q, k, g = (t.reshape(bsz, n, CHUNK, nh, dk) for t in (q, k, g))
    v = v.reshape(bsz, n, CHUNK, nh, dv)
    b = jnp.cumsum(g, axis=2)
    b_last = b[:, :, -1]
    q_e = q * jnp.exp(b)
    k_e = k * jnp.exp(-b)
    k_tail = k * jnp.exp(b_last[:, :, None] - b)
    mask = jnp.tril(jnp.ones((CHUNK, CHUNK), dtype=bool))
    scores = jnp.where(mask, jnp.einsum('bnthd,bnshd->bnhts', q_e, k_e), 0.0)
    o_intra = jnp.einsum('bnhts,bnshv->bnthv', scores, v)
    kv = jnp.einsum('bnshd,bnshv->nbhdv', k_tail, v)
    decay = jnp.moveaxis(jnp.exp(b_last), 1, 0)

    def step(state, inp):
        kv_c, dec_c = inp
        return dec_c[..., None] * state + kv_c, state

    s_fin, s_prev = lax.scan(step, s0, (kv, decay))
    o_inter = jnp.einsum('bnthd,nbhdv->bnthv', q_e, s_prev)
    return (o_intra + o_inter).reshape(bsz, length, nh, dv), s_fin


def _gla_final_state(k, v, g):
    b = jnp.cumsum(g, axis=1)
    k_tail = k * jnp.exp(b[:, -1:] - b)
    return jnp.einsum('blhd,blhv->bhdv', k_tail, v)


def _rev(t):
    return jnp.flip(t, axis=1)


def _token_mixer(h, p, s_f0, s_b0, grid):
    bsz, length, _ = h.shape
    cu, cg, q, k, v, og, zf, zb = _split_proj(h, p)
    y_conv = _conv_module(cu, cg, p, grid)
    q, k, v, gf, gb = _gla_inputs(q, k, v, zf, zb, p)
    o_f, s_f = _gla_chunked(q, k, v, gf, s_f0)
    o_b, s_b = _gla_chunked(_rev(q), _rev(k), _rev(v), _rev(gb), s_b0)
    o = _rmsnorm(o_f + _rev(o_b), p['gla_norm_g'].reshape(GLA_HEADS, GLA_HEAD_DV))
    o = o.reshape(bsz, length, GLA_DV).astype(h.dtype) * jax.nn.silu(og)
    y = jnp.concatenate([y_conv, o], axis=-1) @ p['w_out']
    return y, s_f, s_b


def _context_states(h_ctx, p):
    _, _, q, k, v, _, zf, zb = _split_proj(h_ctx, p)
    _, k, v, gf, gb = _gla_inputs(q, k, v, zf, zb, p)
    return _gla_final_state(k, v, gf), _gla_final_state(_rev(k), _rev(v), _rev(gb))


def _conv_ffn(h, p):
    a, val = jnp.split(h @ p['w_up'], 2, axis=-1)
    a = _dwconv_seq(a, p['ffn_dw'], p['ffn_dw_b'])
    return (jax.nn.silu(a) * val) @ p['w_down']


def setup_inputs(seed: int = 0) -> dict:
    key = jax.random.key(seed)
    ks = jax.random.split(key, 32)
    f32 = jnp.float32
    nrm = lambda k, shape, s: jax.random.normal(k, shape, f32) * s
    L = DEPTH
    return {
        'x': nrm(ks[0], (BATCH, SEQ, D_MODEL), 1.0),
        'c': nrm(ks[1], (BATCH, D_MODEL), 1.0),
        'ctx': nrm(ks[2], (BATCH, CTX_LEN, D_MODEL), 1.0),
        'c_ctx': nrm(ks[3], (D_MODEL,), 1.0),
        'w_mod': nrm(ks[4], (L, D_MODEL, N_MOD * D_MODEL), 0.5 * D_MODEL ** -0.5),
        'b_mod': nrm(ks[5], (L, N_MOD * D_MODEL), 0.02),
        'norm1_g': 1.0 + nrm(ks[6], (L, D_MODEL), 0.02),
        'w_in': nrm(ks[7], (L, D_MODEL, D_IN), D_MODEL ** -0.5),
        'conv_dw': nrm(ks[8], (L, CONV_K, CONV_W), CONV_K ** -0.5),
        'conv_b': nrm(ks[9], (L, CONV_W), 0.02),
        'conv_ln_g': 1.0 + nrm(ks[10], (L, CONV_W), 0.02),
        'conv_ln_b': nrm(ks[11], (L, CONV_W), 0.02),
        'w_gf': nrm(ks[12], (L, GATE_RANK, GLA_DK), GATE_RANK ** -0.5),
        'b_gf': nrm(ks[13], (L, GLA_DK), 0.1),
        'w_gb': nrm(ks[14], (L, GATE_RANK, GLA_DK), GATE_RANK ** -0.5),
        'b_gb': nrm(ks[15], (L, GLA_DK), 0.1),
        'gla_norm_g': 1.0 + nrm(ks[16], (L, GLA_DV), 0.02),
        'w_out': nrm(ks[17], (L, MIX_W, D_MODEL), MIX_W ** -0.5),
        'norm2_g': 1.0 + nrm(ks[18], (L, D_MODEL), 0.02),
        'w_up': nrm(ks[19], (L, D_MODEL, 2 * FFN_HIDDEN), D_MODEL ** -0.5),
        'ffn_dw': nrm(ks[20], (L, FFN_K, FFN_HIDDEN), FFN_K ** -0.5),
        'ffn_dw_b': nrm(ks[21], (L, FFN_HIDDEN), 0.02),
        'w_down': nrm(ks[22], (L, FFN_HIDDEN, D_MODEL), FFN_HIDDEN ** -0.5),
        'final_g': 1.0 + nrm(ks[23], (D_MODEL,), 0.02),
    }


def reference(x, c, ctx, c_ctx, w_mod, b_mod, norm1_g, w_in, conv_dw, conv_b, conv_ln_g, conv_ln_b,
              w_gf, b_gf, w_gb, b_gb, gla_norm_g, w_out, norm2_g, w_up, ffn_dw, ffn_dw_b, w_down, final_g):
    bsz = x.shape[0]
    x_ctx = ctx
    for l in range(DEPTH):
        p = {'w_in': w_in[l], 'conv_dw': conv_dw[l], 'conv_b': conv_b[l], 'conv_ln_g': conv_ln_g[l],
             'conv_ln_b': conv_ln_b[l], 'w_gf': w_gf[l], 'b_gf': b_gf[l], 'w_gb': w_gb[l], 'b_gb': b_gb[l],
             'gla_norm_g': gla_norm_g[l], 'w_out': w_out[l], 'w_up': w_up[l], 'ffn_dw': ffn_dw[l],
             'ffn_dw_b': ffn_dw_b[l], 'w_down': w_down[l]}
        sh1, sc1, g1, sh2, sc2, g2 = _adaln(c, w_mod[l], b_mod[l])
        sh1c, sc1c, g1c, sh2c, sc2c, g2c = _adaln(c_ctx[None, :], w_mod[l], b_mod[l])
        h_ctx = _modulate(_rmsnorm(x_ctx, norm1_g[l]), sh1c, sc1c)
        if l == DEPTH - 1:
            s_f, s_b = _context_states(h_ctx, p)
        else:
            zeros = jnp.zeros((bsz, GLA_HEADS, GLA_HEAD_DK, GLA_HEAD_DV), jnp.float32)
            y_ctx, s_f, s_b = _token_mixer(h_ctx, p, zeros, zeros, grid=False)
            x_ctx = x_ctx + g1c * y_ctx
            x_ctx = x_ctx + g2c * _conv_ffn(_modulate(_rmsnorm(x_ctx, norm2_g[l]), sh2c, sc2c), p)
        h = _modulate(_rmsnorm(x, norm1_g[l]), sh1, sc1)
        y, _, _ = _token_mixer(h, p, s_f, s_b, grid=True)
        x = x + g1 * y
        x = x + g2 * _conv_ffn(_modulate(_rmsnorm(x, norm2_g[l]), sh2, sc2), p)
    return _rmsnorm(x, final_g)
```

```python
from contextlib import ExitStack

import numpy as np
import concourse.bass as bass
import concourse.mybir as mybir
from concourse.bass_utils import run_bass_kernel_spmd

F32 = mybir.dt.float32
BF16 = mybir.dt.bfloat16
AF = mybir.ActivationFunctionType
ALU = mybir.AluOpType

D = 1024
SEQ = 8192
CTX = 256
NT = 64
EXT = 33
HALO_T = 41
DIN = 2592
C_CU, C_CG, C_Q, C_K, C_V, C_OG, C_Z = 0, 512, 1024, 1280, 1536, 2048, 2560
HID = 2816
NH = 22
EPS = 1e-6
FB = 256
SEM_LIMIT = 4000


class Buf:
    __slots__ = ("name", "writer", "readers", "dsem", "dcount", "psum")

    def __init__(self, name, psum=False):
        self.name = name
        self.writer = None
        self.readers = []
        self.dsem = None
        self.dcount = 0
        self.psum = psum


class Op:
    __slots__ = ("idx", "stream", "fn", "deps", "is_dma", "dbuf", "needed", "signal", "cost", "pos")

    def __init__(self, idx, stream, fn, deps, is_dma, dbuf, cost):
        self.idx = idx
        self.stream = stream
        self.fn = fn
        self.deps = deps
        self.is_dma = is_dma
        self.dbuf = dbuf
        self.needed = False
        self.signal = None
        self.cost = cost
        self.pos = -1


class Fn:
    __slots__ = ("f", "cost")

    def __init__(self, f, cost):
        self.f = f
        self.cost = cost

    def __call__(self, e):
        return self.f(e)


CFG = {"f3_nx": 1, "h2T": 1, "p2_hm": 1, "p2_gate": 1, "p2_qk": 1, "p2_og": 1, "p2_xr": 2, "p2_mixT": 1, "p1_hm": 1, "add_eng": "dve", "kv_p2": 1, "p1_gate": 2, "hT_dve": 0, "ffn_ring": 3, "ffn_bal": 2}
SCHED_WINDOW = 600
REORDER = True
SEM_LAT = 0.27
DEFAULT_COST = {"pe": 0.12, "act": 0.4, "dve": 0.3, "pool": 0.5, "sp": 3.0}


class Sched:
    STREAMS = ("pe", "act", "dve", "pool", "sp")

    def __init__(self, nc, stack):
        self.nc = nc
        self.stack = stack
        self.ops = []
        self.order = {s: [] for s in self.STREAMS}
        self.cursor = {s: 0 for s in self.STREAMS}
        self.floor = 0
        self.flushed = 0
        self.nsem = 0
        self.cur = {s: [None, 0] for s in self.STREAMS}
        self.waited = {s: {} for s in self.STREAMS}
        self.last_compute = {s: None for s in self.STREAMS}
        self.phase_dmas = []
        self.sim_time = []

    def new_sem(self, name):
        self.nsem += 1
        return self.stack.enter_context(self.nc.semaphore(f"{name}_{self.nsem}"))

    def op(self, stream, fn, reads=(), writes=(), dma=None, extra=()):
        idx = len(self.ops)
        deps = set(extra)
        for b in reads:
            if b.writer is not None:
                deps.add(b.writer)
            if b.psum:
                for r in b.readers:
                    if self.ops[r].stream != stream:
                        deps.add(r)
        for b in writes:
            if b.writer is not None:
                deps.add(b.writer)
            deps.update(b.readers)
        deps = sorted(d for d in deps if d >= self.floor and d != idx)
        if fn is None:
            cost = 0.0
        else:
            cost = getattr(fn, "cost", None)
            if cost is None:
                cost = DEFAULT_COST["sp" if dma is not None else stream]
            elif stream == "pool" and dma is None:
                cost = cost * 3.5
        o = Op(idx, stream, fn, deps, dma is not None, dma, cost)
        self.ops.append(o)
        for b in reads:
            b.readers.append(idx)
        for b in writes:
            b.writer = idx
            b.readers = []
        if dma is not None:
            self.phase_dmas.append(idx)
        elif fn is not None:
            self.last_compute[stream] = idx
        return idx

    def barrier(self):
        deps = list(range(self.floor, len(self.ops)))
        for s in self.STREAMS:
            self.op(s, None, extra=deps)
        self.floor = len(self.ops)
        self.phase_dmas = []
        self.last_compute = {s: None for s in self.STREAMS}

    def _schedule(self, lo, hi):
        ops = self.ops
        n = hi - lo
        if not REORDER:
            order = {s: [] for s in self.STREAMS}
            for o in ops[lo:hi]:
                order[o.stream].append(o)
            self.sim_time.append(0.0)
            return order
        indeg = [0] * n
        users = [[] for _ in range(n)]
        for o in ops[lo:hi]:
            c = 0
            for d in o.deps:
                if d >= lo:
                    users[d - lo].append(o.idx)
                    c += 1
            indeg[o.idx - lo] = c
        ready = {s: [] for s in self.STREAMS}
        rtime = [0.0] * n
        free_at = {s: 0.0 for s in self.STREAMS}
        for o in ops[lo:hi]:
            if indeg[o.idx - lo] == 0:
                ready[o.stream].append(o.idx)
        order = {s: [] for s in self.STREAMS}
        remaining = n
        tmax = 0.0
        while remaining:
            best = None
            for s in self.STREAMS:
                lst = ready[s]
                if not lst:
                    continue
                fa = free_at[s]
                mi = min(lst)
                for i in lst:
                    if i > mi + SCHED_WINDOW:
                        continue
                    st = rtime[i - lo]
                    if st < fa:
                        st = fa
                    if best is None or (st, i) < best[0]:
                        best = ((st, i), s, i)
            (st, _), s, i = best
            o = ops[i]
            if o.is_dma:
                free_at[s] = st + 0.07
            else:
                free_at[s] = st + o.cost
            fin = st + o.cost
            if fin > tmax:
                tmax = fin
            ready[s].remove(i)
            order[s].append(o)
            remaining -= 1
            for u in users[i - lo]:
                k = u - lo
                lat = 0.0 if (ops[u].stream == s == "pe") else SEM_LAT
                if fin + lat > rtime[k]:
                    rtime[k] = fin + lat
                indeg[k] -= 1
                if indeg[k] == 0:
                    ready[ops[u].stream].append(u)
        self.sim_time.append(tmax)
        return order

    def _assign(self, order):
        ops = self.ops
        for s in self.STREAMS:
            base = len(self.order[s])
            for k, o in enumerate(order[s]):
                o.pos = base + k
        for s in self.STREAMS:
            for o in order[s]:
                latest = {}
                for d in o.deps:
                    p = ops[d]
                    if p.is_dma:
                        p.needed = True
                        continue
                    if p.fn is None:
                        continue
                    if p.stream == o.stream and p.stream in ("pe", "sp"):
                        continue
                    q = latest.get(p.stream)
                    if q is None or p.pos > q.pos:
                        latest[p.stream] = p
                for p in latest.values():
                    p.needed = True
        for s in self.STREAMS:
            for o in order[s]:
                if not o.needed:
                    continue
                if o.is_dma:
                    b = o.dbuf
                    if b.dsem is None or b.dcount + 16 > SEM_LIMIT:
                        b.dsem = self.new_sem("d")
                        b.dcount = 0
                    b.dcount += 16
                    o.signal = (b.dsem, b.dcount, 16)
                else:
                    c = self.cur[s]
                    if c[0] is None or c[1] + 1 > SEM_LIMIT:
                        c[0] = self.new_sem("e" + s)
                        c[1] = 0
                    c[1] += 1
                    o.signal = (c[0], c[1], 1)
            self.order[s].extend(order[s])

    def _emit_stream(self, stream, eng):
        ops = self.ops
        waited = self.waited[stream]
        lst = self.order[stream]
        for o in lst[self.cursor[stream]:]:
            need = {}
            for d in o.deps:
                p = ops[d]
                if p.signal is None:
                    continue
                sem, val, _ = p.signal
                k = id(sem)
                if waited.get(k, 0) >= val:
                    continue
                if k not in need or need[k][1] < val:
                    need[k] = (sem, val)
            for k, (sem, val) in need.items():
                eng.wait_ge(sem, val)
                waited[k] = val
            if o.fn is None:
                continue
            ins = o.fn(eng)
            if o.signal is not None:
                ins.then_inc(o.signal[0], o.signal[2])
        self.cursor[stream] = len(lst)

    def flush(self):
        order = self._schedule(self.flushed, len(self.ops))
        self.flushed = len(self.ops)
        self._assign(order)
        S = self
        with self.nc.Block() as block:
            @block.sync
            def _(e):
                S._emit_stream("sp", e)

            @block.tensor
            def _(e):
                S._emit_stream("pe", e)

            @block.scalar
            def _(e):
                S._emit_stream("act", e)

            @block.vector
            def _(e):
                S._emit_stream("dve", e)

            @block.gpsimd
            def _(e):
                S._emit_stream("pool", e)


class T:
    def __init__(self, t, name):
        self.t = t
        self.b = Buf(name)


class Ring:
    def __init__(self, tiles):
        self.tiles = tiles
        self.i = 0

    def next(self):
        t = self.tiles[self.i % len(self.tiles)]
        self.i += 1
        return t


def _fs(ap):
    try:
        return float(ap.free_size())
    except Exception:
        return 256.0


def f_mm(out, lhsT, rhs, start, stop):
    n = _fs(out)
    c = max(n, 64.0) / 2400.0 + 0.012
    if lhsT.dtype == F32:
        c *= 4.0
    return Fn(lambda e: e.matmul(out, lhsT=lhsT, rhs=rhs, start=start, stop=stop), c)


def f_tr(out, in_, ident):
    return Fn(lambda e: e.transpose(out=out, in_=in_, identity=ident), 0.09)


def f_act(out, in_, func, bias=None, scale=None, accum=None):
    kw = {}
    if bias is not None:
        kw["bias"] = bias
    if scale is not None:
        kw["scale"] = scale
    if accum is not None:
        kw["accum_out"] = accum
    return Fn(lambda e: e.activation(out=out, in_=in_, func=func, **kw), 0.2 + _fs(in_) / 1400.0)


def f_tt(out, in0, in1, op):
    return Fn(lambda e: e.tensor_tensor(out=out, in0=in0, in1=in1, op=op), 0.07 + _fs(out) / 960.0)


def f_ts(out, in0, s1, s2, op0, op1=None):
    c = 0.07 + _fs(out) / 960.0
    if op1 is None:
        return Fn(lambda e: e.tensor_scalar(out=out, in0=in0, scalar1=s1, scalar2=None, op0=op0), c)
    return Fn(lambda e: e.tensor_scalar(out=out, in0=in0, scalar1=s1, scalar2=s2, op0=op0, op1=op1), c)


def f_stt(out, in0, scalar, in1, op0, op1):
    return Fn(lambda e: e.scalar_tensor_tensor(out=out, in0=in0, scalar=scalar, in1=in1, op0=op0, op1=op1),
              0.07 + _fs(out) / 960.0)


def f_cp(out, in_):
    return Fn(lambda e: e.tensor_copy(out=out, in_=in_), 0.07 + _fs(out) / 960.0)


def f_rcp(out, in_):
    return Fn(lambda e: e.reciprocal(out=out, in_=in_), 0.07 + _fs(out) / 960.0)


def f_dma(out, in_):
    try:
        nb = float(out.nbytes())
    except Exception:
        nb = 65536.0
    return Fn(lambda e: e.dma_start(out=out, in_=in_), 2.0 + nb / 150e3)


def f_memset(ap, val):
    return Fn(lambda e: e.memset(ap, val), 0.07 + _fs(ap) / 960.0)


def build_program(stop=99, dbg=False, p2_blocks=99, p2_conv=True, p2_sub=99):
    nc = bass.Bass("TRN2", target_bir_lowering=False)

    def di(name, shape):
        return nc.dram_tensor(name, shape, F32, kind="ExternalInput").ap()

    x_seq = di("x_seq", [SEQ, D])
    ctx_seq = di("ctx_seq", [CTX, D])
    c_fm = di("c_fm", [128, 8])
    cctx_fm = di("cctx_fm", [128, 8])
    w_mod = di("w_mod", [D, 6 * D])
    b_mod = di("b_mod", [6 * D])
    norm1_g = di("norm1_g", [D])
    w_in = di("w_in", [D, DIN])
    cw_fm = di("cw_fm", [128, 4, 31])
    cvec_fm = di("cvec_fm", [128, 3, 4])
    wgF = di("wgF_aug", [33, 256])
    wgB = di("wgB_aug", [33, 256])
    gla_g = di("gla_norm_fm", [128, 4])
    w_out = di("w_out", [D, D])
    norm2_g = di("norm2_g", [D])
    w_up = di("w_up", [D, 2 * HID])
    fdw_fm = di("fdw_fm", [128, NH, 4])
    w_down = di("w_down", [HID, D])
    final_g = di("final_g", [D])
    ident_in = di("ident", [128, 128])
    masks_in = di("masks", [128, 4, 128])
    ind_in = di("ind", [128, 2])
    y_out = nc.dram_tensor("y_out", [SEQ // 2, D], F32, kind="ExternalOutput").ap()
    x1s = nc.dram_tensor("x1_scratch", [EXT * 128, D], F32, kind="ExternalOutput" if dbg else "Internal").ap()
    if dbg:
        d_mod = nc.dram_tensor("d_mod", [4, 128, D], F32, kind="ExternalOutput").ap()
        d_st = nc.dram_tensor("d_st", [4, 128, 256], F32, kind="ExternalOutput").ap()
        d_gcol = nc.dram_tensor("d_gcol", [128, 2 * (15 + 2 * HALO_T) * 64], BF16, kind="ExternalOutput").ap()
        d_sbp = nc.dram_tensor("d_sbp", [128, 2 * EXT * 256], BF16, kind="ExternalOutput").ap()

    modsave = nc.dram_tensor("mod_scratch", [4, D], F32, kind="Internal").ap()
    sbp_dram = nc.dram_tensor("sbp_scratch", [2 * EXT, 128, 256], BF16, kind="Internal").ap()
    w_up_bf = nc.dram_tensor("w_up_bf16", [D, 2 * HID], BF16, kind="Internal").ap()
    kt_d = nc.dram_tensor("kt_scratch", [EXT, 128, 256], F32, kind="Internal").ap()
    v_d = nc.dram_tensor("v_scratch", [EXT, 128, 512], BF16, kind="Internal").ap()
    z_d = nc.dram_tensor("z_scratch", [EXT, 32, 128], BF16, kind="Internal").ap()
    w_dn_bf = nc.dram_tensor("w_dn_bf16", [HID, D], BF16, kind="Internal").ap()

    with ExitStack() as top:
        S = Sched(nc, top)

        def sb(stack, name, shape, dt):
            return T(stack.enter_context(nc.sbuf_tensor("sb_" + name, shape, dt)), name)

        def ps(stack, name, shape, dt):
            t_ = T(stack.enter_context(nc.psum_tensor("ps_" + name, shape, dt)), name)
            t_.b.psum = True
            return t_

        ident_f = sb(top, "ident_f", [128, 128], F32)
        ident_b = sb(top, "ident_b", [128, 128], BF16)
        masks = sb(top, "masks", [128, 4, 128], F32)
        ind = sb(top, "ind", [128, 2], F32)
        masks_b = sb(top, "masks_b", [128, 4, 128], BF16)
        ind_b = sb(top, "ind_b", [128, 2], BF16)
        ones_f = sb(top, "ones_f", [128, 128], F32)
        onesM = sb(top, "onesM", [128, 128], BF16)
        junk = sb(top, "junk", [128, 1024], BF16)
        M_LI, M_UI, M_LS, M_US = 0, 1, 2, 3

        pT = ps(top, "pT", [128, 512], F32)
        pbank = [ps(top, f"p{i}", [128, 512], F32) for i in range(1, 8)]
        p1, p2, p3, p4, p5, p6, p7 = pbank
        p3r = p3.b

        S.op("sp", f_dma(ident_f.t[:], ident_in[:, :]), writes=[ident_f.b], dma=ident_f.b)
        S.op("sp", f_dma(masks.t[:], masks_in[:, :, :]), writes=[masks.b], dma=masks.b)
        S.op("sp", f_dma(ind.t[:], ind_in[:, :]), writes=[ind.b], dma=ind.b)
        S.op("dve", f_cp(ident_b.t[:], ident_f.t[:]), reads=[ident_f.b], writes=[ident_b.b])
        S.op("dve", f_cp(masks_b.t[:], masks.t[:]), reads=[masks.b], writes=[masks_b.b])
        S.op("dve", f_cp(ind_b.t[:], ind.t[:]), reads=[ind.b], writes=[ind_b.b])
        S.op("dve", f_memset(ones_f.t[:], 1.0), writes=[ones_f.b])
        S.op("dve", f_memset(onesM.t[:], 1.0 / 512.0), writes=[onesM.b])

        def front(fr, rows_ap, n, sbt_, bbt_, dst=None):
            xt = fr["x"].next()
            st = fr["st"].next()
            hm1 = fr["hm1"].next()
            hm = fr["hm"].next()
            hT = fr["hT"].next() if dst is None else dst[0]
            hT_ap = hT.t[:, :, 0:n] if dst is None else dst[1]
            fr["last_x"] = S.op("sp", f_dma(xt.t[0:n, :], rows_ap), writes=[xt.b], dma=xt.b)
            S.op("act", f_act(hm1.t[0:n, :], xt.t[0:n, :], AF.Square, accum=st.t[0:n, 0:1]),
                 reads=[xt.b], writes=[st.b, hm1.b])
            S.op("act", f_act(st.t[0:n, 1:2], st.t[0:n, 0:1], AF.Ln, bias=EPS, scale=1.0 / D),
                 reads=[st.b], writes=[st.b])
            S.op("act", f_act(st.t[0:n, 2:3], st.t[0:n, 1:2], AF.Exp, scale=-0.5),
                 reads=[st.b], writes=[st.b])
            S.op("dve", f_stt(hm1.t[0:n, :], xt.t[0:n, :], st.t[0:n, 2:3], sbt_.t[0:n, :], ALU.mult, ALU.mult),
                 reads=[xt.b, st.b, sbt_.b], writes=[hm1.b])
            S.op(CFG["add_eng"], f_tt(hm.t[0:n, :], hm1.t[0:n, :], bbt_.t[0:n, :], ALU.add),
                 reads=[hm1.b, bbt_.b], writes=[hm.b])
            pv = pT.t[:, :].rearrange("p (a b) -> p a b", a=4)
            for half in range(2):
                for q in range(4):
                    kc = half * 4 + q
                    S.op("pe", f_mm(pT.t[:, q * 128:q * 128 + n], hm.t[0:n, kc * 128:(kc + 1) * 128],
                                    ident_b.t[0:n, 0:n], True, True),
                         reads=[hm.b, ident_b.b], writes=[pT.b])
                if CFG["hT_dve"] and half == 1:
                    S.op("dve", f_cp(hT_ap[:, half * 4:half * 4 + 4, :], pv[:, :, 0:n]), reads=[pT.b], writes=[hT.b])
                else:
                    S.op("act", f_act(hT_ap[:, half * 4:half * 4 + 4, :], pv[:, :, 0:n], AF.Copy), reads=[pT.b], writes=[hT.b])
            return hT

        def act_sigmoid(dst, src, dst_b, src_b):
            S.op("act", f_act(dst, src, AF.Exp, scale=-1.0), reads=[src_b], writes=[dst_b])
            S.op("act", f_act(dst, dst, AF.Ln, bias=1.0), reads=[dst_b], writes=[dst_b])
            S.op("act", f_act(dst, dst, AF.Exp, scale=-1.0), reads=[dst_b], writes=[dst_b])

        def make_front(stack, pre, nx=2, nh=2, nm=1):
            return {
                "x": Ring([sb(stack, f"{pre}x{i}", [128, D], F32) for i in range(nx)]),
                "st": Ring([sb(stack, f"{pre}st{i}", [128, 4], F32) for i in range(4)]),
                "hm1": Ring([sb(stack, f"{pre}hm1_{i}", [128, D], BF16) for i in range(nm)]),
                "hm": Ring([sb(stack, f"{pre}hm_{i}", [128, D], BF16) for i in range(nm)]),
                "hT": Ring([sb(stack, f"{pre}hT{i}", [128, 8, 128], BF16) for i in range(nh)]),
            }

        with ExitStack() as mix:
            w_in_sb = sb(mix, "w_in_sb", [128, 8, DIN], BF16)
            w_out_sb = sb(mix, "w_out_sb", [128, 8, D], BF16)
            s1b = sb(mix, "s1b", [128, D], F32)
            b1b = sb(mix, "b1b", [128, D], F32)
            cw = sb(mix, "cw", [128, 4, 31], F32)
            cvec = sb(mix, "cvec", [128, 3, 4], F32)
            wgF_sb = sb(mix, "wgF_sb", [33, 256], BF16)
            wgB_sb = sb(mix, "wgB_sb", [33, 256], BF16)
            gng_sb = sb(mix, "gng_sb", [128, 4], F32)
            SF = sb(mix, "SF", [128, 2, 128], F32)
            SBs = sb(mix, "SBs", [128, 2, 128], F32)
            zTa_r = Ring([sb(mix, f"zTa{i}", [33, 128], BF16) for i in range(2)])
            zcur = {"t": zTa_r.tiles[0]}

            w_in_v = w_in.rearrange("(kc p) n -> p kc n", p=128)
            wgrp = Buf("wgrp")
            crit_cols = [(C_K, C_K + 768), (C_Z, C_Z + 32), (C_CU + 256, C_CU + 512), (C_CG + 256, C_CG + 512)]
            rest_cols = [(C_CU, C_CU + 256), (C_CG, C_CG + 256), (C_Q, C_Q + 256), (C_OG, C_OG + 512)]
            w_all_ops = [S.op("pool", f_dma(w_in_sb.t[:, :, a:b_], w_in_v[:, :, a:b_]), dma=wgrp) for (a, b_) in crit_cols]
            w_out_v = w_out.rearrange("(kc p) n -> p kc n", p=128)
            w_out_regs = [Buf(f"wout{kc}") for kc in range(8)]
            S.op("sp", f_dma(cw.t[:], cw_fm[:, :, :]), writes=[cw.b], dma=cw.b)
            S.op("sp", f_dma(cvec.t[:], cvec_fm[:, :, :]), writes=[cvec.b], dma=cvec.b)
            S.op("pool", f_dma(wgF_sb.t[:], wgF[:, :]), writes=[wgF_sb.b], dma=wgF_sb.b)
            S.op("pool", f_dma(wgB_sb.t[:], wgB[:, :]), writes=[wgB_sb.b], dma=wgB_sb.b)
            S.op("sp", f_dma(gng_sb.t[:], gla_g[:, :]), writes=[gng_sb.b], dma=gng_sb.b)
            for zt in zTa_r.tiles:
                S.op("dve", f_memset(zt.t[32:33, :], 1.0), writes=[zt.b])
            SF.regs = [Buf(f"SF{i}") for i in range(4)]
            SBs.regs = [Buf(f"SB{i}") for i in range(4)]
            S.op("dve", f_memset(SF.t[:], 0.0), writes=SF.regs)
            S.op("dve", f_memset(SBs.t[:], 0.0), writes=SBs.regs)

            def gate(X, g):
                wg = wgF_sb if X == "F" else wgB_sb
                gn = g["gn" + X]
                zTa = zcur["t"]
                S.op("pe", f_mm(p6.t[:, 0:256], zTa.t[0:33, :], wg.t[0:33, :], True, True),
                     reads=[zTa.b, wg.b], writes=[p6.b])
                S.op("act", f_act(g["eg"].t[:], p6.t[:, 0:256], AF.Exp, scale=-1.0), reads=[p6.b], writes=[g["eg"].b])
                S.op("act", f_act(gn.t[:], g["eg"].t[:], AF.Ln, bias=1.0), reads=[g["eg"].b], writes=[gn.b])
                return gn

            def state_stage(X, g, gn, ktok, vbf, save_chunks):
                S_ = SF if X == "F" else SBs
                ms = M_US if X == "F" else M_LS
                Et, ktail, dec = g["Et"], g["ktail"], g["dec"]
                S.op("pe", f_mm(p6.t[:, 256:512], masks_b.t[:, ms, :], gn.t[:], True, True),
                     reads=[masks_b.b, gn.b], writes=[p6.b])
                for j in range(2):
                    S.op("pe", f_mm(p7.t[:, 2 * j:2 * j + 2], gn.t[:, j * 128:(j + 1) * 128], ind_b.t[:], True, True),
                         reads=[gn.b, ind_b.b], writes=[p7.b])
                S.op("act", f_act(Et.t[:], p6.t[:, 256:512], AF.Exp, scale=-1.0 / 16), reads=[p6.b], writes=[Et.b])
                S.op("act", f_act(dec.t[:], p7.t[:, 0:4], AF.Exp, scale=-1.0 / 16), reads=[p7.b], writes=[dec.b])
                S.op("dve", f_tt(ktail.t[:], ktok.t[:], Et.t[:], ALU.mult), reads=[ktok.b, Et.b], writes=[ktail.b])
                order = (0, 1) if X == "F" else (1, 0)
                kvp = {0: (p2 if CFG["kv_p2"] else p3), 1: p5}
                for lc in order:
                    if save_chunks is not None:
                        spt = sps.next()
                        S.op("act", f_act(spt.t[:], S_.t[:], AF.Copy), reads=S_.regs, writes=[spt.b])
                        S.op("sp", f_dma(sbp_dram[save_chunks[lc]], spt.t[:].rearrange("p a b -> p (a b)")),
                             reads=[spt.b], dma=spt.b)
                    pk = kvp[lc]
                    for j in range(2):
                        S.op("pe", f_mm(pk.t[:, j * 256:(j + 1) * 256],
                                        ktail.t[lc * 64:(lc + 1) * 64, j * 128:(j + 1) * 128],
                                        vbf.t[lc * 64:(lc + 1) * 64, j * 256:(j + 1) * 256], True, True),
                             reads=[ktail.b, vbf.b], writes=[pk.b])
                    for j in range(2):
                        for e_ in range(2):
                            sl = slice(e_ * 64, (e_ + 1) * 64)
                            S.op("dve", f_stt(S_.t[sl, j, :], S_.t[sl, j, :], dec.t[sl, 2 * j + lc:2 * j + lc + 1],
                                              pk.t[sl, j * 256 + e_ * 128:j * 256 + (e_ + 1) * 128],
                                              ALU.mult, ALU.add),
                                 reads=[S_.regs[2 * j + e_], dec.b, pk.b], writes=[S_.regs[2 * j + e_]])

            def make_gate_tiles(stack, pre):
                d_ = {
                    "gnF": sb(stack, pre + "gnF", [128, 256], BF16),
                    "gnB": sb(stack, pre + "gnB", [128, 256], BF16),
                    "Et": sb(stack, pre + "Et", [128, 256], F32),
                    "ktail": sb(stack, pre + "ktail", [128, 256], BF16),
                    "dec": sb(stack, pre + "dec", [128, 4], F32),
                    "ktok": sb(stack, pre + "ktok", [128, 256], F32),
                    "vbf": sb(stack, pre + "vbf", [128, 512], BF16),
                }
                d_["eg"] = d_["Et"]
                return d_

            def kvz_proj(hT, g, save_t=None):
                for kc in range(8):
                    S.op("pe", f_mm(p3.t[:, 0:256], hT.t[:, kc, :], w_in_sb.t[:, kc, C_K:C_K + 256], kc == 0, kc == 7),
                         reads=[hT.b], extra=w_all_ops, writes=[p3.b])
                for kc in range(8):
                    S.op("pe", f_mm(p4.t[:, :], hT.t[:, kc, :], w_in_sb.t[:, kc, C_V:C_V + 512], kc == 0, kc == 7),
                         reads=[hT.b], extra=w_all_ops, writes=[p4.b])
                for kc in range(8):
                    S.op("pe", f_mm(p3.t[0:32, 256:384], w_in_sb.t[:, kc, C_Z:C_Z + 32], hT.t[:, kc, :], kc == 0, kc == 7),
                         reads=[hT.b], extra=w_all_ops, writes=[p3r])
                S.op("act", f_act(g["ktok"].t[:], p3.t[:, 0:256], AF.Copy), reads=[p3.b], writes=[g["ktok"].b])
                S.op("dve", f_cp(g["vbf"].t[:], p4.t[:, :]), reads=[p4.b], writes=[g["vbf"].b])
                zTa = zTa_r.next()
                zcur["t"] = zTa
                S.op("act", f_act(zTa.t[0:32, :], p3.t[0:32, 256:384], AF.Copy), reads=[p3r], writes=[zTa.b])
                if save_t is not None:
                    S.op("sp", f_dma(kt_d[save_t], g["ktok"].t[:]), reads=[g["ktok"].b], dma=g["ktok"].b)
                    S.op("sp", f_dma(v_d[save_t], g["vbf"].t[:]), reads=[g["vbf"].b], dma=g["vbf"].b)
                    S.op("sp", f_dma(z_d[save_t], zTa.t[0:32, :]), reads=[zTa.b], dma=zTa.b)

            def adaln(ph, pre, chunks, with_ctx, dst, banks):
                nv = 16 if with_ctx else 8
                c_sb = sb(ph, pre + "c_sb", [128, nv], F32)
                e_c = sb(ph, pre + "e_c", [128, nv], F32)
                screp = sb(ph, pre + "screp", [128, nv, 128], F32)
                wm = Ring([sb(ph, f"{pre}wm{i}", [128, 8, 256], F32) for i in range(2)])
                bm = Ring([sb(ph, f"{pre}bm{i}", [128, 256], F32) for i in range(2)])
                ngb = Ring([sb(ph, f"{pre}ngb{i}", [128, 256], F32) for i in range(2)])
                tmpm = Ring([sb(ph, f"{pre}tmpm{i}", [128, 256], F32) for i in range(2)])
                S.op("sp", f_dma(c_sb.t[:, 0:8], c_fm[:, :]), writes=[c_sb.b], dma=Buf(pre + "c0"))
                if with_ctx:
                    S.op("sp", f_dma(c_sb.t[:, 8:16], cctx_fm[:, :]), writes=[c_sb.b], dma=Buf(pre + "c1"))
                S.op("act", f_act(e_c.t[:], c_sb.t[:], AF.Exp, scale=-1.0), reads=[c_sb.b], writes=[e_c.b])
                S.op("dve", f_ts(e_c.t[:], e_c.t[:], 1.0, None, ALU.add), reads=[e_c.b], writes=[e_c.b])
                S.op("dve", f_rcp(e_c.t[:], e_c.t[:]), reads=[e_c.b], writes=[e_c.b])
                S.op("dve", f_tt(c_sb.t[:], c_sb.t[:], e_c.t[:], ALU.mult), reads=[c_sb.b, e_c.b], writes=[c_sb.b])
                for i in range(nv):
                    S.op("dve", f_ts(screp.t[:, i, :], ones_f.t[:], c_sb.t[:, i:i + 1], None, ALU.mult),
                         reads=[ones_f.b, c_sb.b], writes=[screp.b])
                w_mod_v = w_mod.rearrange("(kc p) n -> p kc n", p=128)
                for ci, nci in enumerate(chunks):
                    n0 = nci * 256
                    wmt = wm.next()
                    bmt = bm.next()
                    q_ = ("sp", "act")[ci % 2] if with_ctx else "sp"
                    S.op(q_, f_dma(wmt.t[:, :, :], w_mod_v[:, :, n0:n0 + 256]), writes=[wmt.b], dma=wmt.b)
                    S.op("sp", f_dma(bmt.t[:], b_mod[n0:n0 + 256].partition_broadcast(128)), writes=[bmt.b], dma=bmt.b)
                    which = nci // 4
                    c0_ = (nci % 4) * 256
                    half = slice(c0_, c0_ + 256)
                    if which in (1, 4):
                        ngt = ngb.next()
                        gsrc = norm1_g if which == 1 else norm2_g
                        S.op("sp", f_dma(ngt.t[:], gsrc[c0_:c0_ + 256].partition_broadcast(128)), writes=[ngt.b], dma=ngt.b)
                    variants = [(0, banks[0])] + ([(8, banks[1])] if (which < 2 and with_ctx) else [])
                    for off, pp_ in variants:
                        pp = T(pp_.t[:, 0:256], pp_.b.name)
                        pp.b = pp_.b
                        for kc in range(8):
                            S.op("pe", f_mm(pp.t, screp.t[:, off + kc, :], wmt.t[:, kc, :], kc == 0, kc == 7),
                                 reads=[screp.b, wmt.b], writes=[pp.b])
                        key = (which, off)
                        if which in (0, 2):
                            d_ = dst[key]
                            S.op("dve", f_tt(d_.t[:, half], pp.t, bmt.t[:], ALU.add), reads=[pp.b, bmt.b], writes=[d_.b])
                        elif which == 1:
                            d_ = dst[key]
                            tm = tmpm.next()
                            S.op("dve", f_tt(tm.t[:], pp.t, bmt.t[:], ALU.add), reads=[pp.b, bmt.b], writes=[tm.b])
                            S.op("dve", f_stt(d_.t[:, half], tm.t[:], 1.0, ngt.t[:], ALU.add, ALU.mult),
                                 reads=[tm.b, ngt.b], writes=[d_.b])
                        else:
                            row = {3: 1, 4: 0, 5: 2}[which]
                            tm = tmpm.next()
                            S.op("dve", f_tt(tm.t[:], pp.t, bmt.t[:], ALU.add), reads=[pp.b, bmt.b], writes=[tm.b])
                            if which == 4:
                                S.op("dve", f_stt(tm.t[:], tm.t[:], 1.0, ngt.t[:], ALU.add, ALU.mult),
                                     reads=[tm.b, ngt.b], writes=[tm.b])
                            S.op("sp", f_dma(modsave[row:row + 1, c0_:c0_ + 256], tm.t[0:1, :]), reads=[tm.b], dma=tm.b)

            with ExitStack() as ph:
                s1c = sb(ph, "s1c", [128, D], F32)
                b1c = sb(ph, "b1c", [128, D], F32)
                fr0 = make_front(ph, "f0", nx=2, nh=2)
                g0 = {"F": make_gate_tiles(ph, "g0F"), "B": make_gate_tiles(ph, "g0B")}
                adaln(ph, "a0", list(range(8)), True,
                      {(0, 0): b1b, (0, 8): b1c, (1, 0): s1b, (1, 8): s1c}, (p1, p2))

                for X, tiles in (("B", (1, 0)), ("F", (0, 1))):
                    for t in tiles:
                        hT = front(fr0, ctx_seq[t * 128:(t + 1) * 128, :], 128, s1c, b1c)
                        kvz_proj(hT, g0[X])
                        gn = gate(X, g0[X])
                        state_stage(X, g0[X], gn, g0[X]["ktok"], g0[X]["vbf"], None)
                if dbg:
                    S.op("sp", f_dma(d_mod[0], s1b.t[:]), reads=[s1b.b], dma=Buf("dd0"))
                    S.op("sp", f_dma(d_mod[1], b1b.t[:]), reads=[b1b.b], dma=Buf("dd1"))
                    S.op("sp", f_dma(d_mod[2], s1c.t[:]), reads=[s1c.b], dma=Buf("dd2"))
                    S.op("sp", f_dma(d_mod[3], b1c.t[:]), reads=[b1c.b], dma=Buf("dd3"))
                    S.op("sp", f_dma(d_st[0], SF.t[:].rearrange("p a b -> p (a b)")), reads=SF.regs, dma=Buf("dd4"))
                    S.op("sp", f_dma(d_st[1], SBs.t[:].rearrange("p a b -> p (a b)")), reads=SBs.regs, dma=Buf("dd5"))
                S.barrier()
                S.flush()
            if stop <= 0:
                return nc

            sps = Ring([sb(mix, f"sps{i}", [128, 2, 128], BF16) for i in range(2)])
            gcol = sb(mix, "gcol", [128, 2, (15 + 2 * HALO_T) * 64], BF16)
            diagT = sb(mix, "diagT", [128, 4 * 31, 128], BF16)
            S.op("pool", f_memset(gcol.t[:, :, 0:15 * 64], 0.0), writes=[gcol.b])
            for c in range(4):
                for k in range(31):
                    S.op("dve", f_ts(diagT.t[:, c * 31 + k, :], ident_b.t[:], cw.t[:, c, k:k + 1], None, ALU.mult),
                         reads=[ident_b.b, cw.b], writes=[diagT.b])

            with ExitStack() as ph:
                pcg = Buf("precast")
                bg = []

                def _bg_dma(out_ap, in_ap, grp, lst=None):
                    def go(dep):
                        i = S.op("pool", f_dma(out_ap, in_ap), dma=grp, extra=[dep])
                        if lst is not None:
                            lst.append(i)
                    return go
                g1b = sb(ph, "g1b", [128, D], F32)
                adaln(ph, "a1", list(range(8, 24)), False, {(2, 0): g1b}, (p7,))
                wgrp1 = Buf("wgrp1")
                w_rest_ops = []
                for kc in (0, 4):
                    bg.append(_bg_dma(w_out_sb.t[:, kc:kc + 4, :], w_out_v[:, kc:kc + 4, :], wgrp1, w_rest_ops))
                for (a, b_) in rest_cols:
                    bg.append(_bg_dma(w_in_sb.t[:, :, a:b_], w_in_v[:, :, a:b_], wgrp1, w_rest_ops))
                for kc in range(8):
                    bg.append(_bg_dma(w_up_bf[kc * 128:(kc + 1) * 128, :], w_up[kc * 128:(kc + 1) * 128, :], pcg))
                for j0 in range(0, NH, 2):
                    bg.append(_bg_dma(w_dn_bf[j0 * 128:(j0 + 2) * 128, :], w_down[j0 * 128:(j0 + 2) * 128, :], pcg))

                def fold_w_out():
                    for kc in range(8):
                        if kc < 4:
                            S.op("dve", f_tt(w_out_sb.t[:, kc, :], w_out_sb.t[:, kc, :], g1b.t[:], ALU.mult),
                                 reads=[g1b.b], writes=[w_out_regs[kc]], extra=w_rest_ops)
                        else:
                            S.op("dve", f_stt(w_out_sb.t[:, kc, :], w_out_sb.t[:, kc, :], gng_sb.t[:, kc - 4:kc - 3], g1b.t[:],
                                              ALU.mult, ALU.mult),
                                 reads=[g1b.b, gng_sb.b], writes=[w_out_regs[kc]], extra=w_rest_ops)
                fold_done = []
                fr1 = make_front(ph, "f1", nx=3, nh=2, nm=CFG["p1_hm"])
                g1r = Ring([make_gate_tiles(ph, f"g1{i}") for i in range(CFG["p1_gate"])])
                ecg = sb(ph, "ecg1", [128, 2, 128], F32)
                for t in range(NT - 1, -1, -1):
                    g1 = g1r.next()
                    hT = front(fr1, x_seq[t * 128:(t + 1) * 128, :], 128, s1b, b1b)
                    if bg and t % 2 == 0:
                        bg.pop(0)(fr1["last_x"])
                        if len(w_rest_ops) == 6 and not fold_done:
                            fold_w_out()
                            fold_done.append(1)
                    kvz_proj(hT, g1, save_t=t if t < EXT else None)
                    if t < HALO_T:
                        for m in range(4):
                            col = (C_CU + 256 + m * 128) if m < 2 else (C_CG + 256 + (m - 2) * 128)
                            for kc in range(8):
                                S.op("pe", f_mm(p1.t[:, m * 128:(m + 1) * 128], w_in_sb.t[:, kc, col:col + 128],
                                                hT.t[:, kc, :], kc == 0, kc == 7),
                                     reads=[hT.b], extra=w_all_ops, writes=[p1.b])
                        p1v = p1.t[:, :].rearrange("p (a b) -> p a b", a=4)
                        act_sigmoid(ecg.t[:], p1v[:, 2:4, :], ecg.b, p1.b)
                        pos = (15 + 2 * t) * 64
                        S.op("dve", f_tt(gcol.t[:, :, pos:pos + 128], p1v[:, 0:2, :], ecg.t[:], ALU.mult),
                             reads=[p1.b, ecg.b], writes=[gcol.b])
                    gn = gate("B", g1)
                    state_stage("B", g1, gn, g1["ktok"], g1["vbf"], (2 * t, 2 * t + 1) if t < EXT else None)
                while bg:
                    bg.pop(0)(fr1["last_x"])
                if not fold_done:
                    fold_w_out()
                if dbg:
                    S.op("sp", f_dma(d_st[2], SBs.t[:].rearrange("p a b -> p (a b)")), reads=SBs.regs, dma=Buf("dd6"))
                    S.op("sp", f_dma(d_gcol[:, :], gcol.t[:].rearrange("p a b -> p (a b)")), reads=[gcol.b], dma=Buf("dd7"))
                S.barrier()
                S.flush()
            if stop <= 1:
                return nc

            with ExitStack() as ph:
                fr2 = make_front(ph, "f2", nx=2, nh=2, nm=CFG["p2_hm"])
                g2r = Ring([make_gate_tiles(ph, f"g2{i}") for i in range(CFG["p2_gate"])])
                qk_r = Ring([sb(ph, f"qk_s{i}", [128, 4, 128], F32) for i in range(CFG["p2_qk"])])
                eog_r = Ring([sb(ph, f"eog{i}", [128, 512], F32) for i in range(CFG["p2_og"])])
                sog_r = Ring([sb(ph, f"sog{i}", [128, 512], F32) for i in range(CFG["p2_og"])])
                EEp = sb(ph, "EEp", [128, 2, 128], F32)
                EEn = sb(ph, "EEn", [128, 2, 128], F32)
                EE = {"Fp": EEp, "Bp": EEp, "Fn": EEn, "Bn": EEn}
                QK = {X + s: sb(ph, "QK" + X + s, [128, 2, 128], BF16) for X in "FB" for s in "qk"}
                scm = sb(ph, "scm", [128, 8, 128], BF16)
                SFbf = Ring([sb(ph, f"SFbf{i}", [128, 2, 128], BF16) for i in range(2)])
                sbl = Ring([sb(ph, f"sbl{i}", [128, 2, 256], BF16) for i in range(2)])
                ost = sb(ph, "ost", [128, 12], F32)
                o_g = sb(ph, "o_g", [128, 512], BF16)
                ecg2 = sb(ph, "ecg2", [128, 2, 128], F32)
                growp = Ring([sb(ph, f"growp{i}", [128, 2, 4, 94], BF16) for i in range(2)])
                mixT_r = Ring([sb(ph, f"mixT{i}", [128, 8, 256], BF16) for i in range(CFG["p2_mixT"])])
                y32 = sb(ph, "y32", [128, 4, 256], F32)
                yb = sb(ph, "yb", [128, 4, 256], BF16)
                ysq = sb(ph, "ysq", [128, 4, 256], BF16)
                lnm = sb(ph, "lnm", [128, 256], F32)
                lnv = sb(ph, "lnv", [128, 256], F32)
                lnr = sb(ph, "lnr", [128, 256], F32)
                ynt = Ring([sb(ph, f"ynt{i}", [128, 256], F32) for i in range(1)])
                eyn = Ring([sb(ph, f"eyn{i}", [128, 256], F32) for i in range(1)])
                xr = Ring([sb(ph, f"xr{i}", [128, D], F32) for i in range(CFG["p2_xr"])])
                for gt in growp.tiles:
                    S.op("pool", f_memset(gt.t[:], 0.0), writes=[gt.b])

                def mixer_tile(t, lt, grow, mixT):
                    g2 = g2r.next()
                    qk_s = qk_r.next()
                    eog = eog_r.next()
                    sog = sog_r.next()
                    sbt2 = sbl.next()
                    S.op("sp", f_dma(sbt2.t[:], sbp_dram[2 * t:2 * t + 2].rearrange("c p f -> p c f")),
                         writes=[sbt2.b], dma=sbt2.b)
                    hT = front(fr2, x_seq[t * 128:(t + 1) * 128, :], 128, s1b, b1b)
                    zTa = zTa_r.next()
                    zcur["t"] = zTa
                    S.op("sp", f_dma(g2["ktok"].t[:], kt_d[t]), writes=[g2["ktok"].b], dma=g2["ktok"].b)
                    S.op("sp", f_dma(g2["vbf"].t[:], v_d[t]), writes=[g2["vbf"].b], dma=g2["vbf"].b)
                    S.op("sp", f_dma(zTa.t[0:32, :], z_d[t]), writes=[zTa.b], dma=zTa.b)
                    if p2_sub <= 1:
                        return
                    for m in range(4):
                        col = (C_CU + m * 128) if m < 2 else (C_CG + (m - 2) * 128)
                        for kc in range(8):
                            S.op("pe", f_mm(p1.t[:, m * 128:(m + 1) * 128], w_in_sb.t[:, kc, col:col + 128],
                                            hT.t[:, kc, :], kc == 0, kc == 7),
                                 reads=[hT.b], extra=w_all_ops, writes=[p1.b])
                    for m in range(4):
                        col = C_Q + m * 128
                        for kc in range(8):
                            S.op("pe", f_mm(p2.t[:, m * 128:(m + 1) * 128], w_in_sb.t[:, kc, col:col + 128],
                                            hT.t[:, kc, :], kc == 0, kc == 7),
                                 reads=[hT.b], extra=w_all_ops, writes=[p2.b])
                    for kc in range(8):
                        S.op("pe", f_mm(p3.t[:, :], hT.t[:, kc, :], w_in_sb.t[:, kc, C_OG:C_OG + 512], kc == 0, kc == 7),
                             reads=[hT.b], extra=w_all_ops, writes=[p3.b])
                    if p2_sub <= 2:
                        return
                    p1v = p1.t[:, :].rearrange("p (a b) -> p a b", a=4)
                    act_sigmoid(ecg2.t[:], p1v[:, 2:4, :], ecg2.b, p1.b)
                    for c in range(2):
                        S.op("dve", f_tt(grow.t[:, c, 2 * lt:2 * lt + 2, 15:79],
                                         p1v[:, c, :].rearrange("p (r w) -> p r w", r=2),
                                         ecg2.t[:, c, :].rearrange("p (r w) -> p r w", r=2), ALU.mult),
                             reads=[p1.b, ecg2.b], writes=[grow.b])
                    if p2_sub <= 3:
                        return
                    S.op("act", f_act(qk_s.t[:], p2.t[:, :].rearrange("p (a b) -> p a b", a=4), AF.Copy),
                         reads=[p2.b], writes=[qk_s.b])
                    act_sigmoid(eog.t[:], p3.t[:, :], eog.b, p3.b)
                    S.op("dve", f_tt(sog.t[:], p3.t[:, :], eog.t[:], ALU.mult), reads=[p3.b, eog.b], writes=[sog.b])
                    if p2_sub <= 4:
                        return
                    gnF = gate("F", g2)
                    gnB = gate("B", g2)
                    Et, ktail, dec = g2["Et"], g2["ktail"], g2["dec"]
                    S.op("pe", f_mm(p6.t[:, 256:512], masks_b.t[:, M_US, :], gnF.t[:], True, True),
                         reads=[masks_b.b, gnF.b], writes=[p6.b])
                    for j in range(2):
                        S.op("pe", f_mm(p6.t[:, 2 * j:2 * j + 2], gnF.t[:, j * 128:(j + 1) * 128], ind_b.t[:], True, True),
                             reads=[gnF.b, ind_b.b], writes=[p6.b])
                    S.op("act", f_act(Et.t[:], p6.t[:, 256:512], AF.Exp, scale=-1.0 / 16), reads=[p6.b], writes=[Et.b])
                    S.op("act", f_act(dec.t[:], p6.t[:, 0:4], AF.Exp, scale=-1.0 / 16), reads=[p6.b], writes=[dec.b])
                    S.op("dve", f_tt(ktail.t[:], g2["ktok"].t[:], Et.t[:], ALU.mult),
                         reads=[g2["ktok"].b, Et.b], writes=[ktail.b])
                    for xi, (X, gn, mk) in enumerate((("F", gnF, M_LI), ("B", gnB, M_UI))):
                        for j in range(2):
                            S.op("pe", f_mm(p7.t[:, xi * 256 + j * 128:xi * 256 + (j + 1) * 128],
                                            gn.t[:, j * 128:(j + 1) * 128], masks_b.t[:, mk, :], True, True),
                                 reads=[gn.b, masks_b.b], writes=[p7.b])
                    for xi, X in enumerate("FB"):
                        src = p7.t[:, xi * 256:(xi + 1) * 256].rearrange("p (a b) -> p a b", a=2)
                        S.op("act", f_act(EE[X + "p"].t[:], src, AF.Exp, scale=-1.0 / 16, bias=float(np.log(0.125))),
                             reads=[p7.b], writes=[EE[X + "p"].b])
                        S.op("act", f_act(EE[X + "n"].t[:], src, AF.Exp, scale=1.0 / 16),
                             reads=[p7.b], writes=[EE[X + "n"].b])
                        S.op("dve", f_tt(QK[X + "q"].t[:], qk_s.t[:, 0:2, :], EE[X + "p"].t[:], ALU.mult),
                             reads=[qk_s.b, EE[X + "p"].b], writes=[QK[X + "q"].b])
                        S.op("dve", f_tt(QK[X + "k"].t[:], qk_s.t[:, 2:4, :], EE[X + "n"].t[:], ALU.mult),
                             reads=[qk_s.b, EE[X + "n"].b], writes=[QK[X + "k"].b])
                    if p2_sub <= 5:
                        return
                    for e_ in range(2):
                        pp = p6 if e_ == 0 else p7
                        sl = slice(e_ * 64, (e_ + 1) * 64)
                        for xi, X in enumerate("FB"):
                            for j in range(2):
                                slot = xi * 2 + j
                                S.op("pe", f_mm(pp.t[:, slot * 128:(slot + 1) * 128], QK[X + "k"].t[sl, j, :],
                                                QK[X + "q"].t[sl, j, :], True, True),
                                     reads=[QK[X + "k"].b, QK[X + "q"].b], writes=[pp.b])
                    for e_ in range(2):
                        pp = p6 if e_ == 0 else p7
                        for xi, X in enumerate("FB"):
                            mk = M_LI if X == "F" else M_UI
                            for j in range(2):
                                slot = xi * 2 + j
                                h = 2 * j + e_
                                S.op("dve", f_tt(scm.t[:, xi * 4 + h, :], pp.t[:, slot * 128:(slot + 1) * 128],
                                                 masks.t[:, mk, :], ALU.mult),
                                     reads=[pp.b, masks.b], writes=[scm.b])
                    if p2_sub <= 6:
                        return
                    S_prev_tiles = {}
                    vbf = g2["vbf"]
                    for lc in range(2):
                        sf = SFbf.next()
                        S.op("act", f_act(sf.t[:], SF.t[:], AF.Copy), reads=SF.regs, writes=[sf.b])
                        S_prev_tiles[lc] = sf
                        for j in range(2):
                            S.op("pe", f_mm(p4.t[:, j * 256:(j + 1) * 256],
                                            ktail.t[lc * 64:(lc + 1) * 64, j * 128:(j + 1) * 128],
                                            vbf.t[lc * 64:(lc + 1) * 64, j * 256:(j + 1) * 256], True, True),
                                 reads=[ktail.b, vbf.b], writes=[p4.b])
                        for j in range(2):
                            for e_ in range(2):
                                sl = slice(e_ * 64, (e_ + 1) * 64)
                                S.op("dve", f_stt(SF.t[sl, j, :], SF.t[sl, j, :], dec.t[sl, 2 * j + lc:2 * j + lc + 1],
                                                  p4.t[sl, j * 256 + e_ * 128:j * 256 + (e_ + 1) * 128],
                                                  ALU.mult, ALU.add),
                                     reads=[SF.regs[2 * j + e_], dec.b, p4.b], writes=[SF.regs[2 * j + e_]])
                    if p2_sub <= 7:
                        return
                    for h in range(4):
                        j, e_ = h // 2, h % 2
                        sl = slice(e_ * 64, (e_ + 1) * 64)
                        oc = slice(h * 128, (h + 1) * 128)
                        S.op("pe", f_mm(p5.t[:, oc], scm.t[:, h, :], vbf.t[:, oc], True, False),
                             reads=[scm.b, vbf.b], writes=[p5.b])
                        S.op("pe", f_mm(p5.t[:, oc], scm.t[:, 4 + h, :], vbf.t[:, oc], False, False),
                             reads=[scm.b, vbf.b], writes=[p5.b])
                        for lc in range(2):
                            tl = slice(lc * 64, (lc + 1) * 64)
                            S.op("pe", f_mm(p5.t[tl, oc], QK["Fq"].t[sl, j, tl], S_prev_tiles[lc].t[sl, j, :], False, False),
                                 reads=[QK["Fq"].b, S_prev_tiles[lc].b], writes=[p5.b])
                            S.op("pe", f_mm(p5.t[tl, oc], QK["Bq"].t[sl, j, tl], sbt2.t[sl, lc, j * 128:(j + 1) * 128], False, True),
                                 reads=[QK["Bq"].b, sbt2.b], writes=[p5.b])
                    if p2_sub <= 8:
                        return
                    for h in range(4):
                        S.op("act", f_act(junk.t[:, h * 128:(h + 1) * 128], p5.t[:, h * 128:(h + 1) * 128], AF.Square,
                                          accum=ost.t[:, h:h + 1]),
                             reads=[p5.b], writes=[ost.b, junk.b])
                    S.op("act", f_act(ost.t[:, 4:8], ost.t[:, 0:4], AF.Ln, bias=EPS, scale=1.0 / 128), reads=[ost.b], writes=[ost.b])
                    S.op("act", f_act(ost.t[:, 8:12], ost.t[:, 4:8], AF.Exp, scale=-0.5), reads=[ost.b], writes=[ost.b])
                    for h in range(4):
                        oc = slice(h * 128, (h + 1) * 128)
                        S.op("dve", f_stt(o_g.t[:, oc], p5.t[:, oc], ost.t[:, 8 + h:9 + h], sog.t[:, oc], ALU.mult, ALU.mult),
                             reads=[p5.b, ost.b, sog.b], writes=[o_g.b])
                    if p2_sub <= 9:
                        return
                    for h in range(4):
                        S.op("pe", f_mm(p7.t[:, h * 128:(h + 1) * 128], o_g.t[:, h * 128:(h + 1) * 128], ident_b.t[:, :], True, True),
                             reads=[o_g.b, ident_b.b], writes=[p7.b])
                    S.op("act", f_act(mixT.t[:, 4:8, lt * 128:(lt + 1) * 128],
                                      p7.t[:, 0:512].rearrange("p (a b) -> p a b", a=4), AF.Copy),
                         reads=[p7.b], writes=[mixT.b])

                def conv_block(t0, ntl, grow, mixT):
                    n = ntl * 128
                    nr = 2 * ntl
                    r0 = 2 * t0
                    cps = {0: p1, 1: p1, 2: p2, 3: p2}
                    for c in range(4):
                        pp = cps[c]
                        oc = slice((c % 2) * 256, (c % 2) * 256 + n)
                        for k in range(31):
                            if c < 2:
                                rhs = grow.t[:, c, 0:nr, k:k + 64]
                            else:
                                rhs = gcol.t[:, c - 2, (r0 + k) * 64:(r0 + k) * 64 + n]
                            S.op("pe", f_mm(pp.t[:, oc], diagT.t[:, c * 31 + k, :], rhs, k == 0, k == 30),
                                 reads=[diagT.b, grow.b if c < 2 else gcol.b], writes=[pp.b])
                    for c in range(4):
                        pp = cps[c]
                        oc = slice((c % 2) * 256, (c % 2) * 256 + n)
                        S.op("act", f_act(y32.t[:, c, 0:n], pp.t[:, oc], AF.Identity, bias=cvec.t[:, 0, c:c + 1]),
                             reads=[pp.b, cvec.b], writes=[y32.b])
                        S.op("act", f_act(ysq.t[:, c, 0:n], pp.t[:, oc], AF.Square, bias=cvec.t[:, 0, c:c + 1]),
                             reads=[pp.b, cvec.b], writes=[ysq.b])
                        S.op("dve", f_cp(yb.t[:, c, 0:n], y32.t[:, c, 0:n]), reads=[y32.b], writes=[yb.b])
                    for c in range(4):
                        S.op("pe", f_mm(p6.t[:, 0:n], onesM.t[:], yb.t[:, c, 0:n], c == 0, c == 3),
                             reads=[onesM.b, yb.b], writes=[p6.b])
                    for c in range(4):
                        S.op("pe", f_mm(p6.t[:, 256:256 + n], onesM.t[:], ysq.t[:, c, 0:n], c == 0, c == 3),
                             reads=[onesM.b, ysq.b], writes=[p6.b])
                    S.op("act", f_act(lnm.t[:, 0:n], p6.t[:, 0:n], AF.Copy), reads=[p6.b], writes=[lnm.b])
                    S.op("dve", f_tt(lnv.t[:, 0:n], lnm.t[:, 0:n], lnm.t[:, 0:n], ALU.mult), reads=[lnm.b], writes=[lnv.b])
                    S.op("dve", f_tt(lnv.t[:, 0:n], p6.t[:, 256:256 + n], lnv.t[:, 0:n], ALU.subtract),
                         reads=[p6.b, lnv.b], writes=[lnv.b])
                    S.op("act", f_act(lnr.t[:, 0:n], lnv.t[:, 0:n], AF.Ln, bias=EPS), reads=[lnv.b], writes=[lnr.b])
                    S.op("act", f_act(lnr.t[:, 0:n], lnr.t[:, 0:n], AF.Exp, scale=-0.5), reads=[lnr.b], writes=[lnr.b])
                    for c in range(4):
                        yn = ynt.next()
                        ey = eyn.next()
                        S.op("dve", f_tt(yn.t[:, 0:n], y32.t[:, c, 0:n], lnm.t[:, 0:n], ALU.subtract),
                             reads=[y32.b, lnm.b], writes=[yn.b])
                        S.op("dve", f_tt(yn.t[:, 0:n], yn.t[:, 0:n], lnr.t[:, 0:n], ALU.mult), reads=[yn.b, lnr.b], writes=[yn.b])
                        S.op("dve", f_ts(yn.t[:, 0:n], yn.t[:, 0:n], cvec.t[:, 1, c:c + 1], cvec.t[:, 2, c:c + 1], ALU.mult, ALU.add),
                             reads=[yn.b, cvec.b], writes=[yn.b])
                        act_sigmoid(ey.t[:, 0:n], yn.t[:, 0:n], ey.b, yn.b)
                        S.op("dve", f_tt(mixT.t[:, c, 0:n], yn.t[:, 0:n], ey.t[:, 0:n], ALU.mult),
                             reads=[yn.b, ey.b], writes=[mixT.b])
                    for lt in range(ntl):
                        t = t0 + lt
                        xrt = xr.next()
                        S.op("sp", f_dma(xrt.t[:], x_seq[t * 128:(t + 1) * 128, :]), writes=[xrt.b], dma=xrt.b)
                        for half, pp in ((0, p3), (1, p5)):
                            for kc in range(8):
                                S.op("pe", f_mm(pp.t[:, :], mixT.t[:, kc, lt * 128:(lt + 1) * 128],
                                                w_out_sb.t[:, kc, half * 512:(half + 1) * 512], kc == 0, kc == 7),
                                     reads=[mixT.b], writes=[pp.b])
                        for half, pp in ((0, p3), (1, p5)):
                            hs = slice(half * 512, (half + 1) * 512)
                            S.op("dve", f_tt(xrt.t[:, hs], pp.t[:, :], xrt.t[:, hs], ALU.add),
                                 reads=[pp.b, xrt.b], writes=[xrt.b])
                        S.op("sp", f_dma(x1s[t * 128:(t + 1) * 128, :], xrt.t[:]), reads=[xrt.b], dma=xrt.b)

                t = 0
                while t < EXT:
                    ntl = min(2, EXT - t)
                    grow = growp.next()
                    mixT = mixT_r.next()
                    for lt in range(ntl):
                        mixer_tile(t + lt, lt, grow, mixT)
                    if p2_conv:
                        conv_block(t, ntl, grow, mixT)
                    t += ntl
                    if t >= 2 * p2_blocks:
                        break
                if dbg:
                    S.op("sp", f_dma(d_st[3], SF.t[:].rearrange("p a b -> p (a b)")), reads=SF.regs, dma=Buf("dd9"))
                S.barrier()
                S.flush()
        if stop <= 2:
            return nc

        with ExitStack() as ffn:
            w_up_sb = sb(ffn, "w_up_sb", [128, 8, 2 * HID], BF16)
            w_dn_sb = sb(ffn, "w_dn_sb", [128, NH, D], BF16)
            g2b = sb(ffn, "g2b", [128, D], F32)
            w_up_v = w_up_bf.rearrange("(kc p) n -> p kc n", p=128)
            NG = 8
            gsz = [3, 3, 3, 3, 3, 3, 2, 2]
            gst = [sum(gsz[:g]) for g in range(NG)]
            w_up_grp = {}
            for g in range(NG):
                gb = Buf(f"wupg{g}")
                ops_ = []
                for hh in range(2):
                    c0_ = hh * HID + gst[g] * 128
                    c1_ = c0_ + gsz[g] * 128
                    ops_.append(S.op(("sp", "act")[g % 2], f_dma(w_up_sb.t[:, :, c0_:c1_], w_up_v[:, :, c0_:c1_]), dma=gb))
                for jj in range(gst[g], gst[g] + gsz[g]):
                    w_up_grp[jj] = ops_
            w_dn_v = w_dn_bf.rearrange("(j p) n -> p j n", p=128)
            S.op("sp", f_dma(g2b.t[:], modsave[2, :].partition_broadcast(128)), writes=[g2b.b], dma=g2b.b)
            w_dn_regs = {}
            for j0 in range(0, NH, 2):
                rb = Buf(f"wdn{j0}")
                S.op(("sp", "act")[(j0 // 2) % 2], f_dma(w_dn_sb.t[:, j0:j0 + 2, :], w_dn_v[:, j0:j0 + 2, :]), writes=[rb], dma=rb)
                for j in (j0, j0 + 1):
                    S.op("dve", f_tt(w_dn_sb.t[:, j, :], w_dn_sb.t[:, j, :], g2b.t[:], ALU.mult),
                         reads=[rb, g2b.b], writes=[rb])
                    w_dn_regs[j] = rb

            s2b = sb(ffn, "s2b", [128, D], F32)
            b2b = sb(ffn, "b2b", [128, D], F32)
            fgb = sb(ffn, "fgb", [128, D], F32)
            fdw = sb(ffn, "fdw", [128, NH, 4], F32)
            fr3 = make_front(ffn, "f3", nx=CFG["f3_nx"], nh=1)
            h2Tr = Ring([sb(ffn, f"h2T{i}", [128, 8, FB], BF16) for i in range(CFG["h2T"])])
            abuf = Ring([sb(ffn, f"abuf{i}", [128, FB + 2], F32) for i in range(CFG["ffn_ring"])])
            vbuf = Ring([sb(ffn, f"vbuf{i}", [128, FB + 1], F32) for i in range(CFG["ffn_ring"])])
            tcv = Ring([sb(ffn, f"tcv{i}", [128, FB], F32) for i in range(CFG["ffn_ring"])])
            esg = Ring([sb(ffn, f"esg{i}", [128, FB], F32) for i in range(CFG["ffn_ring"])])
            car = sb(ffn, "car", [128, NH, 3], F32)
            hidT = sb(ffn, "hidT", [128, NH, FB], BF16)
            hidT.regs = [Buf(f"hid{j}") for j in range(NH)]
            xrf = Ring([sb(ffn, f"xrf{i}", [128, D], F32) for i in range(2)])
            fst = Ring([sb(ffn, f"fst{i}", [128, 4], F32) for i in range(2)])

            S.op("sp", f_dma(s2b.t[:], modsave[0, :].partition_broadcast(128)), writes=[s2b.b], dma=s2b.b)
            S.op("sp", f_dma(b2b.t[:], modsave[1, :].partition_broadcast(128)), writes=[b2b.b], dma=b2b.b)
            S.op("sp", f_dma(fgb.t[:], final_g.partition_broadcast(128)), writes=[fgb.b], dma=fgb.b)
            S.op("sp", f_dma(fdw.t[:], fdw_fm[:, :, :]), writes=[fdw.b], dma=fdw.b)
            S.op("dve", f_memset(car.t[:], 0.0), writes=[car.b])
            for xt_ in xrf.tiles:
                S.op("pool", f_memset(xt_.t[:], 0.0), writes=[xt_.b])

            blocks = [(b * FB, FB) for b in range(SEQ // 2 // FB)] + [(SEQ // 2, 1)]
            for (t0, n) in blocks:
                h2T = h2Tr.next()
                for m in range((n + 127) // 128):
                    r0 = t0 + m * 128
                    nn = min(128, n - m * 128)
                    front(fr3, x1s[r0:r0 + nn, :], nn, s2b, b2b, dst=(h2T, h2T.t[:, :, m * 128:m * 128 + nn]))
                for j in range(NH):
                    pp = pbank[j % 4]
                    ab = abuf.next()
                    vb = vbuf.next()
                    for half in range(2):
                        col = half * HID + j * 128
                        for kc in range(8):
                            S.op("pe", f_mm(pp.t[:, half * 256:half * 256 + n], w_up_sb.t[:, kc, col:col + 128],
                                            h2T.t[:, kc, 0:n], kc == 0, kc == 7),
                                 reads=[h2T.b], extra=w_up_grp[j], writes=[pp.b])
                    S.op("pool", f_cp(ab.t[:, 0:2], car.t[:, j, 0:2]), reads=[car.b], writes=[ab.b])
                    S.op("pool", f_cp(vb.t[:, 0:1], car.t[:, j, 2:3]), reads=[car.b], writes=[vb.b])
                    S.op("act", f_act(ab.t[:, 2:2 + n], pp.t[:, 0:n], AF.Copy), reads=[pp.b], writes=[ab.b])
                    if CFG["ffn_bal"]:
                        S.op("dve", f_cp(vb.t[:, 1:1 + n], pp.t[:, 256:256 + n]), reads=[pp.b], writes=[vb.b])
                    else:
                        S.op("act", f_act(vb.t[:, 1:1 + n], pp.t[:, 256:256 + n], AF.Copy), reads=[pp.b], writes=[vb.b])
                    S.op("pool", f_cp(car.t[:, j, 0:2], ab.t[:, n:n + 2]), reads=[ab.b], writes=[car.b])
                    S.op("pool", f_cp(car.t[:, j, 2:3], vb.t[:, n:n + 1]), reads=[vb.b], writes=[car.b])
                    tc_ = tcv.next()
                    es = esg.next()
                    S.op("dve", f_ts(tc_.t[:, 0:n], ab.t[:, 0:n], fdw.t[:, j, 0:1], fdw.t[:, j, 3:4], ALU.mult, ALU.add),
                         reads=[ab.b, fdw.b], writes=[tc_.b])
                    S.op("dve", f_stt(tc_.t[:, 0:n], ab.t[:, 1:n + 1], fdw.t[:, j, 1:2], tc_.t[:, 0:n], ALU.mult, ALU.add),
                         reads=[ab.b, fdw.b, tc_.b], writes=[tc_.b])
                    S.op("dve", f_stt(tc_.t[:, 0:n], ab.t[:, 2:n + 2], fdw.t[:, j, 2:3], tc_.t[:, 0:n], ALU.mult, ALU.add),
                         reads=[ab.b, fdw.b, tc_.b], writes=[tc_.b])
                    act_sigmoid(es.t[:, 0:n], tc_.t[:, 0:n], es.b, tc_.b)
                    S.op("pool" if CFG["ffn_bal"] == 1 else "dve", f_tt(tc_.t[:, 0:n], tc_.t[:, 0:n], vb.t[:, 0:n], ALU.mult),
                         reads=[tc_.b, vb.b], writes=[tc_.b])
                    S.op("dve", f_tt(hidT.t[:, j, 0:n], tc_.t[:, 0:n], es.t[:, 0:n], ALU.mult),
                         reads=[tc_.b, es.b], writes=[hidT.b])
                for m in range((n + 127) // 128):
                    tk0 = t0 - 1 + m * 128
                    lo = max(0, -tk0)
                    hi = min(128, SEQ // 2 - tk0, n - m * 128)
                    if hi <= lo:
                        continue
                    xt_ = xrf.next()
                    st = fst.next()
                    S.op("sp", f_dma(xt_.t[lo:hi, :], x1s[tk0 + lo:tk0 + hi, :]), writes=[xt_.b], dma=xt_.b)
                    for half, pp in ((0, p5), (1, p6)):
                        for j in range(NH):
                            S.op("pe", f_mm(pp.t[0:hi, :], hidT.t[:, j, m * 128:m * 128 + hi],
                                            w_dn_sb.t[:, j, half * 512:(half + 1) * 512], j == 0, j == NH - 1),
                                 reads=[hidT.b, w_dn_regs[j]], writes=[pp.b])
                    for half, pp in ((0, p5), (1, p6)):
                        hs = slice(half * 512, (half + 1) * 512)
                        S.op("dve", f_tt(xt_.t[0:hi, hs], pp.t[0:hi, :], xt_.t[0:hi, hs], ALU.add), reads=[pp.b, xt_.b], writes=[xt_.b])
                    S.op("act", f_act(junk.t[0:hi, :], xt_.t[0:hi, :], AF.Square, accum=st.t[0:hi, 0:1]), reads=[xt_.b], writes=[st.b, junk.b])
                    S.op("act", f_act(st.t[0:hi, 1:2], st.t[0:hi, 0:1], AF.Ln, bias=EPS, scale=1.0 / D), reads=[st.b], writes=[st.b])
                    S.op("act", f_act(st.t[0:hi, 2:3], st.t[0:hi, 1:2], AF.Exp, scale=-0.5), reads=[st.b], writes=[st.b])
                    S.op("dve", f_stt(xt_.t[0:hi, :], xt_.t[0:hi, :], st.t[0:hi, 2:3], fgb.t[0:hi, :], ALU.mult, ALU.mult),
                         reads=[xt_.b, st.b, fgb.b], writes=[xt_.b])
                    S.op("sp", f_dma(y_out[tk0 + lo:tk0 + hi, :], xt_.t[lo:hi, :]), reads=[xt_.b], dma=xt_.b)
            S.barrier()
            S.flush()
    return nc


def _consts():
    idx = np.arange(128)
    same = (idx[:, None] // 64) == (idx[None, :] // 64)
    a, b = idx[:, None], idx[None, :]
    masks = np.stack([same & (a <= b), same & (a >= b), same & (a < b), same & (a > b)], axis=1).astype(np.float32)
    ind = ((idx[:, None] // 64) == np.arange(2)[None, :]).astype(np.float32)
    return np.eye(128, dtype=np.float32), np.ascontiguousarray(masks), ind


def _fm(v, nchunk):
    return np.ascontiguousarray(np.asarray(v, np.float32).reshape(nchunk, 128).T)


def make_in_maps(x, c, ctx, c_ctx, w_mod, b_mod, norm1_g, w_in, conv_dw, conv_b, conv_ln_g, conv_ln_b,
                 w_gf, b_gf, w_gb, b_gb, gla_norm_g, w_out, norm2_g, w_up, ffn_dw, ffn_dw_b, w_down, final_g):
    f = lambda a: np.asarray(a, dtype=np.float32)
    x, c, ctx, c_ctx = f(x), f(c), f(ctx), f(c_ctx)
    ident, masks, ind = _consts()
    w_in0 = f(w_in)[0]
    w_in_rev = np.ascontiguousarray(np.concatenate([w_in0[:, :C_Z], w_in0[:, C_Z + 16:C_Z + 32], w_in0[:, C_Z:C_Z + 16]], axis=1))
    cdw = f(conv_dw)[0]
    fdw_ = f(ffn_dw)[0]
    zeros16 = np.zeros((16, 256), np.float32)
    gate = {"f": (f(w_gf)[0], f(b_gf)[0]), "b": (f(w_gb)[0], f(b_gb)[0])}
    cvec = np.stack([_fm(f(conv_b)[0], 4), _fm(f(conv_ln_g)[0], 4), _fm(f(conv_ln_b)[0], 4)], axis=1)
    common = dict(
        cctx_fm=_fm(c_ctx, 8), w_mod=f(w_mod)[0], b_mod=f(b_mod)[0], norm1_g=f(norm1_g)[0],
        cvec_fm=np.ascontiguousarray(cvec), gla_norm_fm=_fm(f(gla_norm_g)[0], 4), w_out=f(w_out)[0], norm2_g=f(norm2_g)[0],
        w_up=f(w_up)[0], w_down=f(w_down)[0], final_g=f(final_g), ident=ident, masks=masks, ind=ind,
    )
    in_maps = []
    for core in range(8):
        b, rev = core // 2, core % 2
        xs = x[b][::-1] if rev else x[b]
        cs = ctx[b][::-1] if rev else ctx[b]
        cd = cdw[::-1] if rev else cdw
        fd = fdw_[::-1] if rev else fdw_
        pF, pB = ("b", "f") if rev else ("f", "b")
        wgF_aug = np.concatenate([gate[pF][0], zeros16, gate[pF][1][None, :]], axis=0)
        wgB_aug = np.concatenate([zeros16, gate[pB][0], gate[pB][1][None, :]], axis=0)
        cw_fm = np.ascontiguousarray(cd.T.reshape(4, 128, 31).transpose(1, 0, 2))
        fdw_fm = np.concatenate([fd.T.reshape(NH, 128, 3).transpose(1, 0, 2),
                                 f(ffn_dw_b)[0].reshape(NH, 128).T[:, :, None]], axis=2)
        m = dict(common)
        m.update(
            x_seq=np.ascontiguousarray(xs), ctx_seq=np.ascontiguousarray(cs), c_fm=_fm(c[b], 8),
            w_in=w_in_rev if rev else w_in0, cw_fm=cw_fm, wgF_aug=np.ascontiguousarray(wgF_aug),
            wgB_aug=np.ascontiguousarray(wgB_aug), fdw_fm=np.ascontiguousarray(fdw_fm),
        )
        in_maps.append(m)
    return in_maps


def kernel(**inputs):
    in_maps = make_in_maps(**inputs)
    nc = build_program()
    res = run_bass_kernel_spmd(nc, in_maps, core_ids=list(range(8)))
    out = np.empty((4, SEQ, D), np.float32)
    for core in range(8):
        b, rev = core // 2, core % 2
        y = np.asarray(res.results[core]["y_out"], np.float32)
        if rev:
            out[b, SEQ // 2:] = y[::-1]
        else:
            out[b, :SEQ // 2] = y
    return out
```

```python
from contextlib import ExitStack

import numpy as np
import concourse.bass as bass
import concourse.mybir as mybir
from concourse.bass_utils import run_bass_kernel_spmd

F32 = mybir.dt.float32
BF16 = mybir.dt.bfloat16
AF = mybir.ActivationFunctionType
ALU = mybir.AluOpType

D = 1024
SEQ = 8192
CTX = 256
NT = 64
EXT = 33
HALO_T = 41
DIN = 2592
C_CU, C_CG, C_Q, C_K, C_V, C_OG, C_Z = 0, 512, 1024, 1280, 1536, 2048, 2560
HID = 2816
NH = 22
EPS = 1e-6
FB = 256
SEM_LIMIT = 4000


class Buf:
    __slots__ = ("name", "writer", "readers", "dsem", "dcount", "psum")

    def __init__(self, name, psum=False):
        self.name = name
        self.writer = None
        self.readers = []
        self.dsem = None
        self.dcount = 0
        self.psum = psum


class Op:
    __slots__ = ("idx", "stream", "fn", "deps", "is_dma", "dbuf", "needed", "signal", "cost", "pos")

    def __init__(self, idx, stream, fn, deps, is_dma, dbuf, cost):
        self.idx = idx
        self.stream = stream
        self.fn = fn
        self.deps = deps
        self.is_dma = is_dma
        self.dbuf = dbuf
        self.needed = False
        self.signal = None
        self.cost = cost
        self.pos = -1


class Fn:
    __slots__ = ("f", "cost")

    def __init__(self, f, cost):
        self.f = f
        self.cost = cost

    def __call__(self, e):
        return self.f(e)


CFG = {"f3_nx": 1, "h2T": 1, "p2_hm": 1, "p2_gate": 1, "p2_qk": 1, "p2_og": 1, "p2_xr": 2, "p2_mixT": 1, "p1_hm": 1, "add_eng": "dve", "kv_p2": 1, "p1_gate": 2, "hT_dve": 0, "ffn_ring": 3, "ffn_bal": 2}
SCHED_WINDOW = 600
REORDER = True
SEM_LAT = 0.23
DEFAULT_COST = {"pe": 0.12, "act": 0.4, "dve": 0.3, "pool": 0.5, "sp": 3.0}


class Sched:
    STREAMS = ("pe", "act", "dve", "pool", "sp")

    def __init__(self, nc, stack):
        self.nc = nc
        self.stack = stack
        self.ops = []
        self.order = {s: [] for s in self.STREAMS}
        self.cursor = {s: 0 for s in self.STREAMS}
        self.floor = 0
        self.flushed = 0
        self.nsem = 0
        self.cur = {s: [None, 0] for s in self.STREAMS}
        self.waited = {s: {} for s in self.STREAMS}
        self.last_compute = {s: None for s in self.STREAMS}
        self.phase_dmas = []
        self.sim_time = []

    def new_sem(self, name):
        self.nsem += 1
        return self.stack.enter_context(self.nc.semaphore(f"{name}_{self.nsem}"))

    def op(self, stream, fn, reads=(), writes=(), dma=None, extra=()):
        idx = len(self.ops)
        deps = set(extra)
        for b in reads:
            if b.writer is not None:
                deps.add(b.writer)
            if b.psum:
                for r in b.readers:
                    if self.ops[r].stream != stream:
                        deps.add(r)
        for b in writes:
            if b.writer is not None:
                deps.add(b.writer)
            deps.update(b.readers)
        deps = sorted(d for d in deps if d >= self.floor and d != idx)
        if fn is None:
            cost = 0.0
        else:
            cost = getattr(fn, "cost", None)
            if cost is None:
                cost = DEFAULT_COST["sp" if dma is not None else stream]
            elif stream == "pool" and dma is None:
                cost = cost * 3.5
        o = Op(idx, stream, fn, deps, dma is not None, dma, cost)
        self.ops.append(o)
        for b in reads:
            b.readers.append(idx)
        for b in writes:
            b.writer = idx
            b.readers = []
        if dma is not None:
            self.phase_dmas.append(idx)
        elif fn is not None:
            self.last_compute[stream] = idx
        return idx

    def barrier(self):
        deps = list(range(self.floor, len(self.ops)))
        for s in self.STREAMS:
            self.op(s, None, extra=deps)
        self.floor = len(self.ops)
        self.phase_dmas = []
        self.last_compute = {s: None for s in self.STREAMS}

    def _schedule(self, lo, hi):
        ops = self.ops
        n = hi - lo
        if not REORDER:
            order = {s: [] for s in self.STREAMS}
            for o in ops[lo:hi]:
                order[o.stream].append(o)
            self.sim_time.append(0.0)
            return order
        indeg = [0] * n
        users = [[] for _ in range(n)]
        for o in ops[lo:hi]:
            c = 0
            for d in o.deps:
                if d >= lo:
                    users[d - lo].append(o.idx)
                    c += 1
            indeg[o.idx - lo] = c
        ready = {s: [] for s in self.STREAMS}
        rtime = [0.0] * n
        free_at = {s: 0.0 for s in self.STREAMS}
        for o in ops[lo:hi]:
            if indeg[o.idx - lo] == 0:
                ready[o.stream].append(o.idx)
        order = {s: [] for s in self.STREAMS}
        remaining = n
        tmax = 0.0
        while remaining:
            best = None
            for s in self.STREAMS:
                lst = ready[s]
                if not lst:
                    continue
                fa = free_at[s]
                mi = min(lst)
                for i in lst:
                    if i > mi + SCHED_WINDOW:
                        continue
                    st = rtime[i - lo]
                    if st < fa:
                        st = fa
                    if best is None or (st, i) < best[0]:
                        best = ((st, i), s, i)
            (st, _), s, i = best
            o = ops[i]
            if o.is_dma:
                free_at[s] = st + 0.07
            else:
                free_at[s] = st + o.cost
            fin = st + o.cost
            if fin > tmax:
                tmax = fin
            ready[s].remove(i)
            order[s].append(o)
            remaining -= 1
            for u in users[i - lo]:
                k = u - lo
                lat = 0.0 if (ops[u].stream == s == "pe") else SEM_LAT
                if fin + lat > rtime[k]:
                    rtime[k] = fin + lat
                indeg[k] -= 1
                if indeg[k] == 0:
                    ready[ops[u].stream].append(u)
        self.sim_time.append(tmax)
        return order

    def _assign(self, order):
        ops = self.ops
        for s in self.STREAMS:
            base = len(self.order[s])
            for k, o in enumerate(order[s]):
                o.pos = base + k
        for s in self.STREAMS:
            for o in order[s]:
                latest = {}
                for d in o.deps:
                    p = ops[d]
                    if p.is_dma:
                        p.needed = True
                        continue
                    if p.fn is None:
                        continue
                    if p.stream == o.stream and p.stream in ("pe", "sp"):
                        continue
                    q = latest.get(p.stream)
                    if q is None or p.pos > q.pos:
                        latest[p.stream] = p
                for p in latest.values():
                    p.needed = True
        for s in self.STREAMS:
            for o in order[s]:
                if not o.needed:
                    continue
                if o.is_dma:
                    b = o.dbuf
                    if b.dsem is None or b.dcount + 16 > SEM_LIMIT:
                        b.dsem = self.new_sem("d")
                        b.dcount = 0
                    b.dcount += 16
                    o.signal = (b.dsem, b.dcount, 16)
                else:
                    c = self.cur[s]
                    if c[0] is None or c[1] + 1 > SEM_LIMIT:
                        c[0] = self.new_sem("e" + s)
                        c[1] = 0
                    c[1] += 1
                    o.signal = (c[0], c[1], 1)
            self.order[s].extend(order[s])

    def _emit_stream(self, stream, eng):
        ops = self.ops
        waited = self.waited[stream]
        lst = self.order[stream]
        for o in lst[self.cursor[stream]:]:
            need = {}
            for d in o.deps:
                p = ops[d]
                if p.signal is None:
                    continue
                sem, val, _ = p.signal
                k = id(sem)
                if waited.get(k, 0) >= val:
                    continue
                if k not in need or need[k][1] < val:
                    need[k] = (sem, val)
            for k, (sem, val) in need.items():
                eng.wait_ge(sem, val)
                waited[k] = val
            if o.fn is None:
                continue
            ins = o.fn(eng)
            if o.signal is not None:
                ins.then_inc(o.signal[0], o.signal[2])
        self.cursor[stream] = len(lst)

    def flush(self):
        order = self._schedule(self.flushed, len(self.ops))
        self.flushed = len(self.ops)
        self._assign(order)
        S = self
        with self.nc.Block() as block:
            @block.sync
            def _(e):
                S._emit_stream("sp", e)

            @block.tensor
            def _(e):
                S._emit_stream("pe", e)

            @block.scalar
            def _(e):
                S._emit_stream("act", e)

            @block.vector
            def _(e):
                S._emit_stream("dve", e)

            @block.gpsimd
            def _(e):
                S._emit_stream("pool", e)


class T:
    def __init__(self, t, name):
        self.t = t
        self.b = Buf(name)


class Ring:
    def __init__(self, tiles):
        self.tiles = tiles
        self.i = 0

    def next(self):
        t = self.tiles[self.i % len(self.tiles)]
        self.i += 1
        return t


def _fs(ap):
    try:
        return float(ap.free_size())
    except Exception:
        return 256.0


def f_mm(out, lhsT, rhs, start, stop):
    n = _fs(out)
    c = max(n, 64.0) / 2400.0 + 0.012
    if lhsT.dtype == F32:
        c *= 4.0
    return Fn(lambda e: e.matmul(out, lhsT=lhsT, rhs=rhs, start=start, stop=stop), c)


def f_tr(out, in_, ident):
    return Fn(lambda e: e.transpose(out=out, in_=in_, identity=ident), 0.09)


def f_act(out, in_, func, bias=None, scale=None, accum=None):
    kw = {}
    if bias is not None:
        kw["bias"] = bias
    if scale is not None:
        kw["scale"] = scale
    if accum is not None:
        kw["accum_out"] = accum
    return Fn(lambda e: e.activation(out=out, in_=in_, func=func, **kw), 0.2 + _fs(in_) / 1400.0)


def f_tt(out, in0, in1, op):
    return Fn(lambda e: e.tensor_tensor(out=out, in0=in0, in1=in1, op=op), 0.07 + _fs(out) / 960.0)


def f_ts(out, in0, s1, s2, op0, op1=None):
    c = 0.07 + _fs(out) / 960.0
    if op1 is None:
        return Fn(lambda e: e.tensor_scalar(out=out, in0=in0, scalar1=s1, scalar2=None, op0=op0), c)
    return Fn(lambda e: e.tensor_scalar(out=out, in0=in0, scalar1=s1, scalar2=s2, op0=op0, op1=op1), c)


def f_stt(out, in0, scalar, in1, op0, op1):
    return Fn(lambda e: e.scalar_tensor_tensor(out=out, in0=in0, scalar=scalar, in1=in1, op0=op0, op1=op1),
              0.07 + _fs(out) / 960.0)


def f_cp(out, in_):
    return Fn(lambda e: e.tensor_copy(out=out, in_=in_), 0.07 + _fs(out) / 960.0)


def f_rcp(out, in_):
    return Fn(lambda e: e.reciprocal(out=out, in_=in_), 0.07 + _fs(out) / 960.0)


def f_dma(out, in_):
    try:
        nb = float(out.nbytes())
    except Exception:
        nb = 65536.0
    return Fn(lambda e: e.dma_start(out=out, in_=in_), 2.0 + nb / 150e3)


def f_memset(ap, val):
    return Fn(lambda e: e.memset(ap, val), 0.07 + _fs(ap) / 960.0)


def build_program(stop=99, dbg=False, p2_blocks=99, p2_conv=True, p2_sub=99):
    nc = bass.Bass("TRN2", target_bir_lowering=False)

    def di(name, shape):
        return nc.dram_tensor(name, shape, F32, kind="ExternalInput").ap()

    x_seq = di("x_seq", [SEQ, D])
    ctx_seq = di("ctx_seq", [CTX, D])
    c_fm = di("c_fm", [128, 8])
    cctx_fm = di("cctx_fm", [128, 8])
    w_mod = di("w_mod", [D, 6 * D])
    b_mod = di("b_mod", [6 * D])
    norm1_g = di("norm1_g", [D])
    w_in = di("w_in", [D, DIN])
    cw_fm = di("cw_fm", [128, 4, 31])
    cvec_fm = di("cvec_fm", [128, 3, 4])
    wgF = di("wgF_aug", [33, 256])
    wgB = di("wgB_aug", [33, 256])
    gla_g = di("gla_norm_fm", [128, 4])
    w_out = di("w_out", [D, D])
    norm2_g = di("norm2_g", [D])
    w_up = di("w_up", [D, 2 * HID])
    fdw_fm = di("fdw_fm", [128, NH, 4])
    w_down = di("w_down", [HID, D])
    final_g = di("final_g", [D])
    ident_in = di("ident", [128, 128])
    masks_in = di("masks", [128, 4, 128])
    ind_in = di("ind", [128, 2])
    y_out = nc.dram_tensor("y_out", [SEQ // 2, D], F32, kind="ExternalOutput").ap()
    x1s = nc.dram_tensor("x1_scratch", [EXT * 128, D], F32, kind="ExternalOutput" if dbg else "Internal").ap()
    if dbg:
        d_mod = nc.dram_tensor("d_mod", [4, 128, D], F32, kind="ExternalOutput").ap()
        d_st = nc.dram_tensor("d_st", [4, 128, 256], F32, kind="ExternalOutput").ap()
        d_gcol = nc.dram_tensor("d_gcol", [128, 2 * (15 + 2 * HALO_T) * 64], BF16, kind="ExternalOutput").ap()
        d_sbp = nc.dram_tensor("d_sbp", [128, 2 * EXT * 256], BF16, kind="ExternalOutput").ap()

    modsave = nc.dram_tensor("mod_scratch", [4, D], F32, kind="Internal").ap()
    sbp_dram = nc.dram_tensor("sbp_scratch", [2 * EXT, 128, 256], BF16, kind="Internal").ap()
    w_up_bf = nc.dram_tensor("w_up_bf16", [D, 2 * HID], BF16, kind="Internal").ap()
    kt_d = nc.dram_tensor("kt_scratch", [EXT, 128, 256], F32, kind="Internal").ap()
    v_d = nc.dram_tensor("v_scratch", [EXT, 128, 512], BF16, kind="Internal").ap()
    z_d = nc.dram_tensor("z_scratch", [EXT, 32, 128], BF16, kind="Internal").ap()
    w_dn_bf = nc.dram_tensor("w_dn_bf16", [HID, D], BF16, kind="Internal").ap()

    with ExitStack() as top:
        S = Sched(nc, top)

        def sb(stack, name, shape, dt):
            return T(stack.enter_context(nc.sbuf_tensor("sb_" + name, shape, dt)), name)

        def ps(stack, name, shape, dt):
            t_ = T(stack.enter_context(nc.psum_tensor("ps_" + name, shape, dt)), name)
            t_.b.psum = True
            return t_

        ident_f = sb(top, "ident_f", [128, 128], F32)
        ident_b = sb(top, "ident_b", [128, 128], BF16)
        masks = sb(top, "masks", [128, 4, 128], F32)
        ind = sb(top, "ind", [128, 2], F32)
        masks_b = sb(top, "masks_b", [128, 4, 128], BF16)
        ind_b = sb(top, "ind_b", [128, 2], BF16)
        ones_f = sb(top, "ones_f", [128, 128], F32)
        onesM = sb(top, "onesM", [128, 128], BF16)
        junk = sb(top, "junk", [128, 1024], BF16)
        M_LI, M_UI, M_LS, M_US = 0, 1, 2, 3

        pT = ps(top, "pT", [128, 512], F32)
        pbank = [ps(top, f"p{i}", [128, 512], F32) for i in range(1, 8)]
        p1, p2, p3, p4, p5, p6, p7 = pbank
        p3r = p3.b

        S.op("sp", f_dma(ident_f.t[:], ident_in[:, :]), writes=[ident_f.b], dma=ident_f.b)
        S.op("sp", f_dma(masks.t[:], masks_in[:, :, :]), writes=[masks.b], dma=masks.b)
        S.op("sp", f_dma(ind.t[:], ind_in[:, :]), writes=[ind.b], dma=ind.b)
        S.op("dve", f_cp(ident_b.t[:], ident_f.t[:]), reads=[ident_f.b], writes=[ident_b.b])
        S.op("dve", f_cp(masks_b.t[:], masks.t[:]), reads=[masks.b], writes=[masks_b.b])
        S.op("dve", f_cp(ind_b.t[:], ind.t[:]), reads=[ind.b], writes=[ind_b.b])
        S.op("dve", f_memset(ones_f.t[:], 1.0), writes=[ones_f.b])
        S.op("dve", f_memset(onesM.t[:], 1.0 / 512.0), writes=[onesM.b])

        def front(fr, rows_ap, n, sbt_, bbt_, dst=None):
            xt = fr["x"].next()
            st = fr["st"].next()
            hm1 = fr["hm1"].next()
            hm = fr["hm"].next()
            hT = fr["hT"].next() if dst is None else dst[0]
            hT_ap = hT.t[:, :, 0:n] if dst is None else dst[1]
            fr["last_x"] = S.op("sp", f_dma(xt.t[0:n, :], rows_ap), writes=[xt.b], dma=xt.b)
            S.op("act", f_act(hm1.t[0:n, :], xt.t[0:n, :], AF.Square, accum=st.t[0:n, 0:1]),
                 reads=[xt.b], writes=[st.b, hm1.b])
            S.op("act", f_act(st.t[0:n, 1:2], st.t[0:n, 0:1], AF.Ln, bias=EPS, scale=1.0 / D),
                 reads=[st.b], writes=[st.b])
            S.op("act", f_act(st.t[0:n, 2:3], st.t[0:n, 1:2], AF.Exp, scale=-0.5),
                 reads=[st.b], writes=[st.b])
            S.op("dve", f_stt(hm1.t[0:n, :], xt.t[0:n, :], st.t[0:n, 2:3], sbt_.t[0:n, :], ALU.mult, ALU.mult),
                 reads=[xt.b, st.b, sbt_.b], writes=[hm1.b])
            S.op(CFG["add_eng"], f_tt(hm.t[0:n, :], hm1.t[0:n, :], bbt_.t[0:n, :], ALU.add),
                 reads=[hm1.b, bbt_.b], writes=[hm.b])
            pv = pT.t[:, :].rearrange("p (a b) -> p a b", a=4)
            for half in range(2):
                for q in range(4):
                    kc = half * 4 + q
                    S.op("pe", f_mm(pT.t[:, q * 128:q * 128 + n], hm.t[0:n, kc * 128:(kc + 1) * 128],
                                    ident_b.t[0:n, 0:n], True, True),
                         reads=[hm.b, ident_b.b], writes=[pT.b])
                if CFG["hT_dve"] and half == 1:
                    S.op("dve", f_cp(hT_ap[:, half * 4:half * 4 + 4, :], pv[:, :, 0:n]), reads=[pT.b], writes=[hT.b])
                else:
                    S.op("act", f_act(hT_ap[:, half * 4:half * 4 + 4, :], pv[:, :, 0:n], AF.Copy), reads=[pT.b], writes=[hT.b])
            return hT

        def act_sigmoid(dst, src, dst_b, src_b):
            S.op("act", f_act(dst, src, AF.Exp, scale=-1.0), reads=[src_b], writes=[dst_b])
            S.op("act", f_act(dst, dst, AF.Ln, bias=1.0), reads=[dst_b], writes=[dst_b])
            S.op("act", f_act(dst, dst, AF.Exp, scale=-1.0), reads=[dst_b], writes=[dst_b])

        def make_front(stack, pre, nx=2, nh=2, nm=1):
            return {
                "x": Ring([sb(stack, f"{pre}x{i}", [128, D], F32) for i in range(nx)]),
                "st": Ring([sb(stack, f"{pre}st{i}", [128, 4], F32) for i in range(4)]),
                "hm1": Ring([sb(stack, f"{pre}hm1_{i}", [128, D], BF16) for i in range(nm)]),
                "hm": Ring([sb(stack, f"{pre}hm_{i}", [128, D], BF16) for i in range(nm)]),
                "hT": Ring([sb(stack, f"{pre}hT{i}", [128, 8, 128], BF16) for i in range(nh)]),
            }

        with ExitStack() as mix:
            w_in_sb = sb(mix, "w_in_sb", [128, 8, DIN], BF16)
            w_out_sb = sb(mix, "w_out_sb", [128, 8, D], BF16)
            s1b = sb(mix, "s1b", [128, D], F32)
            b1b = sb(mix, "b1b", [128, D], F32)
            cw = sb(mix, "cw", [128, 4, 31], F32)
            cvec = sb(mix, "cvec", [128, 3, 4], F32)
            wgF_sb = sb(mix, "wgF_sb", [33, 256], BF16)
            wgB_sb = sb(mix, "wgB_sb", [33, 256], BF16)
            gng_sb = sb(mix, "gng_sb", [128, 4], F32)
            SF = sb(mix, "SF", [128, 2, 128], F32)
            SBs = sb(mix, "SBs", [128, 2, 128], F32)
            zTa_r = Ring([sb(mix, f"zTa{i}", [33, 128], BF16) for i in range(2)])
            zcur = {"t": zTa_r.tiles[0]}

            w_in_v = w_in.rearrange("(kc p) n -> p kc n", p=128)
            wgrp = Buf("wgrp")
            crit_cols = [(C_K, C_K + 768), (C_Z, C_Z + 32), (C_CU + 256, C_CU + 512), (C_CG + 256, C_CG + 512)]
            rest_cols = [(C_CU, C_CU + 256), (C_CG, C_CG + 256), (C_Q, C_Q + 256), (C_OG, C_OG + 512)]
            w_all_ops = [S.op("pool", f_dma(w_in_sb.t[:, :, a:b_], w_in_v[:, :, a:b_]), dma=wgrp) for (a, b_) in crit_cols]
            w_out_v = w_out.rearrange("(kc p) n -> p kc n", p=128)
            w_out_regs = [Buf(f"wout{kc}") for kc in range(8)]
            S.op("sp", f_dma(cw.t[:], cw_fm[:, :, :]), writes=[cw.b], dma=cw.b)
            S.op("sp", f_dma(cvec.t[:], cvec_fm[:, :, :]), writes=[cvec.b], dma=cvec.b)
            S.op("pool", f_dma(wgF_sb.t[:], wgF[:, :]), writes=[wgF_sb.b], dma=wgF_sb.b)
            S.op("pool", f_dma(wgB_sb.t[:], wgB[:, :]), writes=[wgB_sb.b], dma=wgB_sb.b)
            S.op("sp", f_dma(gng_sb.t[:], gla_g[:, :]), writes=[gng_sb.b], dma=gng_sb.b)
            for zt in zTa_r.tiles:
                S.op("dve", f_memset(zt.t[32:33, :], 1.0), writes=[zt.b])
            SF.regs = [Buf(f"SF{i}") for i in range(4)]
            SBs.regs = [Buf(f"SB{i}") for i in range(4)]
            S.op("dve", f_memset(SF.t[:], 0.0), writes=SF.regs)
            S.op("dve", f_memset(SBs.t[:], 0.0), writes=SBs.regs)

            def gate(X, g):
                wg = wgF_sb if X == "F" else wgB_sb
                gn = g["gn" + X]
                zTa = zcur["t"]
                S.op("pe", f_mm(p6.t[:, 0:256], zTa.t[0:33, :], wg.t[0:33, :], True, True),
                     reads=[zTa.b, wg.b], writes=[p6.b])
                S.op("act", f_act(g["eg"].t[:], p6.t[:, 0:256], AF.Exp, scale=-1.0), reads=[p6.b], writes=[g["eg"].b])
                S.op("act", f_act(gn.t[:], g["eg"].t[:], AF.Ln, bias=1.0), reads=[g["eg"].b], writes=[gn.b])
                return gn

            def state_stage(X, g, gn, ktok, vbf, save_chunks):
                S_ = SF if X == "F" else SBs
                ms = M_US if X == "F" else M_LS
                Et, ktail, dec = g["Et"], g["ktail"], g["dec"]
                S.op("pe", f_mm(p6.t[:, 256:512], masks_b.t[:, ms, :], gn.t[:], True, True),
                     reads=[masks_b.b, gn.b], writes=[p6.b])
                for j in range(2):
                    S.op("pe", f_mm(p7.t[:, 2 * j:2 * j + 2], gn.t[:, j * 128:(j + 1) * 128], ind_b.t[:], True, True),
                         reads=[gn.b, ind_b.b], writes=[p7.b])
                S.op("act", f_act(Et.t[:], p6.t[:, 256:512], AF.Exp, scale=-1.0 / 16), reads=[p6.b], writes=[Et.b])
                S.op("act", f_act(dec.t[:], p7.t[:, 0:4], AF.Exp, scale=-1.0 / 16), reads=[p7.b], writes=[dec.b])
                S.op("dve", f_tt(ktail.t[:], ktok.t[:], Et.t[:], ALU.mult), reads=[ktok.b, Et.b], writes=[ktail.b])
                order = (0, 1) if X == "F" else (1, 0)
                kvp = {0: (p2 if CFG["kv_p2"] else p3), 1: p5}
                for lc in order:
                    if save_chunks is not None:
                        spt = sps.next()
                        S.op("act", f_act(spt.t[:], S_.t[:], AF.Copy), reads=S_.regs, writes=[spt.b])
                        S.op("sp", f_dma(sbp_dram[save_chunks[lc]], spt.t[:].rearrange("p a b -> p (a b)")),
                             reads=[spt.b], dma=spt.b)
                    pk = kvp[lc]
                    for j in range(2):
                        S.op("pe", f_mm(pk.t[:, j * 256:(j + 1) * 256],
                                        ktail.t[lc * 64:(lc + 1) * 64, j * 128:(j + 1) * 128],
                                        vbf.t[lc * 64:(lc + 1) * 64, j * 256:(j + 1) * 256], True, True),
                             reads=[ktail.b, vbf.b], writes=[pk.b])
                    for j in range(2):
                        for e_ in range(2):
                            sl = slice(e_ * 64, (e_ + 1) * 64)
                            S.op("dve", f_stt(S_.t[sl, j, :], S_.t[sl, j, :], dec.t[sl, 2 * j + lc:2 * j + lc + 1],
                                              pk.t[sl, j * 256 + e_ * 128:j * 256 + (e_ + 1) * 128],
                                              ALU.mult, ALU.add),
                                 reads=[S_.regs[2 * j + e_], dec.b, pk.b], writes=[S_.regs[2 * j + e_]])

            def make_gate_tiles(stack, pre):
                d_ = {
                    "gnF": sb(stack, pre + "gnF", [128, 256], BF16),
                    "gnB": sb(stack, pre + "gnB", [128, 256], BF16),
                    "Et": sb(stack, pre + "Et", [128, 256], F32),
                    "ktail": sb(stack, pre + "ktail", [128, 256], BF16),
                    "dec": sb(stack, pre + "dec", [128, 4], F32),
                    "ktok": sb(stack, pre + "ktok", [128, 256], F32),
                    "vbf": sb(stack, pre + "vbf", [128, 512], BF16),
                }
                d_["eg"] = d_["Et"]
                return d_

            def kvz_proj(hT, g, save_t=None):
                for kc in range(8):
                    S.op("pe", f_mm(p3.t[:, 0:256], hT.t[:, kc, :], w_in_sb.t[:, kc, C_K:C_K + 256], kc == 0, kc == 7),
                         reads=[hT.b], extra=w_all_ops, writes=[p3.b])
                for kc in range(8):
                    S.op("pe", f_mm(p4.t[:, :], hT.t[:, kc, :], w_in_sb.t[:, kc, C_V:C_V + 512], kc == 0, kc == 7),
                         reads=[hT.b], extra=w_all_ops, writes=[p4.b])
                for kc in range(8):
                    S.op("pe", f_mm(p3.t[0:32, 256:384], w_in_sb.t[:, kc, C_Z:C_Z + 32], hT.t[:, kc, :], kc == 0, kc == 7),
                         reads=[hT.b], extra=w_all_ops, writes=[p3r])
                S.op("act", f_act(g["ktok"].t[:], p3.t[:, 0:256], AF.Copy), reads=[p3.b], writes=[g["ktok"].b])
                S.op("dve", f_cp(g["vbf"].t[:], p4.t[:, :]), reads=[p4.b], writes=[g["vbf"].b])
                zTa = zTa_r.next()
                zcur["t"] = zTa
                S.op("act", f_act(zTa.t[0:32, :], p3.t[0:32, 256:384], AF.Copy), reads=[p3r], writes=[zTa.b])
                if save_t is not None:
                    S.op("sp", f_dma(kt_d[save_t], g["ktok"].t[:]), reads=[g["ktok"].b], dma=g["ktok"].b)
                    S.op("sp", f_dma(v_d[save_t], g["vbf"].t[:]), reads=[g["vbf"].b], dma=g["vbf"].b)
                    S.op("sp", f_dma(z_d[save_t], zTa.t[0:32, :]), reads=[zTa.b], dma=zTa.b)

            def adaln(ph, pre, chunks, with_ctx, dst, banks):
                nv = 16 if with_ctx else 8
                c_sb = sb(ph, pre + "c_sb", [128, nv], F32)
                e_c = sb(ph, pre + "e_c", [128, nv], F32)
                screp = sb(ph, pre + "screp", [128, nv, 128], F32)
                wm = Ring([sb(ph, f"{pre}wm{i}", [128, 8, 256], F32) for i in range(2)])
                bm = Ring([sb(ph, f"{pre}bm{i}", [128, 256], F32) for i in range(2)])
                ngb = Ring([sb(ph, f"{pre}ngb{i}", [128, 256], F32) for i in range(2)])
                tmpm = Ring([sb(ph, f"{pre}tmpm{i}", [128, 256], F32) for i in range(2)])
                S.op("sp", f_dma(c_sb.t[:, 0:8], c_fm[:, :]), writes=[c_sb.b], dma=Buf(pre + "c0"))
                if with_ctx:
                    S.op("sp", f_dma(c_sb.t[:, 8:16], cctx_fm[:, :]), writes=[c_sb.b], dma=Buf(pre + "c1"))
                S.op("act", f_act(e_c.t[:], c_sb.t[:], AF.Exp, scale=-1.0), reads=[c_sb.b], writes=[e_c.b])
                S.op("dve", f_ts(e_c.t[:], e_c.t[:], 1.0, None, ALU.add), reads=[e_c.b], writes=[e_c.b])
                S.op("dve", f_rcp(e_c.t[:], e_c.t[:]), reads=[e_c.b], writes=[e_c.b])
                S.op("dve", f_tt(c_sb.t[:], c_sb.t[:], e_c.t[:], ALU.mult), reads=[c_sb.b, e_c.b], writes=[c_sb.b])
                for i in range(nv):
                    S.op("dve", f_ts(screp.t[:, i, :], ones_f.t[:], c_sb.t[:, i:i + 1], None, ALU.mult),
                         reads=[ones_f.b, c_sb.b], writes=[screp.b])
                w_mod_v = w_mod.rearrange("(kc p) n -> p kc n", p=128)
                for ci, nci in enumerate(chunks):
                    n0 = nci * 256
                    wmt = wm.next()
                    bmt = bm.next()
                    q_ = ("sp", "act")[ci % 2] if with_ctx else "sp"
                    S.op(q_, f_dma(wmt.t[:, :, :], w_mod_v[:, :, n0:n0 + 256]), writes=[wmt.b], dma=wmt.b)
                    S.op("sp", f_dma(bmt.t[:], b_mod[n0:n0 + 256].partition_broadcast(128)), writes=[bmt.b], dma=bmt.b)
                    which = nci // 4
                    c0_ = (nci % 4) * 256
                    half = slice(c0_, c0_ + 256)
                    if which in (1, 4):
                        ngt = ngb.next()
                        gsrc = norm1_g if which == 1 else norm2_g
                        S.op("sp", f_dma(ngt.t[:], gsrc[c0_:c0_ + 256].partition_broadcast(128)), writes=[ngt.b], dma=ngt.b)
                    variants = [(0, banks[0])] + ([(8, banks[1])] if (which < 2 and with_ctx) else [])
                    for off, pp_ in variants:
                        pp = T(pp_.t[:, 0:256], pp_.b.name)
                        pp.b = pp_.b
                        for kc in range(8):
                            S.op("pe", f_mm(pp.t, screp.t[:, off + kc, :], wmt.t[:, kc, :], kc == 0, kc == 7),
                                 reads=[screp.b, wmt.b], writes=[pp.b])
                        key = (which, off)
                        if which in (0, 2):
                            d_ = dst[key]
                            S.op("dve", f_tt(d_.t[:, half], pp.t, bmt.t[:], ALU.add), reads=[pp.b, bmt.b], writes=[d_.b])
                        elif which == 1:
                            d_ = dst[key]
                            tm = tmpm.next()
                            S.op("dve", f_tt(tm.t[:], pp.t, bmt.t[:], ALU.add), reads=[pp.b, bmt.b], writes=[tm.b])
                            S.op("dve", f_stt(d_.t[:, half], tm.t[:], 1.0, ngt.t[:], ALU.add, ALU.mult),
                                 reads=[tm.b, ngt.b], writes=[d_.b])
                        else:
                            row = {3: 1, 4: 0, 5: 2}[which]
                            tm = tmpm.next()
                            S.op("dve", f_tt(tm.t[:], pp.t, bmt.t[:], ALU.add), reads=[pp.b, bmt.b], writes=[tm.b])
                            if which == 4:
                                S.op("dve", f_stt(tm.t[:], tm.t[:], 1.0, ngt.t[:], ALU.add, ALU.mult),
                                     reads=[tm.b, ngt.b], writes=[tm.b])
                            S.op("sp", f_dma(modsave[row:row + 1, c0_:c0_ + 256], tm.t[0:1, :]), reads=[tm.b], dma=tm.b)

            with ExitStack() as ph:
                s1c = sb(ph, "s1c", [128, D], F32)
                b1c = sb(ph, "b1c", [128, D], F32)
                fr0 = make_front(ph, "f0", nx=2, nh=2)
                g0 = {"F": make_gate_tiles(ph, "g0F"), "B": make_gate_tiles(ph, "g0B")}
                adaln(ph, "a0", list(range(8)), True,
                      {(0, 0): b1b, (0, 8): b1c, (1, 0): s1b, (1, 8): s1c}, (p1, p2))

                for X, tiles in (("B", (1, 0)), ("F", (0, 1))):
                    for t in tiles:
                        hT = front(fr0, ctx_seq[t * 128:(t + 1) * 128, :], 128, s1c, b1c)
                        kvz_proj(hT, g0[X])
                        gn = gate(X, g0[X])
                        state_stage(X, g0[X], gn, g0[X]["ktok"], g0[X]["vbf"], None)
                if dbg:
                    S.op("sp", f_dma(d_mod[0], s1b.t[:]), reads=[s1b.b], dma=Buf("dd0"))
                    S.op("sp", f_dma(d_mod[1], b1b.t[:]), reads=[b1b.b], dma=Buf("dd1"))
                    S.op("sp", f_dma(d_mod[2], s1c.t[:]), reads=[s1c.b], dma=Buf("dd2"))
                    S.op("sp", f_dma(d_mod[3], b1c.t[:]), reads=[b1c.b], dma=Buf("dd3"))
                    S.op("sp", f_dma(d_st[0], SF.t[:].rearrange("p a b -> p (a b)")), reads=SF.regs, dma=Buf("dd4"))
                    S.op("sp", f_dma(d_st[1], SBs.t[:].rearrange("p a b -> p (a b)")), reads=SBs.regs, dma=Buf("dd5"))
                S.barrier()
                S.flush()
            if stop <= 0:
                return nc

            sps = Ring([sb(mix, f"sps{i}", [128, 2, 128], BF16) for i in range(2)])
            gcol = sb(mix, "gcol", [128, 2, (15 + 2 * HALO_T) * 64], BF16)
            diagT = sb(mix, "diagT", [128, 4 * 31, 128], BF16)
            S.op("pool", f_memset(gcol.t[:, :, 0:15 * 64], 0.0), writes=[gcol.b])
            for c in range(4):
                for k in range(31):
                    S.op("dve", f_ts(diagT.t[:, c * 31 + k, :], ident_b.t[:], cw.t[:, c, k:k + 1], None, ALU.mult),
                         reads=[ident_b.b, cw.b], writes=[diagT.b])

            with ExitStack() as ph:
                pcg = Buf("precast")
                bg = []

                def _bg_dma(out_ap, in_ap, grp, lst=None):
                    def go(dep):
                        i = S.op("pool", f_dma(out_ap, in_ap), dma=grp, extra=[dep])
                        if lst is not None:
                            lst.append(i)
                    return go
                g1b = sb(ph, "g1b", [128, D], F32)
                adaln(ph, "a1", list(range(8, 24)), False, {(2, 0): g1b}, (p7,))
                wgrp1 = Buf("wgrp1")
                w_rest_ops = []
                for kc in (0, 4):
                    bg.append(_bg_dma(w_out_sb.t[:, kc:kc + 4, :], w_out_v[:, kc:kc + 4, :], wgrp1, w_rest_ops))
                for (a, b_) in rest_cols:
                    bg.append(_bg_dma(w_in_sb.t[:, :, a:b_], w_in_v[:, :, a:b_], wgrp1, w_rest_ops))
                for kc in range(8):
                    bg.append(_bg_dma(w_up_bf[kc * 128:(kc + 1) * 128, :], w_up[kc * 128:(kc + 1) * 128, :], pcg))
                for j0 in range(0, NH, 2):
                    bg.append(_bg_dma(w_dn_bf[j0 * 128:(j0 + 2) * 128, :], w_down[j0 * 128:(j0 + 2) * 128, :], pcg))

                def fold_w_out():
                    for kc in range(8):
                        if kc < 4:
                            S.op("dve", f_tt(w_out_sb.t[:, kc, :], w_out_sb.t[:, kc, :], g1b.t[:], ALU.mult),
                                 reads=[g1b.b], writes=[w_out_regs[kc]], extra=w_rest_ops)
                        else:
                            S.op("dve", f_stt(w_out_sb.t[:, kc, :], w_out_sb.t[:, kc, :], gng_sb.t[:, kc - 4:kc - 3], g1b.t[:],
                                              ALU.mult, ALU.mult),
                                 reads=[g1b.b, gng_sb.b], writes=[w_out_regs[kc]], extra=w_rest_ops)
                fold_done = []
                fr1 = make_front(ph, "f1", nx=3, nh=2, nm=CFG["p1_hm"])
                g1r = Ring([make_gate_tiles(ph, f"g1{i}") for i in range(CFG["p1_gate"])])
                ecg = sb(ph, "ecg1", [128, 2, 128], F32)
                for t in range(NT - 1, -1, -1):
                    g1 = g1r.next()
                    hT = front(fr1, x_seq[t * 128:(t + 1) * 128, :], 128, s1b, b1b)
                    if bg and t % 2 == 0:
                        bg.pop(0)(fr1["last_x"])
                        if len(w_rest_ops) == 6 and not fold_done:
                            fold_w_out()
                            fold_done.append(1)
                    kvz_proj(hT, g1, save_t=t if t < EXT else None)
                    if t < HALO_T:
                        for m in range(4):
                            col = (C_CU + 256 + m * 128) if m < 2 else (C_CG + 256 + (m - 2) * 128)
                            for kc in range(8):
                                S.op("pe", f_mm(p1.t[:, m * 128:(m + 1) * 128], w_in_sb.t[:, kc, col:col + 128],
                                                hT.t[:, kc, :], kc == 0, kc == 7),
                                     reads=[hT.b], extra=w_all_ops, writes=[p1.b])
                        p1v = p1.t[:, :].rearrange("p (a b) -> p a b", a=4)
                        act_sigmoid(ecg.t[:], p1v[:, 2:4, :], ecg.b, p1.b)
                        pos = (15 + 2 * t) * 64
                        S.op("dve", f_tt(gcol.t[:, :, pos:pos + 128], p1v[:, 0:2, :], ecg.t[:], ALU.mult),
                             reads=[p1.b, ecg.b], writes=[gcol.b])
                    gn = gate("B", g1)
                    state_stage("B", g1, gn, g1["ktok"], g1["vbf"], (2 * t, 2 * t + 1) if t < EXT else None)
                while bg:
                    bg.pop(0)(fr1["last_x"])
                if not fold_done:
                    fold_w_out()
                if dbg:
                    S.op("sp", f_dma(d_st[2], SBs.t[:].rearrange("p a b -> p (a b)")), reads=SBs.regs, dma=Buf("dd6"))
                    S.op("sp", f_dma(d_gcol[:, :], gcol.t[:].rearrange("p a b -> p (a b)")), reads=[gcol.b], dma=Buf("dd7"))
                S.barrier()
                S.flush()
            if stop <= 1:
                return nc

            with ExitStack() as ph:
                fr2 = make_front(ph, "f2", nx=2, nh=2, nm=CFG["p2_hm"])
                g2r = Ring([make_gate_tiles(ph, f"g2{i}") for i in range(CFG["p2_gate"])])
                qk_r = Ring([sb(ph, f"qk_s{i}", [128, 4, 128], F32) for i in range(CFG["p2_qk"])])
                eog_r = Ring([sb(ph, f"eog{i}", [128, 512], F32) for i in range(CFG["p2_og"])])
                sog_r = Ring([sb(ph, f"sog{i}", [128, 512], F32) for i in range(CFG["p2_og"])])
                EEp = sb(ph, "EEp", [128, 2, 128], F32)
                EEn = sb(ph, "EEn", [128, 2, 128], F32)
                EE = {"Fp": EEp, "Bp": EEp, "Fn": EEn, "Bn": EEn}
                QK = {X + s: sb(ph, "QK" + X + s, [128, 2, 128], BF16) for X in "FB" for s in "qk"}
                scm = sb(ph, "scm", [128, 8, 128], BF16)
                SFbf = Ring([sb(ph, f"SFbf{i}", [128, 2, 128], BF16) for i in range(2)])
                sbl = Ring([sb(ph, f"sbl{i}", [128, 2, 256], BF16) for i in range(2)])
                ost = sb(ph, "ost", [128, 12], F32)
                o_g = sb(ph, "o_g", [128, 512], BF16)
                ecg2 = sb(ph, "ecg2", [128, 2, 128], F32)
                growp = Ring([sb(ph, f"growp{i}", [128, 2, 4, 94], BF16) for i in range(2)])
                mixT_r = Ring([sb(ph, f"mixT{i}", [128, 8, 256], BF16) for i in range(CFG["p2_mixT"])])
                y32 = sb(ph, "y32", [128, 4, 256], F32)
                yb = sb(ph, "yb", [128, 4, 256], BF16)
                ysq = sb(ph, "ysq", [128, 4, 256], BF16)
                lnm = sb(ph, "lnm", [128, 256], F32)
                lnv = sb(ph, "lnv", [128, 256], F32)
                lnr = sb(ph, "lnr", [128, 256], F32)
                ynt = Ring([sb(ph, f"ynt{i}", [128, 256], F32) for i in range(1)])
                eyn = Ring([sb(ph, f"eyn{i}", [128, 256], F32) for i in range(1)])
                xr = Ring([sb(ph, f"xr{i}", [128, D], F32) for i in range(CFG["p2_xr"])])
                for gt in growp.tiles:
                    S.op("pool", f_memset(gt.t[:], 0.0), writes=[gt.b])

                def mixer_tile(t, lt, grow, mixT):
                    g2 = g2r.next()
                    qk_s = qk_r.next()
                    eog = eog_r.next()
                    sog = sog_r.next()
                    sbt2 = sbl.next()
                    S.op("sp", f_dma(sbt2.t[:], sbp_dram[2 * t:2 * t + 2].rearrange("c p f -> p c f")),
                         writes=[sbt2.b], dma=sbt2.b)
                    hT = front(fr2, x_seq[t * 128:(t + 1) * 128, :], 128, s1b, b1b)
                    zTa = zTa_r.next()
                    zcur["t"] = zTa
                    S.op("sp", f_dma(g2["ktok"].t[:], kt_d[t]), writes=[g2["ktok"].b], dma=g2["ktok"].b)
                    S.op("sp", f_dma(g2["vbf"].t[:], v_d[t]), writes=[g2["vbf"].b], dma=g2["vbf"].b)
                    S.op("sp", f_dma(zTa.t[0:32, :], z_d[t]), writes=[zTa.b], dma=zTa.b)
                    if p2_sub <= 1:
                        return
                    for m in range(4):
                        col = (C_CU + m * 128) if m < 2 else (C_CG + (m - 2) * 128)
                        for kc in range(8):
                            S.op("pe", f_mm(p1.t[:, m * 128:(m + 1) * 128], w_in_sb.t[:, kc, col:col + 128],
                                            hT.t[:, kc, :], kc == 0, kc == 7),
                                 reads=[hT.b], extra=w_all_ops, writes=[p1.b])
                    for m in range(4):
                        col = C_Q + m * 128
                        for kc in range(8):
                            S.op("pe", f_mm(p2.t[:, m * 128:(m + 1) * 128], w_in_sb.t[:, kc, col:col + 128],
                                            hT.t[:, kc, :], kc == 0, kc == 7),
                                 reads=[hT.b], extra=w_all_ops, writes=[p2.b])
                    for kc in range(8):
                        S.op("pe", f_mm(p3.t[:, :], hT.t[:, kc, :], w_in_sb.t[:, kc, C_OG:C_OG + 512], kc == 0, kc == 7),
                             reads=[hT.b], extra=w_all_ops, writes=[p3.b])
                    if p2_sub <= 2:
                        return
                    p1v = p1.t[:, :].rearrange("p (a b) -> p a b", a=4)
                    act_sigmoid(ecg2.t[:], p1v[:, 2:4, :], ecg2.b, p1.b)
                    for c in range(2):
                        S.op("dve", f_tt(grow.t[:, c, 2 * lt:2 * lt + 2, 15:79],
                                         p1v[:, c, :].rearrange("p (r w) -> p r w", r=2),
                                         ecg2.t[:, c, :].rearrange("p (r w) -> p r w", r=2), ALU.mult),
                             reads=[p1.b, ecg2.b], writes=[grow.b])
                    if p2_sub <= 3:
                        return
                    S.op("act", f_act(qk_s.t[:], p2.t[:, :].rearrange("p (a b) -> p a b", a=4), AF.Copy),
                         reads=[p2.b], writes=[qk_s.b])
                    act_sigmoid(eog.t[:], p3.t[:, :], eog.b, p3.b)
                    S.op("dve", f_tt(sog.t[:], p3.t[:, :], eog.t[:], ALU.mult), reads=[p3.b, eog.b], writes=[sog.b])
                    if p2_sub <= 4:
                        return
                    gnF = gate("F", g2)
                    gnB = gate("B", g2)
                    Et, ktail, dec = g2["Et"], g2["ktail"], g2["dec"]
                    S.op("pe", f_mm(p6.t[:, 256:512], masks_b.t[:, M_US, :], gnF.t[:], True, True),
                         reads=[masks_b.b, gnF.b], writes=[p6.b])
                    for j in range(2):
                        S.op("pe", f_mm(p6.t[:, 2 * j:2 * j + 2], gnF.t[:, j * 128:(j + 1) * 128], ind_b.t[:], True, True),
                             reads=[gnF.b, ind_b.b], writes=[p6.b])
                    S.op("act", f_act(Et.t[:], p6.t[:, 256:512], AF.Exp, scale=-1.0 / 16), reads=[p6.b], writes=[Et.b])
                    S.op("act", f_act(dec.t[:], p6.t[:, 0:4], AF.Exp, scale=-1.0 / 16), reads=[p6.b], writes=[dec.b])
                    S.op("dve", f_tt(ktail.t[:], g2["ktok"].t[:], Et.t[:], ALU.mult),
                         reads=[g2["ktok"].b, Et.b], writes=[ktail.b])
                    for xi, (X, gn, mk) in enumerate((("F", gnF, M_LI), ("B", gnB, M_UI))):
                        for j in range(2):
                            S.op("pe", f_mm(p7.t[:, xi * 256 + j * 128:xi * 256 + (j + 1) * 128],
                                            gn.t[:, j * 128:(j + 1) * 128], masks_b.t[:, mk, :], True, True),
                                 reads=[gn.b, masks_b.b], writes=[p7.b])
                    for xi, X in enumerate("FB"):
                        src = p7.t[:, xi * 256:(xi + 1) * 256].rearrange("p (a b) -> p a b", a=2)
                        S.op("act", f_act(EE[X + "p"].t[:], src, AF.Exp, scale=-1.0 / 16, bias=float(np.log(0.125))),
                             reads=[p7.b], writes=[EE[X + "p"].b])
                        S.op("act", f_act(EE[X + "n"].t[:], src, AF.Exp, scale=1.0 / 16),
                             reads=[p7.b], writes=[EE[X + "n"].b])
                        S.op("dve", f_tt(QK[X + "q"].t[:], qk_s.t[:, 0:2, :], EE[X + "p"].t[:], ALU.mult),
                             reads=[qk_s.b, EE[X + "p"].b], writes=[QK[X + "q"].b])
                        S.op("dve", f_tt(QK[X + "k"].t[:], qk_s.t[:, 2:4, :], EE[X + "n"].t[:], ALU.mult),
                             reads=[qk_s.b, EE[X + "n"].b], writes=[QK[X + "k"].b])
                    if p2_sub <= 5:
                        return
                    for e_ in range(2):
                        pp = p6 if e_ == 0 else p7
                        sl = slice(e_ * 64, (e_ + 1) * 64)
                        for xi, X in enumerate("FB"):
                            for j in range(2):
                                slot = xi * 2 + j
                                S.op("pe", f_mm(pp.t[:, slot * 128:(slot + 1) * 128], QK[X + "k"].t[sl, j, :],
                                                QK[X + "q"].t[sl, j, :], True, True),
                                     reads=[QK[X + "k"].b, QK[X + "q"].b], writes=[pp.b])
                    for e_ in range(2):
                        pp = p6 if e_ == 0 else p7
                        for xi, X in enumerate("FB"):
                            mk = M_LI if X == "F" else M_UI
                            for j in range(2):
                                slot = xi * 2 + j
                                h = 2 * j + e_
                                S.op("dve", f_tt(scm.t[:, xi * 4 + h, :], pp.t[:, slot * 128:(slot + 1) * 128],
                                                 masks.t[:, mk, :], ALU.mult),
                                     reads=[pp.b, masks.b], writes=[scm.b])
                    if p2_sub <= 6:
                        return
                    S_prev_tiles = {}
                    vbf = g2["vbf"]
                    for lc in range(2):
                        sf = SFbf.next()
                        S.op("act", f_act(sf.t[:], SF.t[:], AF.Copy), reads=SF.regs, writes=[sf.b])
                        S_prev_tiles[lc] = sf
                        for j in range(2):
                            S.op("pe", f_mm(p4.t[:, j * 256:(j + 1) * 256],
                                            ktail.t[lc * 64:(lc + 1) * 64, j * 128:(j + 1) * 128],
                                            vbf.t[lc * 64:(lc + 1) * 64, j * 256:(j + 1) * 256], True, True),
                                 reads=[ktail.b, vbf.b], writes=[p4.b])
                        for j in range(2):
                            for e_ in range(2):
                                sl = slice(e_ * 64, (e_ + 1) * 64)
                                S.op("dve", f_stt(SF.t[sl, j, :], SF.t[sl, j, :], dec.t[sl, 2 * j + lc:2 * j + lc + 1],
                                                  p4.t[sl, j * 256 + e_ * 128:j * 256 + (e_ + 1) * 128],
                                                  ALU.mult, ALU.add),
                                     reads=[SF.regs[2 * j + e_], dec.b, p4.b], writes=[SF.regs[2 * j + e_]])
                    if p2_sub <= 7:
                        return
                    for h in range(4):
                        j, e_ = h // 2, h % 2
                        sl = slice(e_ * 64, (e_ + 1) * 64)
                        oc = slice(h * 128, (h + 1) * 128)
                        S.op("pe", f_mm(p5.t[:, oc], scm.t[:, h, :], vbf.t[:, oc], True, False),
                             reads=[scm.b, vbf.b], writes=[p5.b])
                        S.op("pe", f_mm(p5.t[:, oc], scm.t[:, 4 + h, :], vbf.t[:, oc], False, False),
                             reads=[scm.b, vbf.b], writes=[p5.b])
                        for lc in range(2):
                            tl = slice(lc * 64, (lc + 1) * 64)
                            S.op("pe", f_mm(p5.t[tl, oc], QK["Fq"].t[sl, j, tl], S_prev_tiles[lc].t[sl, j, :], False, False),
                                 reads=[QK["Fq"].b, S_prev_tiles[lc].b], writes=[p5.b])
                            S.op("pe", f_mm(p5.t[tl, oc], QK["Bq"].t[sl, j, tl], sbt2.t[sl, lc, j * 128:(j + 1) * 128], False, True),
                                 reads=[QK["Bq"].b, sbt2.b], writes=[p5.b])
                    if p2_sub <= 8:
                        return
                    for h in range(4):
                        S.op("act", f_act(junk.t[:, h * 128:(h + 1) * 128], p5.t[:, h * 128:(h + 1) * 128], AF.Square,
                                          accum=ost.t[:, h:h + 1]),
                             reads=[p5.b], writes=[ost.b, junk.b])
                    S.op("act", f_act(ost.t[:, 4:8], ost.t[:, 0:4], AF.Ln, bias=EPS, scale=1.0 / 128), reads=[ost.b], writes=[ost.b])
                    S.op("act", f_act(ost.t[:, 8:12], ost.t[:, 4:8], AF.Exp, scale=-0.5), reads=[ost.b], writes=[ost.b])
                    for h in range(4):
                        oc = slice(h * 128, (h + 1) * 128)
                        S.op("dve", f_stt(o_g.t[:, oc], p5.t[:, oc], ost.t[:, 8 + h:9 + h], sog.t[:, oc], ALU.mult, ALU.mult),
                             reads=[p5.b, ost.b, sog.b], writes=[o_g.b])
                    if p2_sub <= 9:
                        return
                    for h in range(4):
                        S.op("pe", f_mm(p7.t[:, h * 128:(h + 1) * 128], o_g.t[:, h * 128:(h + 1) * 128], ident_b.t[:, :], True, True),
                             reads=[o_g.b, ident_b.b], writes=[p7.b])
                    S.op("act", f_act(mixT.t[:, 4:8, lt * 128:(lt + 1) * 128],
                                      p7.t[:, 0:512].rearrange("p (a b) -> p a b", a=4), AF.Copy),
                         reads=[p7.b], writes=[mixT.b])

                def conv_block(t0, ntl, grow, mixT):
                    n = ntl * 128
                    nr = 2 * ntl
                    r0 = 2 * t0
                    cps = {0: p1, 1: p1, 2: p2, 3: p2}
                    for c in range(4):
                        pp = cps[c]
                        oc = slice((c % 2) * 256, (c % 2) * 256 + n)
                        for k in range(31):
                            if c < 2:
                                rhs = grow.t[:, c, 0:nr, k:k + 64]
                            else:
                                rhs = gcol.t[:, c - 2, (r0 + k) * 64:(r0 + k) * 64 + n]
                            S.op("pe", f_mm(pp.t[:, oc], diagT.t[:, c * 31 + k, :], rhs, k == 0, k == 30),
                                 reads=[diagT.b, grow.b if c < 2 else gcol.b], writes=[pp.b])
                    for c in range(4):
                        pp = cps[c]
                        oc = slice((c % 2) * 256, (c % 2) * 256 + n)
                        S.op("act", f_act(y32.t[:, c, 0:n], pp.t[:, oc], AF.Identity, bias=cvec.t[:, 0, c:c + 1]),
                             reads=[pp.b, cvec.b], writes=[y32.b])
                        S.op("act", f_act(ysq.t[:, c, 0:n], pp.t[:, oc], AF.Square, bias=cvec.t[:, 0, c:c + 1]),
                             reads=[pp.b, cvec.b], writes=[ysq.b])
                        S.op("dve", f_cp(yb.t[:, c, 0:n], y32.t[:, c, 0:n]), reads=[y32.b], writes=[yb.b])
                    for c in range(4):
                        S.op("pe", f_mm(p6.t[:, 0:n], onesM.t[:], yb.t[:, c, 0:n], c == 0, c == 3),
                             reads=[onesM.b, yb.b], writes=[p6.b])
                    for c in range(4):
                        S.op("pe", f_mm(p6.t[:, 256:256 + n], onesM.t[:], ysq.t[:, c, 0:n], c == 0, c == 3),
                             reads=[onesM.b, ysq.b], writes=[p6.b])
                    S.op("act", f_act(lnm.t[:, 0:n], p6.t[:, 0:n], AF.Copy), reads=[p6.b], writes=[lnm.b])
                    S.op("dve", f_tt(lnv.t[:, 0:n], lnm.t[:, 0:n], lnm.t[:, 0:n], ALU.mult), reads=[lnm.b], writes=[lnv.b])
                    S.op("dve", f_tt(lnv.t[:, 0:n], p6.t[:, 256:256 + n], lnv.t[:, 0:n], ALU.subtract),
                         reads=[p6.b, lnv.b], writes=[lnv.b])
                    S.op("act", f_act(lnr.t[:, 0:n], lnv.t[:, 0:n], AF.Ln, bias=EPS), reads=[lnv.b], writes=[lnr.b])
                    S.op("act", f_act(lnr.t[:, 0:n], lnr.t[:, 0:n], AF.Exp, scale=-0.5), reads=[lnr.b], writes=[lnr.b])
                    for c in range(4):
                        yn = ynt.next()
                        ey = eyn.next()
                        S.op("dve", f_tt(yn.t[:, 0:n], y32.t[:, c, 0:n], lnm.t[:, 0:n], ALU.subtract),
                             reads=[y32.b, lnm.b], writes=[yn.b])
                        S.op("dve", f_tt(yn.t[:, 0:n], yn.t[:, 0:n], lnr.t[:, 0:n], ALU.mult), reads=[yn.b, lnr.b], writes=[yn.b])
                        S.op("dve", f_ts(yn.t[:, 0:n], yn.t[:, 0:n], cvec.t[:, 1, c:c + 1], cvec.t[:, 2, c:c + 1], ALU.mult, ALU.add),
                             reads=[yn.b, cvec.b], writes=[yn.b])
                        act_sigmoid(ey.t[:, 0:n], yn.t[:, 0:n], ey.b, yn.b)
                        S.op("dve", f_tt(mixT.t[:, c, 0:n], yn.t[:, 0:n], ey.t[:, 0:n], ALU.mult),
                             reads=[yn.b, ey.b], writes=[mixT.b])
                    for lt in range(ntl):
                        t = t0 + lt
                        xrt = xr.next()
                        S.op("sp", f_dma(xrt.t[:], x_seq[t * 128:(t + 1) * 128, :]), writes=[xrt.b], dma=xrt.b)
                        for half, pp in ((0, p3), (1, p5)):
                            for kc in range(8):
                                S.op("pe", f_mm(pp.t[:, :], mixT.t[:, kc, lt * 128:(lt + 1) * 128],
                                                w_out_sb.t[:, kc, half * 512:(half + 1) * 512], kc == 0, kc == 7),
                                     reads=[mixT.b], writes=[pp.b])
                        for half, pp in ((0, p3), (1, p5)):
                            hs = slice(half * 512, (half + 1) * 512)
                            S.op("dve", f_tt(xrt.t[:, hs], pp.t[:, :], xrt.t[:, hs], ALU.add),
                                 reads=[pp.b, xrt.b], writes=[xrt.b])
                        S.op("sp", f_dma(x1s[t * 128:(t + 1) * 128, :], xrt.t[:]), reads=[xrt.b], dma=xrt.b)

                t = 0
                while t < EXT:
                    ntl = min(2, EXT - t)
                    grow = growp.next()
                    mixT = mixT_r.next()
                    for lt in range(ntl):
                        mixer_tile(t + lt, lt, grow, mixT)
                    if p2_conv:
                        conv_block(t, ntl, grow, mixT)
                    t += ntl
                    if t >= 2 * p2_blocks:
                        break
                if dbg:
                    S.op("sp", f_dma(d_st[3], SF.t[:].rearrange("p a b -> p (a b)")), reads=SF.regs, dma=Buf("dd9"))
                S.barrier()
                S.flush()
        if stop <= 2:
            return nc

        with ExitStack() as ffn:
            w_up_sb = sb(ffn, "w_up_sb", [128, 8, 2 * HID], BF16)
            w_dn_sb = sb(ffn, "w_dn_sb", [128, NH, D], BF16)
            g2b = sb(ffn, "g2b", [128, D], F32)
            w_up_v = w_up_bf.rearrange("(kc p) n -> p kc n", p=128)
            NG = 8
            gsz = [3, 3, 3, 3, 3, 3, 2, 2]
            gst = [sum(gsz[:g]) for g in range(NG)]
            w_up_grp = {}
            for g in range(NG):
                gb = Buf(f"wupg{g}")
                ops_ = []
                for hh in range(2):
                    c0_ = hh * HID + gst[g] * 128
                    c1_ = c0_ + gsz[g] * 128
                    ops_.append(S.op(("sp", "act")[g % 2], f_dma(w_up_sb.t[:, :, c0_:c1_], w_up_v[:, :, c0_:c1_]), dma=gb))
                for jj in range(gst[g], gst[g] + gsz[g]):
                    w_up_grp[jj] = ops_
            w_dn_v = w_dn_bf.rearrange("(j p) n -> p j n", p=128)
            S.op("sp", f_dma(g2b.t[:], modsave[2, :].partition_broadcast(128)), writes=[g2b.b], dma=g2b.b)
            w_dn_regs = {}
            for j0 in range(0, NH, 2):
                rb = Buf(f"wdn{j0}")
                S.op(("sp", "act")[(j0 // 2) % 2], f_dma(w_dn_sb.t[:, j0:j0 + 2, :], w_dn_v[:, j0:j0 + 2, :]), writes=[rb], dma=rb)
                for j in (j0, j0 + 1):
                    S.op("dve", f_tt(w_dn_sb.t[:, j, :], w_dn_sb.t[:, j, :], g2b.t[:], ALU.mult),
                         reads=[rb, g2b.b], writes=[rb])
                    w_dn_regs[j] = rb

            s2b = sb(ffn, "s2b", [128, D], F32)
            b2b = sb(ffn, "b2b", [128, D], F32)
            fgb = sb(ffn, "fgb", [128, D], F32)
            fdw = sb(ffn, "fdw", [128, NH, 4], F32)
            fr3 = make_front(ffn, "f3", nx=CFG["f3_nx"], nh=1)
            h2Tr = Ring([sb(ffn, f"h2T{i}", [128, 8, FB], BF16) for i in range(CFG["h2T"])])
            abuf = Ring([sb(ffn, f"abuf{i}", [128, FB + 2], F32) for i in range(CFG["ffn_ring"])])
            vbuf = Ring([sb(ffn, f"vbuf{i}", [128, FB + 1], F32) for i in range(CFG["ffn_ring"])])
            tcv = Ring([sb(ffn, f"tcv{i}", [128, FB], F32) for i in range(CFG["ffn_ring"])])
            esg = Ring([sb(ffn, f"esg{i}", [128, FB], F32) for i in range(CFG["ffn_ring"])])
            car = sb(ffn, "car", [128, NH, 3], F32)
            hidT = sb(ffn, "hidT", [128, NH, FB], BF16)
            hidT.regs = [Buf(f"hid{j}") for j in range(NH)]
            xrf = Ring([sb(ffn, f"xrf{i}", [128, D], F32) for i in range(2)])
            fst = Ring([sb(ffn, f"fst{i}", [128, 4], F32) for i in range(2)])

            S.op("sp", f_dma(s2b.t[:], modsave[0, :].partition_broadcast(128)), writes=[s2b.b], dma=s2b.b)
            S.op("sp", f_dma(b2b.t[:], modsave[1, :].partition_broadcast(128)), writes=[b2b.b], dma=b2b.b)
            S.op("sp", f_dma(fgb.t[:], final_g.partition_broadcast(128)), writes=[fgb.b], dma=fgb.b)
            S.op("sp", f_dma(fdw.t[:], fdw_fm[:, :, :]), writes=[fdw.b], dma=fdw.b)
            S.op("dve", f_memset(car.t[:], 0.0), writes=[car.b])
            for xt_ in xrf.tiles:
                S.op("pool", f_memset(xt_.t[:], 0.0), writes=[xt_.b])

            blocks = [(b * FB, FB) for b in range(SEQ // 2 // FB)] + [(SEQ // 2, 1)]
            for (t0, n) in blocks:
                h2T = h2Tr.next()
                for m in range((n + 127) // 128):
                    r0 = t0 + m * 128
                    nn = min(128, n - m * 128)
                    front(fr3, x1s[r0:r0 + nn, :], nn, s2b, b2b, dst=(h2T, h2T.t[:, :, m * 128:m * 128 + nn]))
                for j in range(NH):
                    pp = pbank[j % 4]
                    ab = abuf.next()
                    vb = vbuf.next()
                    for half in range(2):
                        col = half * HID + j * 128
                        for kc in range(8):
                            S.op("pe", f_mm(pp.t[:, half * 256:half * 256 + n], w_up_sb.t[:, kc, col:col + 128],
                                            h2T.t[:, kc, 0:n], kc == 0, kc == 7),
                                 reads=[h2T.b], extra=w_up_grp[j], writes=[pp.b])
                    S.op("pool", f_cp(ab.t[:, 0:2], car.t[:, j, 0:2]), reads=[car.b], writes=[ab.b])
                    S.op("pool", f_cp(vb.t[:, 0:1], car.t[:, j, 2:3]), reads=[car.b], writes=[vb.b])
                    S.op("act", f_act(ab.t[:, 2:2 + n], pp.t[:, 0:n], AF.Copy), reads=[pp.b], writes=[ab.b])
                    if CFG["ffn_bal"]:
                        S.op("dve", f_cp(vb.t[:, 1:1 + n], pp.t[:, 256:256 + n]), reads=[pp.b], writes=[vb.b])
                    else:
                        S.op("act", f_act(vb.t[:, 1:1 + n], pp.t[:, 256:256 + n], AF.Copy), reads=[pp.b], writes=[vb.b])
                    S.op("pool", f_cp(car.t[:, j, 0:2], ab.t[:, n:n + 2]), reads=[ab.b], writes=[car.b])
                    S.op("pool", f_cp(car.t[:, j, 2:3], vb.t[:, n:n + 1]), reads=[vb.b], writes=[car.b])
                    tc_ = tcv.next()
                    es = esg.next()
                    S.op("dve", f_ts(tc_.t[:, 0:n], ab.t[:, 0:n], fdw.t[:, j, 0:1], fdw.t[:, j, 3:4], ALU.mult, ALU.add),
                         reads=[ab.b, fdw.b], writes=[tc_.b])
                    S.op("dve", f_stt(tc_.t[:, 0:n], ab.t[:, 1:n + 1], fdw.t[:, j, 1:2], tc_.t[:, 0:n], ALU.mult, ALU.add),
                         reads=[ab.b, fdw.b, tc_.b], writes=[tc_.b])
                    S.op("dve", f_stt(tc_.t[:, 0:n], ab.t[:, 2:n + 2], fdw.t[:, j, 2:3], tc_.t[:, 0:n], ALU.mult, ALU.add),
                         reads=[ab.b, fdw.b, tc_.b], writes=[tc_.b])
                    act_sigmoid(es.t[:, 0:n], tc_.t[:, 0:n], es.b, tc_.b)
                    S.op("pool" if CFG["ffn_bal"] == 1 else "dve", f_tt(tc_.t[:, 0:n], tc_.t[:, 0:n], vb.t[:, 0:n], ALU.mult),
                         reads=[tc_.b, vb.b], writes=[tc_.b])
                    S.op("dve", f_tt(hidT.t[:, j, 0:n], tc_.t[:, 0:n], es.t[:, 0:n], ALU.mult),
                         reads=[tc_.b, es.b], writes=[hidT.b])
                for m in range((n + 127) // 128):
                    tk0 = t0 - 1 + m * 128
                    lo = max(0, -tk0)
                    hi = min(128, SEQ // 2 - tk0, n - m * 128)
                    if hi <= lo:
                        continue
                    xt_ = xrf.next()
                    st = fst.next()
                    S.op("sp", f_dma(xt_.t[lo:hi, :], x1s[tk0 + lo:tk0 + hi, :]), writes=[xt_.b], dma=xt_.b)
                    for half, pp in ((0, p5), (1, p6)):
                        for j in range(NH):
                            S.op("pe", f_mm(pp.t[0:hi, :], hidT.t[:, j, m * 128:m * 128 + hi],
                                            w_dn_sb.t[:, j, half * 512:(half + 1) * 512], j == 0, j == NH - 1),
                                 reads=[hidT.b, w_dn_regs[j]], writes=[pp.b])
                    for half, pp in ((0, p5), (1, p6)):
                        hs = slice(half * 512, (half + 1) * 512)
                        S.op("dve", f_tt(xt_.t[0:hi, hs], pp.t[0:hi, :], xt_.t[0:hi, hs], ALU.add), reads=[pp.b, xt_.b], writes=[xt_.b])
                    S.op("act", f_act(junk.t[0:hi, :], xt_.t[0:hi, :], AF.Square, accum=st.t[0:hi, 0:1]), reads=[xt_.b], writes=[st.b, junk.b])
                    S.op("act", f_act(st.t[0:hi, 1:2], st.t[0:hi, 0:1], AF.Ln, bias=EPS, scale=1.0 / D), reads=[st.b], writes=[st.b])
                    S.op("act", f_act(st.t[0:hi, 2:3], st.t[0:hi, 1:2], AF.Exp, scale=-0.5), reads=[st.b], writes=[st.b])
                    S.op("dve", f_stt(xt_.t[0:hi, :], xt_.t[0:hi, :], st.t[0:hi, 2:3], fgb.t[0:hi, :], ALU.mult, ALU.mult),
                         reads=[xt_.b, st.b, fgb.b], writes=[xt_.b])
                    S.op("sp", f_dma(y_out[tk0 + lo:tk0 + hi, :], xt_.t[lo:hi, :]), reads=[xt_.b], dma=xt_.b)
            S.barrier()
            S.flush()
    return nc


def _consts():
    idx = np.arange(128)
    same = (idx[:, None] // 64) == (idx[None, :] // 64)
    a, b = idx[:, None], idx[None, :]
    masks = np.stack([same & (a <= b), same & (a >= b), same & (a < b), same & (a > b)], axis=1).astype(np.float32)
    ind = ((idx[:, None] // 64) == np.arange(2)[None, :]).astype(np.float32)
    return np.eye(128, dtype=np.float32), np.ascontiguousarray(masks), ind


def _fm(v, nchunk):
    return np.ascontiguousarray(np.asarray(v, np.float32).reshape(nchunk, 128).T)


def make_in_maps(x, c, ctx, c_ctx, w_mod, b_mod, norm1_g, w_in, conv_dw, conv_b, conv_ln_g, conv_ln_b,
                 w_gf, b_gf, w_gb, b_gb, gla_norm_g, w_out, norm2_g, w_up, ffn_dw, ffn_dw_b, w_down, final_g):
    f = lambda a: np.asarray(a, dtype=np.float32)
    x, c, ctx, c_ctx = f(x), f(c), f(ctx), f(c_ctx)
    ident, masks, ind = _consts()
    w_in0 = f(w_in)[0]
    w_in_rev = np.ascontiguousarray(np.concatenate([w_in0[:, :C_Z], w_in0[:, C_Z + 16:C_Z + 32], w_in0[:, C_Z:C_Z + 16]], axis=1))
    cdw = f(conv_dw)[0]
    fdw_ = f(ffn_dw)[0]
    zeros16 = np.zeros((16, 256), np.float32)
    gate = {"f": (f(w_gf)[0], f(b_gf)[0]), "b": (f(w_gb)[0], f(b_gb)[0])}
    cvec = np.stack([_fm(f(conv_b)[0], 4), _fm(f(conv_ln_g)[0], 4), _fm(f(conv_ln_b)[0], 4)], axis=1)
    common = dict(
        cctx_fm=_fm(c_ctx, 8), w_mod=f(w_mod)[0], b_mod=f(b_mod)[0], norm1_g=f(norm1_g)[0],
        cvec_fm=np.ascontiguousarray(cvec), gla_norm_fm=_fm(f(gla_norm_g)[0], 4), w_out=f(w_out)[0], norm2_g=f(norm2_g)[0],
        w_up=f(w_up)[0], w_down=f(w_down)[0], final_g=f(final_g), ident=ident, masks=masks, ind=ind,
    )
    in_maps = []
    for core in range(8):
        b, rev = core // 2, core % 2
        xs = x[b][::-1] if rev else x[b]
        cs = ctx[b][::-1] if rev else ctx[b]
        cd = cdw[::-1] if rev else cdw
        fd = fdw_[::-1] if rev else fdw_
        pF, pB = ("b", "f") if rev else ("f", "b")
        wgF_aug = np.concatenate([gate[pF][0], zeros16, gate[pF][1][None, :]], axis=0)
        wgB_aug = np.concatenate([zeros16, gate[pB][0], gate[pB][1][None, :]], axis=0)
        cw_fm = np.ascontiguousarray(cd.T.reshape(4, 128, 31).transpose(1, 0, 2))
        fdw_fm = np.concatenate([fd.T.reshape(NH, 128, 3).transpose(1, 0, 2),
                                 f(ffn_dw_b)[0].reshape(NH, 128).T[:, :, None]], axis=2)
        m = dict(common)
        m.update(
            x_seq=np.ascontiguousarray(xs), ctx_seq=np.ascontiguousarray(cs), c_fm=_fm(c[b], 8),
            w_in=w_in_rev if rev else w_in0, cw_fm=cw_fm, wgF_aug=np.ascontiguousarray(wgF_aug),
            wgB_aug=np.ascontiguousarray(wgB_aug), fdw_fm=np.ascontiguousarray(fdw_fm),
        )
        in_maps.append(m)
    return in_maps


def kernel(**inputs):
    in_maps = make_in_maps(**inputs)
    nc = build_program()
    res = run_bass_kernel_spmd(nc, in_maps, core_ids=list(range(8)))
    out = np.empty((4, SEQ, D), np.float32)
    for core in range(8):
        b, rev = core // 2, core % 2
        y = np.asarray(res.results[core]["y_out"], np.float32)
        if rev:
            out[b, SEQ // 2:] = y[::-1]
        else:
            out[b, :SEQ // 2] = y
    return out
```

```python
from contextlib import ExitStack

import numpy as np
import concourse.bass as bass
import concourse.mybir as mybir
from concourse.bass_utils import run_bass_kernel_spmd

F32 = mybir.dt.float32
BF16 = mybir.dt.bfloat16
AF = mybir.ActivationFunctionType
ALU = mybir.AluOpType

D = 1024
SEQ = 8192
CTX = 256
NT = 64
EXT = 33
HALO_T = 41
DIN = 2592
C_CU, C_CG, C_Q, C_K, C_V, C_OG, C_Z = 0, 512, 1024, 1280, 1536, 2048, 2560
HID = 2816
NH = 22
EPS = 1e-6
FB = 256
SEM_LIMIT = 4000


class Buf:
    __slots__ = ("name", "writer", "readers", "dsem", "dcount", "psum")

    def __init__(self, name, psum=False):
        self.name = name
        self.writer = None
        self.readers = []
        self.dsem = None
        self.dcount = 0
        self.psum = psum


class Op:
    __slots__ = ("idx", "stream", "fn", "deps", "is_dma", "dbuf", "needed", "signal", "cost", "pos")

    def __init__(self, idx, stream, fn, deps, is_dma, dbuf, cost):
        self.idx = idx
        self.stream = stream
        self.fn = fn
        self.deps = deps
        self.is_dma = is_dma
        self.dbuf = dbuf
        self.needed = False
        self.signal = None
        self.cost = cost
        self.pos = -1


class Fn:
    __slots__ = ("f", "cost")

    def __init__(self, f, cost):
        self.f = f
        self.cost = cost

    def __call__(self, e):
        return self.f(e)


CFG = {"f3_nx": 1, "h2T": 1, "p2_hm": 1, "p2_gate": 1, "p2_qk": 1, "p2_og": 1, "p2_xr": 2, "p2_mixT": 1, "p1_hm": 1, "add_eng": "dve", "kv_p2": 1, "p1_gate": 2, "hT_dve": 0, "ffn_ring": 3, "ffn_bal": 2}
PRIO_CP = True
PRIO_SLACK = 0.0
SCHED_WINDOW = 600
REORDER = True
SEM_LAT = 0.25
DEFAULT_COST = {"pe": 0.12, "act": 0.4, "dve": 0.3, "pool": 0.5, "sp": 3.0}


class Sched:
    STREAMS = ("pe", "act", "dve", "pool", "sp")

    def __init__(self, nc, stack):
        self.nc = nc
        self.stack = stack
        self.ops = []
        self.order = {s: [] for s in self.STREAMS}
        self.cursor = {s: 0 for s in self.STREAMS}
        self.floor = 0
        self.flushed = 0
        self.nsem = 0
        self.cur = {s: [None, 0] for s in self.STREAMS}
        self.waited = {s: {} for s in self.STREAMS}
        self.last_compute = {s: None for s in self.STREAMS}
        self.phase_dmas = []
        self.sim_time = []

    def new_sem(self, name):
        self.nsem += 1
        return self.stack.enter_context(self.nc.semaphore(f"{name}_{self.nsem}"))

    def op(self, stream, fn, reads=(), writes=(), dma=None, extra=()):
        idx = len(self.ops)
        deps = set(extra)
        for b in reads:
            if b.writer is not None:
                deps.add(b.writer)
            if b.psum:
                for r in b.readers:
                    if self.ops[r].stream != stream:
                        deps.add(r)
        for b in writes:
            if b.writer is not None:
                deps.add(b.writer)
            deps.update(b.readers)
        deps = sorted(d for d in deps if d >= self.floor and d != idx)
        if fn is None:
            cost = 0.0
        else:
            cost = getattr(fn, "cost", None)
            if cost is None:
                cost = DEFAULT_COST["sp" if dma is not None else stream]
            elif stream == "pool" and dma is None:
                cost = cost * 3.5
        o = Op(idx, stream, fn, deps, dma is not None, dma, cost)
        self.ops.append(o)
        for b in reads:
            b.readers.append(idx)
        for b in writes:
            b.writer = idx
            b.readers = []
        if dma is not None:
            self.phase_dmas.append(idx)
        elif fn is not None:
            self.last_compute[stream] = idx
        return idx

    def barrier(self):
        deps = list(range(self.floor, len(self.ops)))
        for s in self.STREAMS:
            self.op(s, None, extra=deps)
        self.floor = len(self.ops)
        self.phase_dmas = []
        self.last_compute = {s: None for s in self.STREAMS}

    def _schedule(self, lo, hi):
        ops = self.ops
        n = hi - lo
        if not REORDER:
            order = {s: [] for s in self.STREAMS}
            for o in ops[lo:hi]:
                order[o.stream].append(o)
            self.sim_time.append(0.0)
            return order
        indeg = [0] * n
        users = [[] for _ in range(n)]
        for o in ops[lo:hi]:
            c = 0
            for d in o.deps:
                if d >= lo:
                    users[d - lo].append(o.idx)
                    c += 1
            indeg[o.idx - lo] = c
        ready = {s: [] for s in self.STREAMS}
        rtime = [0.0] * n
        free_at = {s: 0.0 for s in self.STREAMS}
        blev = [0.0] * n
        if PRIO_CP:
            for k in range(n - 1, -1, -1):
                o = ops[lo + k]
                m = 0.0
                for u in users[k]:
                    v = blev[u - lo] + (0.0 if (ops[u].stream == o.stream == "pe") else SEM_LAT)
                    if v > m:
                        m = v
                blev[k] = m + o.cost
        for o in ops[lo:hi]:
            if indeg[o.idx - lo] == 0:
                ready[o.stream].append(o.idx)
        order = {s: [] for s in self.STREAMS}
        remaining = n
        tmax = 0.0
        while remaining:
            best = None
            for s in self.STREAMS:
                lst = ready[s]
                if not lst:
                    continue
                fa = free_at[s]
                mi = min(lst)
                if PRIO_CP:
                    cands = []
                    mst = None
                    for i in lst:
                        if i > mi + SCHED_WINDOW:
                            continue
                        st = rtime[i - lo]
                        if st < fa:
                            st = fa
                        cands.append((st, i))
                        if mst is None or st < mst:
                            mst = st
                    bi = None
                    for st, i in cands:
                        if st <= mst + PRIO_SLACK:
                            if bi is None or blev[i - lo] > blev[bi[1] - lo]:
                                bi = (st, i)
                    if best is None or (bi[0], bi[1]) < best[0]:
                        best = ((bi[0], bi[1]), s, bi[1])
                    continue
                for i in lst:
                    if i > mi + SCHED_WINDOW:
                        continue
                    st = rtime[i - lo]
                    if st < fa:
                        st = fa
                    if best is None or (st, i) < best[0]:
                        best = ((st, i), s, i)
            (st, _), s, i = best
            o = ops[i]
            if o.is_dma:
                free_at[s] = st + 0.07
            else:
                free_at[s] = st + o.cost
            fin = st + o.cost
            if fin > tmax:
                tmax = fin
            ready[s].remove(i)
            order[s].append(o)
            remaining -= 1
            for u in users[i - lo]:
                k = u - lo
                lat = 0.0 if (ops[u].stream == s == "pe") else SEM_LAT
                if fin + lat > rtime[k]:
                    rtime[k] = fin + lat
                indeg[k] -= 1
                if indeg[k] == 0:
                    ready[ops[u].stream].append(u)
        self.sim_time.append(tmax)
        return order

    def _assign(self, order):
        ops = self.ops
        for s in self.STREAMS:
            base = len(self.order[s])
            for k, o in enumerate(order[s]):
                o.pos = base + k
        for s in self.STREAMS:
            for o in order[s]:
                latest = {}
                for d in o.deps:
                    p = ops[d]
                    if p.is_dma:
                        p.needed = True
                        continue
                    if p.fn is None:
                        continue
                    if p.stream == o.stream and p.stream in ("pe", "sp"):
                        continue
                    q = latest.get(p.stream)
                    if q is None or p.pos > q.pos:
                        latest[p.stream] = p
                for p in latest.values():
                    p.needed = True
        for s in self.STREAMS:
            for o in order[s]:
                if not o.needed:
                    continue
                if o.is_dma:
                    b = o.dbuf
                    if b.dsem is None or b.dcount + 16 > SEM_LIMIT:
                        b.dsem = self.new_sem("d")
                        b.dcount = 0
                    b.dcount += 16
                    o.signal = (b.dsem, b.dcount, 16)
                else:
                    c = self.cur[s]
                    if c[0] is None or c[1] + 1 > SEM_LIMIT:
                        c[0] = self.new_sem("e" + s)
                        c[1] = 0
                    c[1] += 1
                    o.signal = (c[0], c[1], 1)
            self.order[s].extend(order[s])

    def _emit_stream(self, stream, eng):
        ops = self.ops
        waited = self.waited[stream]
        lst = self.order[stream]
        for o in lst[self.cursor[stream]:]:
            need = {}
            for d in o.deps:
                p = ops[d]
                if p.signal is None:
                    continue
                sem, val, _ = p.signal
                k = id(sem)
                if waited.get(k, 0) >= val:
                    continue
                if k not in need or need[k][1] < val:
                    need[k] = (sem, val)
            for k, (sem, val) in need.items():
                eng.wait_ge(sem, val)
                waited[k] = val
            if o.fn is None:
                continue
            ins = o.fn(eng)
            if o.signal is not None:
                ins.then_inc(o.signal[0], o.signal[2])
        self.cursor[stream] = len(lst)

    def flush(self):
        order = self._schedule(self.flushed, len(self.ops))
        self.flushed = len(self.ops)
        self._assign(order)
        S = self
        with self.nc.Block() as block:
            @block.sync
            def _(e):
                S._emit_stream("sp", e)

            @block.tensor
            def _(e):
                S._emit_stream("pe", e)

            @block.scalar
            def _(e):
                S._emit_stream("act", e)

            @block.vector
            def _(e):
                S._emit_stream("dve", e)

            @block.gpsimd
            def _(e):
                S._emit_stream("pool", e)


class T:
    def __init__(self, t, name):
        self.t = t
        self.b = Buf(name)


class Ring:
    def __init__(self, tiles):
        self.tiles = tiles
        self.i = 0

    def next(self):
        t = self.tiles[self.i % len(self.tiles)]
        self.i += 1
        return t


def _fs(ap):
    try:
        return float(ap.free_size())
    except Exception:
        return 256.0


def f_mm(out, lhsT, rhs, start, stop):
    n = _fs(out)
    c = max(n, 64.0) / 2400.0 + 0.012
    if lhsT.dtype == F32:
        c *= 4.0
    return Fn(lambda e: e.matmul(out, lhsT=lhsT, rhs=rhs, start=start, stop=stop), c)


def f_tr(out, in_, ident):
    return Fn(lambda e: e.transpose(out=out, in_=in_, identity=ident), 0.09)


def f_act(out, in_, func, bias=None, scale=None, accum=None):
    kw = {}
    if bias is not None:
        kw["bias"] = bias
    if scale is not None:
        kw["scale"] = scale
    if accum is not None:
        kw["accum_out"] = accum
    return Fn(lambda e: e.activation(out=out, in_=in_, func=func, **kw), 0.2 + _fs(in_) / 1400.0)


def f_tt(out, in0, in1, op):
    return Fn(lambda e: e.tensor_tensor(out=out, in0=in0, in1=in1, op=op), 0.07 + _fs(out) / 960.0)


def f_ts(out, in0, s1, s2, op0, op1=None):
    c = 0.07 + _fs(out) / 960.0
    if op1 is None:
        return Fn(lambda e: e.tensor_scalar(out=out, in0=in0, scalar1=s1, scalar2=None, op0=op0), c)
    return Fn(lambda e: e.tensor_scalar(out=out, in0=in0, scalar1=s1, scalar2=s2, op0=op0, op1=op1), c)


def f_stt(out, in0, scalar, in1, op0, op1):
    return Fn(lambda e: e.scalar_tensor_tensor(out=out, in0=in0, scalar=scalar, in1=in1, op0=op0, op1=op1),
              0.07 + _fs(out) / 960.0)


def f_cp(out, in_):
    return Fn(lambda e: e.tensor_copy(out=out, in_=in_), 0.07 + _fs(out) / 960.0)


def f_rcp(out, in_):
    return Fn(lambda e: e.reciprocal(out=out, in_=in_), 0.07 + _fs(out) / 960.0)


def f_dma(out, in_):
    try:
        nb = float(out.nbytes())
    except Exception:
        nb = 65536.0
    return Fn(lambda e: e.dma_start(out=out, in_=in_), 2.0 + nb / 150e3)


def f_memset(ap, val):
    return Fn(lambda e: e.memset(ap, val), 0.07 + _fs(ap) / 960.0)


def build_program(stop=99, dbg=False, p2_blocks=99, p2_conv=True, p2_sub=99):
    nc = bass.Bass("TRN2", target_bir_lowering=False)

    def di(name, shape):
        return nc.dram_tensor(name, shape, F32, kind="ExternalInput").ap()

    x_seq = di("x_seq", [SEQ, D])
    ctx_seq = di("ctx_seq", [CTX, D])
    c_fm = di("c_fm", [128, 8])
    cctx_fm = di("cctx_fm", [128, 8])
    w_mod = di("w_mod", [D, 6 * D])
    b_mod = di("b_mod", [6 * D])
    norm1_g = di("norm1_g", [D])
    w_in = di("w_in", [D, DIN])
    cw_fm = di("cw_fm", [128, 4, 31])
    cvec_fm = di("cvec_fm", [128, 3, 4])
    wgF = di("wgF_aug", [33, 256])
    wgB = di("wgB_aug", [33, 256])
    gla_g = di("gla_norm_fm", [128, 4])
    w_out = di("w_out", [D, D])
    norm2_g = di("norm2_g", [D])
    w_up = di("w_up", [D, 2 * HID])
    fdw_fm = di("fdw_fm", [128, NH, 4])
    w_down = di("w_down", [HID, D])
    final_g = di("final_g", [D])
    ident_in = di("ident", [128, 128])
    masks_in = di("masks", [128, 4, 128])
    ind_in = di("ind", [128, 2])
    y_out = nc.dram_tensor("y_out", [SEQ // 2, D], F32, kind="ExternalOutput").ap()
    x1s = nc.dram_tensor("x1_scratch", [EXT * 128, D], F32, kind="ExternalOutput" if dbg else "Internal").ap()
    if dbg:
        d_mod = nc.dram_tensor("d_mod", [4, 128, D], F32, kind="ExternalOutput").ap()
        d_st = nc.dram_tensor("d_st", [4, 128, 256], F32, kind="ExternalOutput").ap()
        d_gcol = nc.dram_tensor("d_gcol", [128, 2 * (15 + 2 * HALO_T) * 64], BF16, kind="ExternalOutput").ap()
        d_sbp = nc.dram_tensor("d_sbp", [128, 2 * EXT * 256], BF16, kind="ExternalOutput").ap()

    modsave = nc.dram_tensor("mod_scratch", [4, D], F32, kind="Internal").ap()
    sbp_dram = nc.dram_tensor("sbp_scratch", [2 * EXT, 128, 256], BF16, kind="Internal").ap()
    w_up_bf = nc.dram_tensor("w_up_bf16", [D, 2 * HID], BF16, kind="Internal").ap()
    kt_d = nc.dram_tensor("kt_scratch", [EXT, 128, 256], F32, kind="Internal").ap()
    v_d = nc.dram_tensor("v_scratch", [EXT, 128, 512], BF16, kind="Internal").ap()
    z_d = nc.dram_tensor("z_scratch", [EXT, 32, 128], BF16, kind="Internal").ap()
    w_dn_bf = nc.dram_tensor("w_dn_bf16", [HID, D], BF16, kind="Internal").ap()

    with ExitStack() as top:
        S = Sched(nc, top)

        def sb(stack, name, shape, dt):
            return T(stack.enter_context(nc.sbuf_tensor("sb_" + name, shape, dt)), name)

        def ps(stack, name, shape, dt):
            t_ = T(stack.enter_context(nc.psum_tensor("ps_" + name, shape, dt)), name)
            t_.b.psum = True
            return t_

        ident_f = sb(top, "ident_f", [128, 128], F32)
        ident_b = sb(top, "ident_b", [128, 128], BF16)
        masks = sb(top, "masks", [128, 4, 128], F32)
        ind = sb(top, "ind", [128, 2], F32)
        masks_b = sb(top, "masks_b", [128, 4, 128], BF16)
        ind_b = sb(top, "ind_b", [128, 2], BF16)
        ones_f = sb(top, "ones_f", [128, 128], F32)
        onesM = sb(top, "onesM", [128, 128], BF16)
        junk = sb(top, "junk", [128, 1024], BF16)
        M_LI, M_UI, M_LS, M_US = 0, 1, 2, 3

        pT = ps(top, "pT", [128, 512], F32)
        pbank = [ps(top, f"p{i}", [128, 512], F32) for i in range(1, 8)]
        p1, p2, p3, p4, p5, p6, p7 = pbank
        p3r = p3.b

        S.op("sp", f_dma(ident_f.t[:], ident_in[:, :]), writes=[ident_f.b], dma=ident_f.b)
        S.op("sp", f_dma(masks.t[:], masks_in[:, :, :]), writes=[masks.b], dma=masks.b)
        S.op("sp", f_dma(ind.t[:], ind_in[:, :]), writes=[ind.b], dma=ind.b)
        S.op("dve", f_cp(ident_b.t[:], ident_f.t[:]), reads=[ident_f.b], writes=[ident_b.b])
        S.op("dve", f_cp(masks_b.t[:], masks.t[:]), reads=[masks.b], writes=[masks_b.b])
        S.op("dve", f_cp(ind_b.t[:], ind.t[:]), reads=[ind.b], writes=[ind_b.b])
        S.op("dve", f_memset(ones_f.t[:], 1.0), writes=[ones_f.b])
        S.op("dve", f_memset(onesM.t[:], 1.0 / 512.0), writes=[onesM.b])

        def front(fr, rows_ap, n, sbt_, bbt_, dst=None):
            xt = fr["x"].next()
            st = fr["st"].next()
            hm1 = fr["hm1"].next()
            hm = fr["hm"].next()
            hT = fr["hT"].next() if dst is None else dst[0]
            hT_ap = hT.t[:, :, 0:n] if dst is None else dst[1]
            fr["last_x"] = S.op("sp", f_dma(xt.t[0:n, :], rows_ap), writes=[xt.b], dma=xt.b)
            S.op("act", f_act(hm1.t[0:n, :], xt.t[0:n, :], AF.Square, accum=st.t[0:n, 0:1]),
                 reads=[xt.b], writes=[st.b, hm1.b])
            S.op("act", f_act(st.t[0:n, 1:2], st.t[0:n, 0:1], AF.Ln, bias=EPS, scale=1.0 / D),
                 reads=[st.b], writes=[st.b])
            S.op("act", f_act(st.t[0:n, 2:3], st.t[0:n, 1:2], AF.Exp, scale=-0.5),
                 reads=[st.b], writes=[st.b])
            S.op("dve", f_stt(hm1.t[0:n, :], xt.t[0:n, :], st.t[0:n, 2:3], sbt_.t[0:n, :], ALU.mult, ALU.mult),
                 reads=[xt.b, st.b, sbt_.b], writes=[hm1.b])
            S.op(CFG["add_eng"], f_tt(hm.t[0:n, :], hm1.t[0:n, :], bbt_.t[0:n, :], ALU.add),
                 reads=[hm1.b, bbt_.b], writes=[hm.b])
            pv = pT.t[:, :].rearrange("p (a b) -> p a b", a=4)
            for half in range(2):
                for q in range(4):
                    kc = half * 4 + q
                    S.op("pe", f_mm(pT.t[:, q * 128:q * 128 + n], hm.t[0:n, kc * 128:(kc + 1) * 128],
                                    ident_b.t[0:n, 0:n], True, True),
                         reads=[hm.b, ident_b.b], writes=[pT.b])
                if CFG["hT_dve"] and half == 1:
                    S.op("dve", f_cp(hT_ap[:, half * 4:half * 4 + 4, :], pv[:, :, 0:n]), reads=[pT.b], writes=[hT.b])
                else:
                    S.op("act", f_act(hT_ap[:, half * 4:half * 4 + 4, :], pv[:, :, 0:n], AF.Copy), reads=[pT.b], writes=[hT.b])
            return hT

        def act_sigmoid(dst, src, dst_b, src_b):
            S.op("act", f_act(dst, src, AF.Exp, scale=-1.0), reads=[src_b], writes=[dst_b])
            S.op("act", f_act(dst, dst, AF.Ln, bias=1.0), reads=[dst_b], writes=[dst_b])
            S.op("act", f_act(dst, dst, AF.Exp, scale=-1.0), reads=[dst_b], writes=[dst_b])

        def make_front(stack, pre, nx=2, nh=2, nm=1):
            return {
                "x": Ring([sb(stack, f"{pre}x{i}", [128, D], F32) for i in range(nx)]),
                "st": Ring([sb(stack, f"{pre}st{i}", [128, 4], F32) for i in range(4)]),
                "hm1": Ring([sb(stack, f"{pre}hm1_{i}", [128, D], BF16) for i in range(nm)]),
                "hm": Ring([sb(stack, f"{pre}hm_{i}", [128, D], BF16) for i in range(nm)]),
                "hT": Ring([sb(stack, f"{pre}hT{i}", [128, 8, 128], BF16) for i in range(nh)]),
            }

        with ExitStack() as mix:
            w_in_sb = sb(mix, "w_in_sb", [128, 8, DIN], BF16)
            w_out_sb = sb(mix, "w_out_sb", [128, 8, D], BF16)
            s1b = sb(mix, "s1b", [128, D], F32)
            b1b = sb(mix, "b1b", [128, D], F32)
            cw = sb(mix, "cw", [128, 4, 31], F32)
            cvec = sb(mix, "cvec", [128, 3, 4], F32)
            wgF_sb = sb(mix, "wgF_sb", [33, 256], BF16)
            wgB_sb = sb(mix, "wgB_sb", [33, 256], BF16)
            gng_sb = sb(mix, "gng_sb", [128, 4], F32)
            SF = sb(mix, "SF", [128, 2, 128], F32)
            SBs = sb(mix, "SBs", [128, 2, 128], F32)
            zTa_r = Ring([sb(mix, f"zTa{i}", [33, 128], BF16) for i in range(2)])
            zcur = {"t": zTa_r.tiles[0]}

            w_in_v = w_in.rearrange("(kc p) n -> p kc n", p=128)
            wgrp = Buf("wgrp")
            crit_cols = [(C_K, C_K + 768), (C_Z, C_Z + 32), (C_CU + 256, C_CU + 512), (C_CG + 256, C_CG + 512)]
            rest_cols = [(C_CU, C_CU + 256), (C_CG, C_CG + 256), (C_Q, C_Q + 256), (C_OG, C_OG + 512)]
            w_all_ops = [S.op("pool", f_dma(w_in_sb.t[:, :, a:b_], w_in_v[:, :, a:b_]), dma=wgrp) for (a, b_) in crit_cols]
            w_out_v = w_out.rearrange("(kc p) n -> p kc n", p=128)
            w_out_regs = [Buf(f"wout{kc}") for kc in range(8)]
            S.op("sp", f_dma(cw.t[:], cw_fm[:, :, :]), writes=[cw.b], dma=cw.b)
            S.op("sp", f_dma(cvec.t[:], cvec_fm[:, :, :]), writes=[cvec.b], dma=cvec.b)
            S.op("pool", f_dma(wgF_sb.t[:], wgF[:, :]), writes=[wgF_sb.b], dma=wgF_sb.b)
            S.op("pool", f_dma(wgB_sb.t[:], wgB[:, :]), writes=[wgB_sb.b], dma=wgB_sb.b)
            S.op("sp", f_dma(gng_sb.t[:], gla_g[:, :]), writes=[gng_sb.b], dma=gng_sb.b)
            for zt in zTa_r.tiles:
                S.op("dve", f_memset(zt.t[32:33, :], 1.0), writes=[zt.b])
            SF.regs = [Buf(f"SF{i}") for i in range(4)]
            SBs.regs = [Buf(f"SB{i}") for i in range(4)]
            S.op("dve", f_memset(SF.t[:], 0.0), writes=SF.regs)
            S.op("dve", f_memset(SBs.t[:], 0.0), writes=SBs.regs)

            def gate(X, g):
                wg = wgF_sb if X == "F" else wgB_sb
                gn = g["gn" + X]
                zTa = zcur["t"]
                S.op("pe", f_mm(p6.t[:, 0:256], zTa.t[0:33, :], wg.t[0:33, :], True, True),
                     reads=[zTa.b, wg.b], writes=[p6.b])
                S.op("act", f_act(g["eg"].t[:], p6.t[:, 0:256], AF.Exp, scale=-1.0), reads=[p6.b], writes=[g["eg"].b])
                S.op("act", f_act(gn.t[:], g["eg"].t[:], AF.Ln, bias=1.0), reads=[g["eg"].b], writes=[gn.b])
                return gn

            def state_stage(X, g, gn, ktok, vbf, save_chunks):
                S_ = SF if X == "F" else SBs
                ms = M_US if X == "F" else M_LS
                Et, ktail, dec = g["Et"], g["ktail"], g["dec"]
                S.op("pe", f_mm(p6.t[:, 256:512], masks_b.t[:, ms, :], gn.t[:], True, True),
                     reads=[masks_b.b, gn.b], writes=[p6.b])
                for j in range(2):
                    S.op("pe", f_mm(p7.t[:, 2 * j:2 * j + 2], gn.t[:, j * 128:(j + 1) * 128], ind_b.t[:], True, True),
                         reads=[gn.b, ind_b.b], writes=[p7.b])
                S.op("act", f_act(Et.t[:], p6.t[:, 256:512], AF.Exp, scale=-1.0 / 16), reads=[p6.b], writes=[Et.b])
                S.op("act", f_act(dec.t[:], p7.t[:, 0:4], AF.Exp, scale=-1.0 / 16), reads=[p7.b], writes=[dec.b])
                S.op("dve", f_tt(ktail.t[:], ktok.t[:], Et.t[:], ALU.mult), reads=[ktok.b, Et.b], writes=[ktail.b])
                order = (0, 1) if X == "F" else (1, 0)
                kvp = {0: (p2 if CFG["kv_p2"] else p3), 1: p5}
                for lc in order:
                    if save_chunks is not None:
                        spt = sps.next()
                        S.op("act", f_act(spt.t[:], S_.t[:], AF.Copy), reads=S_.regs, writes=[spt.b])
                        S.op("sp", f_dma(sbp_dram[save_chunks[lc]], spt.t[:].rearrange("p a b -> p (a b)")),
                             reads=[spt.b], dma=spt.b)
                    pk = kvp[lc]
                    for j in range(2):
                        S.op("pe", f_mm(pk.t[:, j * 256:(j + 1) * 256],
                                        ktail.t[lc * 64:(lc + 1) * 64, j * 128:(j + 1) * 128],
                                        vbf.t[lc * 64:(lc + 1) * 64, j * 256:(j + 1) * 256], True, True),
                             reads=[ktail.b, vbf.b], writes=[pk.b])
                    for j in range(2):
                        for e_ in range(2):
                            sl = slice(e_ * 64, (e_ + 1) * 64)
                            S.op("dve", f_stt(S_.t[sl, j, :], S_.t[sl, j, :], dec.t[sl, 2 * j + lc:2 * j + lc + 1],
                                              pk.t[sl, j * 256 + e_ * 128:j * 256 + (e_ + 1) * 128],
                                              ALU.mult, ALU.add),
                                 reads=[S_.regs[2 * j + e_], dec.b, pk.b], writes=[S_.regs[2 * j + e_]])

            def make_gate_tiles(stack, pre):
                d_ = {
                    "gnF": sb(stack, pre + "gnF", [128, 256], BF16),
                    "gnB": sb(stack, pre + "gnB", [128, 256], BF16),
                    "Et": sb(stack, pre + "Et", [128, 256], F32),
                    "ktail": sb(stack, pre + "ktail", [128, 256], BF16),
                    "dec": sb(stack, pre + "dec", [128, 4], F32),
                    "ktok": sb(stack, pre + "ktok", [128, 256], F32),
                    "vbf": sb(stack, pre + "vbf", [128, 512], BF16),
                }
                d_["eg"] = d_["Et"]
                return d_

            def kvz_proj(hT, g, save_t=None):
                for kc in range(8):
                    S.op("pe", f_mm(p3.t[:, 0:256], hT.t[:, kc, :], w_in_sb.t[:, kc, C_K:C_K + 256], kc == 0, kc == 7),
                         reads=[hT.b], extra=w_all_ops, writes=[p3.b])
                for kc in range(8):
                    S.op("pe", f_mm(p4.t[:, :], hT.t[:, kc, :], w_in_sb.t[:, kc, C_V:C_V + 512], kc == 0, kc == 7),
                         reads=[hT.b], extra=w_all_ops, writes=[p4.b])
                for kc in range(8):
                    S.op("pe", f_mm(p3.t[0:32, 256:384], w_in_sb.t[:, kc, C_Z:C_Z + 32], hT.t[:, kc, :], kc == 0, kc == 7),
                         reads=[hT.b], extra=w_all_ops, writes=[p3r])
                S.op("act", f_act(g["ktok"].t[:], p3.t[:, 0:256], AF.Copy), reads=[p3.b], writes=[g["ktok"].b])
                S.op("dve", f_cp(g["vbf"].t[:], p4.t[:, :]), reads=[p4.b], writes=[g["vbf"].b])
                zTa = zTa_r.next()
                zcur["t"] = zTa
                S.op("act", f_act(zTa.t[0:32, :], p3.t[0:32, 256:384], AF.Copy), reads=[p3r], writes=[zTa.b])
                if save_t is not None:
                    S.op("sp", f_dma(kt_d[save_t], g["ktok"].t[:]), reads=[g["ktok"].b], dma=g["ktok"].b)
                    S.op("sp", f_dma(v_d[save_t], g["vbf"].t[:]), reads=[g["vbf"].b], dma=g["vbf"].b)
                    S.op("sp", f_dma(z_d[save_t], zTa.t[0:32, :]), reads=[zTa.b], dma=zTa.b)

            def adaln(ph, pre, chunks, with_ctx, dst, banks):
                nv = 16 if with_ctx else 8
                c_sb = sb(ph, pre + "c_sb", [128, nv], F32)
                e_c = sb(ph, pre + "e_c", [128, nv], F32)
                screp = sb(ph, pre + "screp", [128, nv, 128], F32)
                wm = Ring([sb(ph, f"{pre}wm{i}", [128, 8, 256], F32) for i in range(2)])
                bm = Ring([sb(ph, f"{pre}bm{i}", [128, 256], F32) for i in range(2)])
                ngb = Ring([sb(ph, f"{pre}ngb{i}", [128, 256], F32) for i in range(2)])
                tmpm = Ring([sb(ph, f"{pre}tmpm{i}", [128, 256], F32) for i in range(2)])
                S.op("sp", f_dma(c_sb.t[:, 0:8], c_fm[:, :]), writes=[c_sb.b], dma=Buf(pre + "c0"))
                if with_ctx:
                    S.op("sp", f_dma(c_sb.t[:, 8:16], cctx_fm[:, :]), writes=[c_sb.b], dma=Buf(pre + "c1"))
                S.op("act", f_act(e_c.t[:], c_sb.t[:], AF.Exp, scale=-1.0), reads=[c_sb.b], writes=[e_c.b])
                S.op("dve", f_ts(e_c.t[:], e_c.t[:], 1.0, None, ALU.add), reads=[e_c.b], writes=[e_c.b])
                S.op("dve", f_rcp(e_c.t[:], e_c.t[:]), reads=[e_c.b], writes=[e_c.b])
                S.op("dve", f_tt(c_sb.t[:], c_sb.t[:], e_c.t[:], ALU.mult), reads=[c_sb.b, e_c.b], writes=[c_sb.b])
                for i in range(nv):
                    S.op("dve", f_ts(screp.t[:, i, :], ones_f.t[:], c_sb.t[:, i:i + 1], None, ALU.mult),
                         reads=[ones_f.b, c_sb.b], writes=[screp.b])
                w_mod_v = w_mod.rearrange("(kc p) n -> p kc n", p=128)
                for ci, nci in enumerate(chunks):
                    n0 = nci * 256
                    wmt = wm.next()
                    bmt = bm.next()
                    q_ = ("sp", "act")[ci % 2] if with_ctx else "sp"
                    S.op(q_, f_dma(wmt.t[:, :, :], w_mod_v[:, :, n0:n0 + 256]), writes=[wmt.b], dma=wmt.b)
                    S.op("sp", f_dma(bmt.t[:], b_mod[n0:n0 + 256].partition_broadcast(128)), writes=[bmt.b], dma=bmt.b)
                    which = nci // 4
                    c0_ = (nci % 4) * 256
                    half = slice(c0_, c0_ + 256)
                    if which in (1, 4):
                        ngt = ngb.next()
                        gsrc = norm1_g if which == 1 else norm2_g
                        S.op("sp", f_dma(ngt.t[:], gsrc[c0_:c0_ + 256].partition_broadcast(128)), writes=[ngt.b], dma=ngt.b)
                    variants = [(0, banks[0])] + ([(8, banks[1])] if (which < 2 and with_ctx) else [])
                    for off, pp_ in variants:
                        pp = T(pp_.t[:, 0:256], pp_.b.name)
                        pp.b = pp_.b
                        for kc in range(8):
                            S.op("pe", f_mm(pp.t, screp.t[:, off + kc, :], wmt.t[:, kc, :], kc == 0, kc == 7),
                                 reads=[screp.b, wmt.b], writes=[pp.b])
                        key = (which, off)
                        if which in (0, 2):
                            d_ = dst[key]
                            S.op("dve", f_tt(d_.t[:, half], pp.t, bmt.t[:], ALU.add), reads=[pp.b, bmt.b], writes=[d_.b])
                        elif which == 1:
                            d_ = dst[key]
                            tm = tmpm.next()
                            S.op("dve", f_tt(tm.t[:], pp.t, bmt.t[:], ALU.add), reads=[pp.b, bmt.b], writes=[tm.b])
                            S.op("dve", f_stt(d_.t[:, half], tm.t[:], 1.0, ngt.t[:], ALU.add, ALU.mult),
                                 reads=[tm.b, ngt.b], writes=[d_.b])
                        else:
                            row = {3: 1, 4: 0, 5: 2}[which]
                            tm = tmpm.next()
                            S.op("dve", f_tt(tm.t[:], pp.t, bmt.t[:], ALU.add), reads=[pp.b, bmt.b], writes=[tm.b])
                            if which == 4:
                                S.op("dve", f_stt(tm.t[:], tm.t[:], 1.0, ngt.t[:], ALU.add, ALU.mult),
                                     reads=[tm.b, ngt.b], writes=[tm.b])
                            S.op("sp", f_dma(modsave[row:row + 1, c0_:c0_ + 256], tm.t[0:1, :]), reads=[tm.b], dma=tm.b)

            with ExitStack() as ph:
                s1c = sb(ph, "s1c", [128, D], F32)
                b1c = sb(ph, "b1c", [128, D], F32)
                fr0 = make_front(ph, "f0", nx=2, nh=2)
                g0 = {"F": make_gate_tiles(ph, "g0F"), "B": make_gate_tiles(ph, "g0B")}
                adaln(ph, "a0", list(range(8)), True,
                      {(0, 0): b1b, (0, 8): b1c, (1, 0): s1b, (1, 8): s1c}, (p1, p2))

                for X, tiles in (("B", (1, 0)), ("F", (0, 1))):
                    for t in tiles:
                        hT = front(fr0, ctx_seq[t * 128:(t + 1) * 128, :], 128, s1c, b1c)
                        kvz_proj(hT, g0[X])
                        gn = gate(X, g0[X])
                        state_stage(X, g0[X], gn, g0[X]["ktok"], g0[X]["vbf"], None)
                if dbg:
                    S.op("sp", f_dma(d_mod[0], s1b.t[:]), reads=[s1b.b], dma=Buf("dd0"))
                    S.op("sp", f_dma(d_mod[1], b1b.t[:]), reads=[b1b.b], dma=Buf("dd1"))
                    S.op("sp", f_dma(d_mod[2], s1c.t[:]), reads=[s1c.b], dma=Buf("dd2"))
                    S.op("sp", f_dma(d_mod[3], b1c.t[:]), reads=[b1c.b], dma=Buf("dd3"))
                    S.op("sp", f_dma(d_st[0], SF.t[:].rearrange("p a b -> p (a b)")), reads=SF.regs, dma=Buf("dd4"))
                    S.op("sp", f_dma(d_st[1], SBs.t[:].rearrange("p a b -> p (a b)")), reads=SBs.regs, dma=Buf("dd5"))
                S.barrier()
                S.flush()
            if stop <= 0:
                return nc

            sps = Ring([sb(mix, f"sps{i}", [128, 2, 128], BF16) for i in range(2)])
            gcol = sb(mix, "gcol", [128, 2, (15 + 2 * HALO_T) * 64], BF16)
            diagT = sb(mix, "diagT", [128, 4 * 31, 128], BF16)
            S.op("pool", f_memset(gcol.t[:, :, 0:15 * 64], 0.0), writes=[gcol.b])
            for c in range(4):
                for k in range(31):
                    S.op("dve", f_ts(diagT.t[:, c * 31 + k, :], ident_b.t[:], cw.t[:, c, k:k + 1], None, ALU.mult),
                         reads=[ident_b.b, cw.b], writes=[diagT.b])

            with ExitStack() as ph:
                pcg = Buf("precast")
                bg = []

                def _bg_dma(out_ap, in_ap, grp, lst=None):
                    def go(dep):
                        i = S.op("pool", f_dma(out_ap, in_ap), dma=grp, extra=[dep])
                        if lst is not None:
                            lst.append(i)
                    return go
                g1b = sb(ph, "g1b", [128, D], F32)
                adaln(ph, "a1", list(range(8, 24)), False, {(2, 0): g1b}, (p7,))
                wgrp1 = Buf("wgrp1")
                w_rest_ops = []
                for kc in (0, 4):
                    bg.append(_bg_dma(w_out_sb.t[:, kc:kc + 4, :], w_out_v[:, kc:kc + 4, :], wgrp1, w_rest_ops))
                for (a, b_) in rest_cols:
                    bg.append(_bg_dma(w_in_sb.t[:, :, a:b_], w_in_v[:, :, a:b_], wgrp1, w_rest_ops))
                for kc in range(8):
                    bg.append(_bg_dma(w_up_bf[kc * 128:(kc + 1) * 128, :], w_up[kc * 128:(kc + 1) * 128, :], pcg))
                for j0 in range(0, NH, 2):
                    bg.append(_bg_dma(w_dn_bf[j0 * 128:(j0 + 2) * 128, :], w_down[j0 * 128:(j0 + 2) * 128, :], pcg))

                def fold_w_out():
                    for kc in range(8):
                        if kc < 4:
                            S.op("dve", f_tt(w_out_sb.t[:, kc, :], w_out_sb.t[:, kc, :], g1b.t[:], ALU.mult),
                                 reads=[g1b.b], writes=[w_out_regs[kc]], extra=w_rest_ops)
                        else:
                            S.op("dve", f_stt(w_out_sb.t[:, kc, :], w_out_sb.t[:, kc, :], gng_sb.t[:, kc - 4:kc - 3], g1b.t[:],
                                              ALU.mult, ALU.mult),
                                 reads=[g1b.b, gng_sb.b], writes=[w_out_regs[kc]], extra=w_rest_ops)
                fold_done = []
                fr1 = make_front(ph, "f1", nx=3, nh=2, nm=CFG["p1_hm"])
                g1r = Ring([make_gate_tiles(ph, f"g1{i}") for i in range(CFG["p1_gate"])])
                ecg = sb(ph, "ecg1", [128, 2, 128], F32)
                for t in range(NT - 1, -1, -1):
                    g1 = g1r.next()
                    hT = front(fr1, x_seq[t * 128:(t + 1) * 128, :], 128, s1b, b1b)
                    if bg and t % 2 == 0:
                        bg.pop(0)(fr1["last_x"])
                        if len(w_rest_ops) == 6 and not fold_done:
                            fold_w_out()
                            fold_done.append(1)
                    kvz_proj(hT, g1, save_t=t if t < EXT else None)
                    if t < HALO_T:
                        for m in range(4):
                            col = (C_CU + 256 + m * 128) if m < 2 else (C_CG + 256 + (m - 2) * 128)
                            for kc in range(8):
                                S.op("pe", f_mm(p1.t[:, m * 128:(m + 1) * 128], w_in_sb.t[:, kc, col:col + 128],
                                                hT.t[:, kc, :], kc == 0, kc == 7),
                                     reads=[hT.b], extra=w_all_ops, writes=[p1.b])
                        p1v = p1.t[:, :].rearrange("p (a b) -> p a b", a=4)
                        act_sigmoid(ecg.t[:], p1v[:, 2:4, :], ecg.b, p1.b)
                        pos = (15 + 2 * t) * 64
                        S.op("dve", f_tt(gcol.t[:, :, pos:pos + 128], p1v[:, 0:2, :], ecg.t[:], ALU.mult),
                             reads=[p1.b, ecg.b], writes=[gcol.b])
                    gn = gate("B", g1)
                    state_stage("B", g1, gn, g1["ktok"], g1["vbf"], (2 * t, 2 * t + 1) if t < EXT else None)
                while bg:
                    bg.pop(0)(fr1["last_x"])
                if not fold_done:
                    fold_w_out()
                if dbg:
                    S.op("sp", f_dma(d_st[2], SBs.t[:].rearrange("p a b -> p (a b)")), reads=SBs.regs, dma=Buf("dd6"))
                    S.op("sp", f_dma(d_gcol[:, :], gcol.t[:].rearrange("p a b -> p (a b)")), reads=[gcol.b], dma=Buf("dd7"))
                S.barrier()
                S.flush()
            if stop <= 1:
                return nc

            with ExitStack() as ph:
                fr2 = make_front(ph, "f2", nx=2, nh=2, nm=CFG["p2_hm"])
                g2r = Ring([make_gate_tiles(ph, f"g2{i}") for i in range(CFG["p2_gate"])])
                qk_r = Ring([sb(ph, f"qk_s{i}", [128, 4, 128], F32) for i in range(CFG["p2_qk"])])
                eog_r = Ring([sb(ph, f"eog{i}", [128, 512], F32) for i in range(CFG["p2_og"])])
                sog_r = Ring([sb(ph, f"sog{i}", [128, 512], F32) for i in range(CFG["p2_og"])])
                EEp = sb(ph, "EEp", [128, 2, 128], F32)
                EEn = sb(ph, "EEn", [128, 2, 128], F32)
                EE = {"Fp": EEp, "Bp": EEp, "Fn": EEn, "Bn": EEn}
                QK = {X + s: sb(ph, "QK" + X + s, [128, 2, 128], BF16) for X in "FB" for s in "qk"}
                scm = sb(ph, "scm", [128, 8, 128], BF16)
                SFbf = Ring([sb(ph, f"SFbf{i}", [128, 2, 128], BF16) for i in range(2)])
                sbl = Ring([sb(ph, f"sbl{i}", [128, 2, 256], BF16) for i in range(2)])
                ost = sb(ph, "ost", [128, 12], F32)
                o_g = sb(ph, "o_g", [128, 512], BF16)
                ecg2 = sb(ph, "ecg2", [128, 2, 128], F32)
                growp = Ring([sb(ph, f"growp{i}", [128, 2, 4, 94], BF16) for i in range(2)])
                mixT_r = Ring([sb(ph, f"mixT{i}", [128, 8, 256], BF16) for i in range(CFG["p2_mixT"])])
                y32 = sb(ph, "y32", [128, 4, 256], F32)
                yb = sb(ph, "yb", [128, 4, 256], BF16)
                ysq = sb(ph, "ysq", [128, 4, 256], BF16)
                lnm = sb(ph, "lnm", [128, 256], F32)
                lnv = sb(ph, "lnv", [128, 256], F32)
                lnr = sb(ph, "lnr", [128, 256], F32)
                ynt = Ring([sb(ph, f"ynt{i}", [128, 256], F32) for i in range(1)])
                eyn = Ring([sb(ph, f"eyn{i}", [128, 256], F32) for i in range(1)])
                xr = Ring([sb(ph, f"xr{i}", [128, D], F32) for i in range(CFG["p2_xr"])])
                for gt in growp.tiles:
                    S.op("pool", f_memset(gt.t[:], 0.0), writes=[gt.b])

                def mixer_tile(t, lt, grow, mixT):
                    g2 = g2r.next()
                    qk_s = qk_r.next()
                    eog = eog_r.next()
                    sog = sog_r.next()
                    sbt2 = sbl.next()
                    S.op("sp", f_dma(sbt2.t[:], sbp_dram[2 * t:2 * t + 2].rearrange("c p f -> p c f")),
                         writes=[sbt2.b], dma=sbt2.b)
                    hT = front(fr2, x_seq[t * 128:(t + 1) * 128, :], 128, s1b, b1b)
                    zTa = zTa_r.next()
                    zcur["t"] = zTa
                    S.op("sp", f_dma(g2["ktok"].t[:], kt_d[t]), writes=[g2["ktok"].b], dma=g2["ktok"].b)
                    S.op("sp", f_dma(g2["vbf"].t[:], v_d[t]), writes=[g2["vbf"].b], dma=g2["vbf"].b)
                    S.op("sp", f_dma(zTa.t[0:32, :], z_d[t]), writes=[zTa.b], dma=zTa.b)
                    if p2_sub <= 1:
                        return
                    for m in range(4):
                        col = (C_CU + m * 128) if m < 2 else (C_CG + (m - 2) * 128)
                        for kc in range(8):
                            S.op("pe", f_mm(p1.t[:, m * 128:(m + 1) * 128], w_in_sb.t[:, kc, col:col + 128],
                                            hT.t[:, kc, :], kc == 0, kc == 7),
                                 reads=[hT.b], extra=w_all_ops, writes=[p1.b])
                    for m in range(4):
                        col = C_Q + m * 128
                        for kc in range(8):
                            S.op("pe", f_mm(p2.t[:, m * 128:(m + 1) * 128], w_in_sb.t[:, kc, col:col + 128],
                                            hT.t[:, kc, :], kc == 0, kc == 7),
                                 reads=[hT.b], extra=w_all_ops, writes=[p2.b])
                    for kc in range(8):
                        S.op("pe", f_mm(p3.t[:, :], hT.t[:, kc, :], w_in_sb.t[:, kc, C_OG:C_OG + 512], kc == 0, kc == 7),
                             reads=[hT.b], extra=w_all_ops, writes=[p3.b])
                    if p2_sub <= 2:
                        return
                    p1v = p1.t[:, :].rearrange("p (a b) -> p a b", a=4)
                    act_sigmoid(ecg2.t[:], p1v[:, 2:4, :], ecg2.b, p1.b)
                    for c in range(2):
                        S.op("dve", f_tt(grow.t[:, c, 2 * lt:2 * lt + 2, 15:79],
                                         p1v[:, c, :].rearrange("p (r w) -> p r w", r=2),
                                         ecg2.t[:, c, :].rearrange("p (r w) -> p r w", r=2), ALU.mult),
                             reads=[p1.b, ecg2.b], writes=[grow.b])
                    if p2_sub <= 3:
                        return
                    S.op("act", f_act(qk_s.t[:], p2.t[:, :].rearrange("p (a b) -> p a b", a=4), AF.Copy),
                         reads=[p2.b], writes=[qk_s.b])
                    act_sigmoid(eog.t[:], p3.t[:, :], eog.b, p3.b)
                    S.op("dve", f_tt(sog.t[:], p3.t[:, :], eog.t[:], ALU.mult), reads=[p3.b, eog.b], writes=[sog.b])
                    if p2_sub <= 4:
                        return
                    gnF = gate("F", g2)
                    gnB = gate("B", g2)
                    Et, ktail, dec = g2["Et"], g2["ktail"], g2["dec"]
                    S.op("pe", f_mm(p6.t[:, 256:512], masks_b.t[:, M_US, :], gnF.t[:], True, True),
                         reads=[masks_b.b, gnF.b], writes=[p6.b])
                    for j in range(2):
                        S.op("pe", f_mm(p6.t[:, 2 * j:2 * j + 2], gnF.t[:, j * 128:(j + 1) * 128], ind_b.t[:], True, True),
                             reads=[gnF.b, ind_b.b], writes=[p6.b])
                    S.op("act", f_act(Et.t[:], p6.t[:, 256:512], AF.Exp, scale=-1.0 / 16), reads=[p6.b], writes=[Et.b])
                    S.op("act", f_act(dec.t[:], p6.t[:, 0:4], AF.Exp, scale=-1.0 / 16), reads=[p6.b], writes=[dec.b])
                    S.op("dve", f_tt(ktail.t[:], g2["ktok"].t[:], Et.t[:], ALU.mult),
                         reads=[g2["ktok"].b, Et.b], writes=[ktail.b])
                    for xi, (X, gn, mk) in enumerate((("F", gnF, M_LI), ("B", gnB, M_UI))):
                        for j in range(2):
                            S.op("pe", f_mm(p7.t[:, xi * 256 + j * 128:xi * 256 + (j + 1) * 128],
                                            gn.t[:, j * 128:(j + 1) * 128], masks_b.t[:, mk, :], True, True),
                                 reads=[gn.b, masks_b.b], writes=[p7.b])
                    for xi, X in enumerate("FB"):
                        src = p7.t[:, xi * 256:(xi + 1) * 256].rearrange("p (a b) -> p a b", a=2)
                        S.op("act", f_act(EE[X + "p"].t[:], src, AF.Exp, scale=-1.0 / 16, bias=float(np.log(0.125))),
                             reads=[p7.b], writes=[EE[X + "p"].b])
                        S.op("act", f_act(EE[X + "n"].t[:], src, AF.Exp, scale=1.0 / 16),
                             reads=[p7.b], writes=[EE[X + "n"].b])
                        S.op("dve", f_tt(QK[X + "q"].t[:], qk_s.t[:, 0:2, :], EE[X + "p"].t[:], ALU.mult),
                             reads=[qk_s.b, EE[X + "p"].b], writes=[QK[X + "q"].b])
                        S.op("dve", f_tt(QK[X + "k"].t[:], qk_s.t[:, 2:4, :], EE[X + "n"].t[:], ALU.mult),
                             reads=[qk_s.b, EE[X + "n"].b], writes=[QK[X + "k"].b])
                    if p2_sub <= 5:
                        return
                    for e_ in range(2):
                        pp = p6 if e_ == 0 else p7
                        sl = slice(e_ * 64, (e_ + 1) * 64)
                        for xi, X in enumerate("FB"):
                            for j in range(2):
                                slot = xi * 2 + j
                                S.op("pe", f_mm(pp.t[:, slot * 128:(slot + 1) * 128], QK[X + "k"].t[sl, j, :],
                                                QK[X + "q"].t[sl, j, :], True, True),
                                     reads=[QK[X + "k"].b, QK[X + "q"].b], writes=[pp.b])
                    for e_ in range(2):
                        pp = p6 if e_ == 0 else p7
                        for xi, X in enumerate("FB"):
                            mk = M_LI if X == "F" else M_UI
                            for j in range(2):
                                slot = xi * 2 + j
                                h = 2 * j + e_
                                S.op("dve", f_tt(scm.t[:, xi * 4 + h, :], pp.t[:, slot * 128:(slot + 1) * 128],
                                                 masks.t[:, mk, :], ALU.mult),
                                     reads=[pp.b, masks.b], writes=[scm.b])
                    if p2_sub <= 6:
                        return
                    S_prev_tiles = {}
                    vbf = g2["vbf"]
                    for lc in range(2):
                        sf = SFbf.next()
                        S.op("act", f_act(sf.t[:], SF.t[:], AF.Copy), reads=SF.regs, writes=[sf.b])
                        S_prev_tiles[lc] = sf
                        for j in range(2):
                            S.op("pe", f_mm(p4.t[:, j * 256:(j + 1) * 256],
                                            ktail.t[lc * 64:(lc + 1) * 64, j * 128:(j + 1) * 128],
                                            vbf.t[lc * 64:(lc + 1) * 64, j * 256:(j + 1) * 256], True, True),
                                 reads=[ktail.b, vbf.b], writes=[p4.b])
                        for j in range(2):
                            for e_ in range(2):
                                sl = slice(e_ * 64, (e_ + 1) * 64)
                                S.op("dve", f_stt(SF.t[sl, j, :], SF.t[sl, j, :], dec.t[sl, 2 * j + lc:2 * j + lc + 1],
                                                  p4.t[sl, j * 256 + e_ * 128:j * 256 + (e_ + 1) * 128],
                                                  ALU.mult, ALU.add),
                                     reads=[SF.regs[2 * j + e_], dec.b, p4.b], writes=[SF.regs[2 * j + e_]])
                    if p2_sub <= 7:
                        return
                    for h in range(4):
                        j, e_ = h // 2, h % 2
                        sl = slice(e_ * 64, (e_ + 1) * 64)
                        oc = slice(h * 128, (h + 1) * 128)
                        S.op("pe", f_mm(p5.t[:, oc], scm.t[:, h, :], vbf.t[:, oc], True, False),
                             reads=[scm.b, vbf.b], writes=[p5.b])
                        S.op("pe", f_mm(p5.t[:, oc], scm.t[:, 4 + h, :], vbf.t[:, oc], False, False),
                             reads=[scm.b, vbf.b], writes=[p5.b])
                        for lc in range(2):
                            tl = slice(lc * 64, (lc + 1) * 64)
                            S.op("pe", f_mm(p5.t[tl, oc], QK["Fq"].t[sl, j, tl], S_prev_tiles[lc].t[sl, j, :], False, False),
                                 reads=[QK["Fq"].b, S_prev_tiles[lc].b], writes=[p5.b])
                            S.op("pe", f_mm(p5.t[tl, oc], QK["Bq"].t[sl, j, tl], sbt2.t[sl, lc, j * 128:(j + 1) * 128], False, True),
                                 reads=[QK["Bq"].b, sbt2.b], writes=[p5.b])
                    if p2_sub <= 8:
                        return
                    for h in range(4):
                        S.op("act", f_act(junk.t[:, h * 128:(h + 1) * 128], p5.t[:, h * 128:(h + 1) * 128], AF.Square,
                                          accum=ost.t[:, h:h + 1]),
                             reads=[p5.b], writes=[ost.b, junk.b])
                    S.op("act", f_act(ost.t[:, 4:8], ost.t[:, 0:4], AF.Ln, bias=EPS, scale=1.0 / 128), reads=[ost.b], writes=[ost.b])
                    S.op("act", f_act(ost.t[:, 8:12], ost.t[:, 4:8], AF.Exp, scale=-0.5), reads=[ost.b], writes=[ost.b])
                    for h in range(4):
                        oc = slice(h * 128, (h + 1) * 128)
                        S.op("dve", f_stt(o_g.t[:, oc], p5.t[:, oc], ost.t[:, 8 + h:9 + h], sog.t[:, oc], ALU.mult, ALU.mult),
                             reads=[p5.b, ost.b, sog.b], writes=[o_g.b])
                    if p2_sub <= 9:
                        return
                    for h in range(4):
                        S.op("pe", f_mm(p7.t[:, h * 128:(h + 1) * 128], o_g.t[:, h * 128:(h + 1) * 128], ident_b.t[:, :], True, True),
                             reads=[o_g.b, ident_b.b], writes=[p7.b])
                    S.op("act", f_act(mixT.t[:, 4:8, lt * 128:(lt + 1) * 128],
                                      p7.t[:, 0:512].rearrange("p (a b) -> p a b", a=4), AF.Copy),
                         reads=[p7.b], writes=[mixT.b])

                def conv_block(t0, ntl, grow, mixT):
                    n = ntl * 128
                    nr = 2 * ntl
                    r0 = 2 * t0
                    cps = {0: p1, 1: p1, 2: p2, 3: p2}
                    for c in range(4):
                        pp = cps[c]
                        oc = slice((c % 2) * 256, (c % 2) * 256 + n)
                        for k in range(31):
                            if c < 2:
                                rhs = grow.t[:, c, 0:nr, k:k + 64]
                            else:
                                rhs = gcol.t[:, c - 2, (r0 + k) * 64:(r0 + k) * 64 + n]
                            S.op("pe", f_mm(pp.t[:, oc], diagT.t[:, c * 31 + k, :], rhs, k == 0, k == 30),
                                 reads=[diagT.b, grow.b if c < 2 else gcol.b], writes=[pp.b])
                    for c in range(4):
                        pp = cps[c]
                        oc = slice((c % 2) * 256, (c % 2) * 256 + n)
                        S.op("act", f_act(y32.t[:, c, 0:n], pp.t[:, oc], AF.Identity, bias=cvec.t[:, 0, c:c + 1]),
                             reads=[pp.b, cvec.b], writes=[y32.b])
                        S.op("act", f_act(ysq.t[:, c, 0:n], pp.t[:, oc], AF.Square, bias=cvec.t[:, 0, c:c + 1]),
                             reads=[pp.b, cvec.b], writes=[ysq.b])
                        S.op("dve", f_cp(yb.t[:, c, 0:n], y32.t[:, c, 0:n]), reads=[y32.b], writes=[yb.b])
                    for c in range(4):
                        S.op("pe", f_mm(p6.t[:, 0:n], onesM.t[:], yb.t[:, c, 0:n], c == 0, c == 3),
                             reads=[onesM.b, yb.b], writes=[p6.b])
                    for c in range(4):
                        S.op("pe", f_mm(p6.t[:, 256:256 + n], onesM.t[:], ysq.t[:, c, 0:n], c == 0, c == 3),
                             reads=[onesM.b, ysq.b], writes=[p6.b])
                    S.op("act", f_act(lnm.t[:, 0:n], p6.t[:, 0:n], AF.Copy), reads=[p6.b], writes=[lnm.b])
                    S.op("dve", f_tt(lnv.t[:, 0:n], lnm.t[:, 0:n], lnm.t[:, 0:n], ALU.mult), reads=[lnm.b], writes=[lnv.b])
                    S.op("dve", f_tt(lnv.t[:, 0:n], p6.t[:, 256:256 + n], lnv.t[:, 0:n], ALU.subtract),
                         reads=[p6.b, lnv.b], writes=[lnv.b])
                    S.op("act", f_act(lnr.t[:, 0:n], lnv.t[:, 0:n], AF.Ln, bias=EPS), reads=[lnv.b], writes=[lnr.b])
                    S.op("act", f_act(lnr.t[:, 0:n], lnr.t[:, 0:n], AF.Exp, scale=-0.5), reads=[lnr.b], writes=[lnr.b])
                    for c in range(4):
                        yn = ynt.next()
                        ey = eyn.next()
                        S.op("dve", f_tt(yn.t[:, 0:n], y32.t[:, c, 0:n], lnm.t[:, 0:n], ALU.subtract),
                             reads=[y32.b, lnm.b], writes=[yn.b])
                        S.op("dve", f_tt(yn.t[:, 0:n], yn.t[:, 0:n], lnr.t[:, 0:n], ALU.mult), reads=[yn.b, lnr.b], writes=[yn.b])
                        S.op("dve", f_ts(yn.t[:, 0:n], yn.t[:, 0:n], cvec.t[:, 1, c:c + 1], cvec.t[:, 2, c:c + 1], ALU.mult, ALU.add),
                             reads=[yn.b, cvec.b], writes=[yn.b])
                        act_sigmoid(ey.t[:, 0:n], yn.t[:, 0:n], ey.b, yn.b)
                        S.op("dve", f_tt(mixT.t[:, c, 0:n], yn.t[:, 0:n], ey.t[:, 0:n], ALU.mult),
                             reads=[yn.b, ey.b], writes=[mixT.b])
                    for lt in range(ntl):
                        t = t0 + lt
                        xrt = xr.next()
                        S.op("sp", f_dma(xrt.t[:], x_seq[t * 128:(t + 1) * 128, :]), writes=[xrt.b], dma=xrt.b)
                        for half, pp in ((0, p3), (1, p5)):
                            for kc in range(8):
                                S.op("pe", f_mm(pp.t[:, :], mixT.t[:, kc, lt * 128:(lt + 1) * 128],
                                                w_out_sb.t[:, kc, half * 512:(half + 1) * 512], kc == 0, kc == 7),
                                     reads=[mixT.b], writes=[pp.b])
                        for half, pp in ((0, p3), (1, p5)):
                            hs = slice(half * 512, (half + 1) * 512)
                            S.op("dve", f_tt(xrt.t[:, hs], pp.t[:, :], xrt.t[:, hs], ALU.add),
                                 reads=[pp.b, xrt.b], writes=[xrt.b])
                        S.op("sp", f_dma(x1s[t * 128:(t + 1) * 128, :], xrt.t[:]), reads=[xrt.b], dma=xrt.b)

                t = 0
                while t < EXT:
                    ntl = min(2, EXT - t)
                    grow = growp.next()
                    mixT = mixT_r.next()
                    for lt in range(ntl):
                        mixer_tile(t + lt, lt, grow, mixT)
                    if p2_conv:
                        conv_block(t, ntl, grow, mixT)
                    t += ntl
                    if t >= 2 * p2_blocks:
                        break
                if dbg:
                    S.op("sp", f_dma(d_st[3], SF.t[:].rearrange("p a b -> p (a b)")), reads=SF.regs, dma=Buf("dd9"))
                S.barrier()
                S.flush()
        if stop <= 2:
            return nc

        with ExitStack() as ffn:
            w_up_sb = sb(ffn, "w_up_sb", [128, 8, 2 * HID], BF16)
            w_dn_sb = sb(ffn, "w_dn_sb", [128, NH, D], BF16)
            g2b = sb(ffn, "g2b", [128, D], F32)
            w_up_v = w_up_bf.rearrange("(kc p) n -> p kc n", p=128)
            NG = 8
            gsz = [3, 3, 3, 3, 3, 3, 2, 2]
            gst = [sum(gsz[:g]) for g in range(NG)]
            w_up_grp = {}
            for g in range(NG):
                gb = Buf(f"wupg{g}")
                ops_ = []
                for hh in range(2):
                    c0_ = hh * HID + gst[g] * 128
                    c1_ = c0_ + gsz[g] * 128
                    ops_.append(S.op(("sp", "act")[g % 2], f_dma(w_up_sb.t[:, :, c0_:c1_], w_up_v[:, :, c0_:c1_]), dma=gb))
                for jj in range(gst[g], gst[g] + gsz[g]):
                    w_up_grp[jj] = ops_
            w_dn_v = w_dn_bf.rearrange("(j p) n -> p j n", p=128)
            S.op("sp", f_dma(g2b.t[:], modsave[2, :].partition_broadcast(128)), writes=[g2b.b], dma=g2b.b)
            w_dn_regs = {}
            for j0 in range(0, NH, 2):
                rb = Buf(f"wdn{j0}")
                S.op(("sp", "act")[(j0 // 2) % 2], f_dma(w_dn_sb.t[:, j0:j0 + 2, :], w_dn_v[:, j0:j0 + 2, :]), writes=[rb], dma=rb)
                for j in (j0, j0 + 1):
                    S.op("dve", f_tt(w_dn_sb.t[:, j, :], w_dn_sb.t[:, j, :], g2b.t[:], ALU.mult),
                         reads=[rb, g2b.b], writes=[rb])
                    w_dn_regs[j] = rb

            s2b = sb(ffn, "s2b", [128, D], F32)
            b2b = sb(ffn, "b2b", [128, D], F32)
            fgb = sb(ffn, "fgb", [128, D], F32)
            fdw = sb(ffn, "fdw", [128, NH, 4], F32)
            fr3 = make_front(ffn, "f3", nx=CFG["f3_nx"], nh=1)
            h2Tr = Ring([sb(ffn, f"h2T{i}", [128, 8, FB], BF16) for i in range(CFG["h2T"])])
            abuf = Ring([sb(ffn, f"abuf{i}", [128, FB + 2], F32) for i in range(CFG["ffn_ring"])])
            vbuf = Ring([sb(ffn, f"vbuf{i}", [128, FB + 1], F32) for i in range(CFG["ffn_ring"])])
            tcv = Ring([sb(ffn, f"tcv{i}", [128, FB], F32) for i in range(CFG["ffn_ring"])])
            esg = Ring([sb(ffn, f"esg{i}", [128, FB], F32) for i in range(CFG["ffn_ring"])])
            car = sb(ffn, "car", [128, NH, 3], F32)
            hidT = sb(ffn, "hidT", [128, NH, FB], BF16)
            hidT.regs = [Buf(f"hid{j}") for j in range(NH)]
            xrf = Ring([sb(ffn, f"xrf{i}", [128, D], F32) for i in range(2)])
            fst = Ring([sb(ffn, f"fst{i}", [128, 4], F32) for i in range(2)])

            S.op("sp", f_dma(s2b.t[:], modsave[0, :].partition_broadcast(128)), writes=[s2b.b], dma=s2b.b)
            S.op("sp", f_dma(b2b.t[:], modsave[1, :].partition_broadcast(128)), writes=[b2b.b], dma=b2b.b)
            S.op("sp", f_dma(fgb.t[:], final_g.partition_broadcast(128)), writes=[fgb.b], dma=fgb.b)
            S.op("sp", f_dma(fdw.t[:], fdw_fm[:, :, :]), writes=[fdw.b], dma=fdw.b)
            S.op("dve", f_memset(car.t[:], 0.0), writes=[car.b])
            for xt_ in xrf.tiles:
                S.op("pool", f_memset(xt_.t[:], 0.0), writes=[xt_.b])

            blocks = [(b * FB, FB) for b in range(SEQ // 2 // FB)] + [(SEQ // 2, 1)]
            for (t0, n) in blocks:
                h2T = h2Tr.next()
                for m in range((n + 127) // 128):
                    r0 = t0 + m * 128
                    nn = min(128, n - m * 128)
                    front(fr3, x1s[r0:r0 + nn, :], nn, s2b, b2b, dst=(h2T, h2T.t[:, :, m * 128:m * 128 + nn]))
                for j in range(NH):
                    pp = pbank[j % 4]
                    ab = abuf.next()
                    vb = vbuf.next()
                    for half in range(2):
                        col = half * HID + j * 128
                        for kc in range(8):
                            S.op("pe", f_mm(pp.t[:, half * 256:half * 256 + n], w_up_sb.t[:, kc, col:col + 128],
                                            h2T.t[:, kc, 0:n], kc == 0, kc == 7),
                                 reads=[h2T.b], extra=w_up_grp[j], writes=[pp.b])
                    S.op("pool", f_cp(ab.t[:, 0:2], car.t[:, j, 0:2]), reads=[car.b], writes=[ab.b])
                    S.op("pool", f_cp(vb.t[:, 0:1], car.t[:, j, 2:3]), reads=[car.b], writes=[vb.b])
                    S.op("act", f_act(ab.t[:, 2:2 + n], pp.t[:, 0:n], AF.Copy), reads=[pp.b], writes=[ab.b])
                    if CFG["ffn_bal"]:
                        S.op("dve", f_cp(vb.t[:, 1:1 + n], pp.t[:, 256:256 + n]), reads=[pp.b], writes=[vb.b])
                    else:
                        S.op("act", f_act(vb.t[:, 1:1 + n], pp.t[:, 256:256 + n], AF.Copy), reads=[pp.b], writes=[vb.b])
                    S.op("pool", f_cp(car.t[:, j, 0:2], ab.t[:, n:n + 2]), reads=[ab.b], writes=[car.b])
                    S.op("pool", f_cp(car.t[:, j, 2:3], vb.t[:, n:n + 1]), reads=[vb.b], writes=[car.b])
                    tc_ = tcv.next()
                    es = esg.next()
                    S.op("dve", f_ts(tc_.t[:, 0:n], ab.t[:, 0:n], fdw.t[:, j, 0:1], fdw.t[:, j, 3:4], ALU.mult, ALU.add),
                         reads=[ab.b, fdw.b], writes=[tc_.b])
                    S.op("dve", f_stt(tc_.t[:, 0:n], ab.t[:, 1:n + 1], fdw.t[:, j, 1:2], tc_.t[:, 0:n], ALU.mult, ALU.add),
                         reads=[ab.b, fdw.b, tc_.b], writes=[tc_.b])
                    S.op("dve", f_stt(tc_.t[:, 0:n], ab.t[:, 2:n + 2], fdw.t[:, j, 2:3], tc_.t[:, 0:n], ALU.mult, ALU.add),
                         reads=[ab.b, fdw.b, tc_.b], writes=[tc_.b])
                    act_sigmoid(es.t[:, 0:n], tc_.t[:, 0:n], es.b, tc_.b)
                    S.op("pool" if CFG["ffn_bal"] == 1 else "dve", f_tt(tc_.t[:, 0:n], tc_.t[:, 0:n], vb.t[:, 0:n], ALU.mult),
                         reads=[tc_.b, vb.b], writes=[tc_.b])
                    S.op("dve", f_tt(hidT.t[:, j, 0:n], tc_.t[:, 0:n], es.t[:, 0:n], ALU.mult),
                         reads=[tc_.b, es.b], writes=[hidT.b])
                for m in range((n + 127) // 128):
                    tk0 = t0 - 1 + m * 128
                    lo = max(0, -tk0)
                    hi = min(128, SEQ // 2 - tk0, n - m * 128)
                    if hi <= lo:
                        continue
                    xt_ = xrf.next()
                    st = fst.next()
                    S.op("sp", f_dma(xt_.t[lo:hi, :], x1s[tk0 + lo:tk0 + hi, :]), writes=[xt_.b], dma=xt_.b)
                    for half, pp in ((0, p5), (1, p6)):
                        for j in range(NH):
                            S.op("pe", f_mm(pp.t[0:hi, :], hidT.t[:, j, m * 128:m * 128 + hi],
                                            w_dn_sb.t[:, j, half * 512:(half + 1) * 512], j == 0, j == NH - 1),
                                 reads=[hidT.b, w_dn_regs[j]], writes=[pp.b])
                    for half, pp in ((0, p5), (1, p6)):
                        hs = slice(half * 512, (half + 1) * 512)
                        S.op("dve", f_tt(xt_.t[0:hi, hs], pp.t[0:hi, :], xt_.t[0:hi, hs], ALU.add), reads=[pp.b, xt_.b], writes=[xt_.b])
                    S.op("act", f_act(junk.t[0:hi, :], xt_.t[0:hi, :], AF.Square, accum=st.t[0:hi, 0:1]), reads=[xt_.b], writes=[st.b, junk.b])
                    S.op("act", f_act(st.t[0:hi, 1:2], st.t[0:hi, 0:1], AF.Ln, bias=EPS, scale=1.0 / D), reads=[st.b], writes=[st.b])
                    S.op("act", f_act(st.t[0:hi, 2:3], st.t[0:hi, 1:2], AF.Exp, scale=-0.5), reads=[st.b], writes=[st.b])
                    S.op("dve", f_stt(xt_.t[0:hi, :], xt_.t[0:hi, :], st.t[0:hi, 2:3], fgb.t[0:hi, :], ALU.mult, ALU.mult),
                         reads=[xt_.b, st.b, fgb.b], writes=[xt_.b])
                    S.op("sp", f_dma(y_out[tk0 + lo:tk0 + hi, :], xt_.t[lo:hi, :]), reads=[xt_.b], dma=xt_.b)
            S.barrier()
            S.flush()
    return nc


def _consts():
    idx = np.arange(128)
    same = (idx[:, None] // 64) == (idx[None, :] // 64)
    a, b = idx[:, None], idx[None, :]
    masks = np.stack([same & (a <= b), same & (a >= b), same & (a < b), same & (a > b)], axis=1).astype(np.float32)
    ind = ((idx[:, None] // 64) == np.arange(2)[None, :]).astype(np.float32)
    return np.eye(128, dtype=np.float32), np.ascontiguousarray(masks), ind


def _fm(v, nchunk):
    return np.ascontiguousarray(np.asarray(v, np.float32).reshape(nchunk, 128).T)


def make_in_maps(x, c, ctx, c_ctx, w_mod, b_mod, norm1_g, w_in, conv_dw, conv_b, conv_ln_g, conv_ln_b,
                 w_gf, b_gf, w_gb, b_gb, gla_norm_g, w_out, norm2_g, w_up, ffn_dw, ffn_dw_b, w_down, final_g):
    f = lambda a: np.asarray(a, dtype=np.float32)
    x, c, ctx, c_ctx = f(x), f(c), f(ctx), f(c_ctx)
    ident, masks, ind = _consts()
    w_in0 = f(w_in)[0]
    w_in_rev = np.ascontiguousarray(np.concatenate([w_in0[:, :C_Z], w_in0[:, C_Z + 16:C_Z + 32], w_in0[:, C_Z:C_Z + 16]], axis=1))
    cdw = f(conv_dw)[0]
    fdw_ = f(ffn_dw)[0]
    zeros16 = np.zeros((16, 256), np.float32)
    gate = {"f": (f(w_gf)[0], f(b_gf)[0]), "b": (f(w_gb)[0], f(b_gb)[0])}
    cvec = np.stack([_fm(f(conv_b)[0], 4), _fm(f(conv_ln_g)[0], 4), _fm(f(conv_ln_b)[0], 4)], axis=1)
    common = dict(
        cctx_fm=_fm(c_ctx, 8), w_mod=f(w_mod)[0], b_mod=f(b_mod)[0], norm1_g=f(norm1_g)[0],
        cvec_fm=np.ascontiguousarray(cvec), gla_norm_fm=_fm(f(gla_norm_g)[0], 4), w_out=f(w_out)[0], norm2_g=f(norm2_g)[0],
        w_up=f(w_up)[0], w_down=f(w_down)[0], final_g=f(final_g), ident=ident, masks=masks, ind=ind,
    )
    in_maps = []
    for core in range(8):
        b, rev = core // 2, core % 2
        xs = x[b][::-1] if rev else x[b]
        cs = ctx[b][::-1] if rev else ctx[b]
        cd = cdw[::-1] if rev else cdw
        fd = fdw_[::-1] if rev else fdw_
        pF, pB = ("b", "f") if rev else ("f", "b")
        wgF_aug = np.concatenate([gate[pF][0], zeros16, gate[pF][1][None, :]], axis=0)
        wgB_aug = np.concatenate([zeros16, gate[pB][0], gate[pB][1][None, :]], axis=0)
        cw_fm = np.ascontiguousarray(cd.T.reshape(4, 128, 31).transpose(1, 0, 2))
        fdw_fm = np.concatenate([fd.T.reshape(NH, 128, 3).transpose(1, 0, 2),
                                 f(ffn_dw_b)[0].reshape(NH, 128).T[:, :, None]], axis=2)
        m = dict(common)
        m.update(
            x_seq=np.ascontiguousarray(xs), ctx_seq=np.ascontiguousarray(cs), c_fm=_fm(c[b], 8),
            w_in=w_in_rev if rev else w_in0, cw_fm=cw_fm, wgF_aug=np.ascontiguousarray(wgF_aug),
            wgB_aug=np.ascontiguousarray(wgB_aug), fdw_fm=np.ascontiguousarray(fdw_fm),
        )
        in_maps.append(m)
    return in_maps


def kernel(**inputs):
    in_maps = make_in_maps(**inputs)
    nc = build_program()
    res = run_bass_kernel_spmd(nc, in_maps, core_ids=list(range(8)))
    out = np.empty((4, SEQ, D), np.float32)
    for core in range(8):
        b, rev = core // 2, core % 2
        y = np.asarray(res.results[core]["y_out"], np.float32)
        if rev:
            out[b, SEQ // 2:] = y[::-1]
        else:
            out[b, :SEQ // 2] = y
    return out
```

```python
from contextlib import ExitStack

import numpy as np
import concourse.bass as bass
import concourse.mybir as mybir
from concourse.bass_utils import run_bass_kernel_spmd

F32 = mybir.dt.float32
BF16 = mybir.dt.bfloat16
AF = mybir.ActivationFunctionType
ALU = mybir.AluOpType

D = 1024
SEQ = 8192
CTX = 256
NT = 64
EXT = 33
HALO_T = 41
DIN = 2592
C_CU, C_CG, C_Q, C_K, C_V, C_OG, C_Z = 0, 512, 1024, 1280, 1536, 2048, 2560
HID = 2816
NH = 22
EPS = 1e-6
FB = 256
SEM_LIMIT = 4000


class Buf:
    __slots__ = ("name", "writer", "readers", "dsem", "dcount", "psum")

    def __init__(self, name, psum=False):
        self.name = name
        self.writer = None
        self.readers = []
        self.dsem = None
        self.dcount = 0
        self.psum = psum


class Op:
    __slots__ = ("idx", "stream", "fn", "deps", "is_dma", "dbuf", "needed", "signal", "cost", "pos")

    def __init__(self, idx, stream, fn, deps, is_dma, dbuf, cost):
        self.idx = idx
        self.stream = stream
        self.fn = fn
        self.deps = deps
        self.is_dma = is_dma
        self.dbuf = dbuf
        self.needed = False
        self.signal = None
        self.cost = cost
        self.pos = -1


class Fn:
    __slots__ = ("f", "cost")

    def __init__(self, f, cost):
        self.f = f
        self.cost = cost

    def __call__(self, e):
        return self.f(e)


CFG = {"f3_nx": 1, "h2T": 1, "p2_hm": 1, "p2_gate": 1, "p2_qk": 1, "p2_og": 1, "p2_xr": 2, "p2_mixT": 1, "p1_hm": 1, "add_eng": "dve", "kv_p2": 1, "p1_gate": 2, "hT_dve": 0, "ffn_ring": 3, "ffn_bal": 2}
PRIO_CP_PHASES = (0, 1)
PRIO_SLACK = 0.0
SCHED_WINDOW = 600
REORDER = True
SEM_LAT = 0.25
DEFAULT_COST = {"pe": 0.12, "act": 0.4, "dve": 0.3, "pool": 0.5, "sp": 3.0}


class Sched:
    STREAMS = ("pe", "act", "dve", "pool", "sp")

    def __init__(self, nc, stack):
        self.nc = nc
        self.stack = stack
        self.ops = []
        self.order = {s: [] for s in self.STREAMS}
        self.cursor = {s: 0 for s in self.STREAMS}
        self.floor = 0
        self.flushed = 0
        self.nsem = 0
        self.cur = {s: [None, 0] for s in self.STREAMS}
        self.waited = {s: {} for s in self.STREAMS}
        self.last_compute = {s: None for s in self.STREAMS}
        self.phase_dmas = []
        self.sim_time = []

    def new_sem(self, name):
        self.nsem += 1
        return self.stack.enter_context(self.nc.semaphore(f"{name}_{self.nsem}"))

    def op(self, stream, fn, reads=(), writes=(), dma=None, extra=()):
        idx = len(self.ops)
        deps = set(extra)
        for b in reads:
            if b.writer is not None:
                deps.add(b.writer)
            if b.psum:
                for r in b.readers:
                    if self.ops[r].stream != stream:
                        deps.add(r)
        for b in writes:
            if b.writer is not None:
                deps.add(b.writer)
            deps.update(b.readers)
        deps = sorted(d for d in deps if d >= self.floor and d != idx)
        if fn is None:
            cost = 0.0
        else:
            cost = getattr(fn, "cost", None)
            if cost is None:
                cost = DEFAULT_COST["sp" if dma is not None else stream]
            elif stream == "pool" and dma is None:
                cost = cost * 3.5
        o = Op(idx, stream, fn, deps, dma is not None, dma, cost)
        self.ops.append(o)
        for b in reads:
            b.readers.append(idx)
        for b in writes:
            b.writer = idx
            b.readers = []
        if dma is not None:
            self.phase_dmas.append(idx)
        elif fn is not None:
            self.last_compute[stream] = idx
        return idx

    def barrier(self):
        deps = list(range(self.floor, len(self.ops)))
        for s in self.STREAMS:
            self.op(s, None, extra=deps)
        self.floor = len(self.ops)
        self.phase_dmas = []
        self.last_compute = {s: None for s in self.STREAMS}

    def _schedule(self, lo, hi):
        ops = self.ops
        n = hi - lo
        if not REORDER:
            order = {s: [] for s in self.STREAMS}
            for o in ops[lo:hi]:
                order[o.stream].append(o)
            self.sim_time.append(0.0)
            return order
        indeg = [0] * n
        users = [[] for _ in range(n)]
        for o in ops[lo:hi]:
            c = 0
            for d in o.deps:
                if d >= lo:
                    users[d - lo].append(o.idx)
                    c += 1
            indeg[o.idx - lo] = c
        ready = {s: [] for s in self.STREAMS}
        rtime = [0.0] * n
        free_at = {s: 0.0 for s in self.STREAMS}
        blev = [0.0] * n
        PRIO_CP = len(self.sim_time) in PRIO_CP_PHASES
        if PRIO_CP:
            for k in range(n - 1, -1, -1):
                o = ops[lo + k]
                m = 0.0
                for u in users[k]:
                    v = blev[u - lo] + (0.0 if (ops[u].stream == o.stream == "pe") else SEM_LAT)
                    if v > m:
                        m = v
                blev[k] = m + o.cost
        for o in ops[lo:hi]:
            if indeg[o.idx - lo] == 0:
                ready[o.stream].append(o.idx)
        order = {s: [] for s in self.STREAMS}
        remaining = n
        tmax = 0.0
        while remaining:
            best = None
            for s in self.STREAMS:
                lst = ready[s]
                if not lst:
                    continue
                fa = free_at[s]
                mi = min(lst)
                if PRIO_CP:
                    cands = []
                    mst = None
                    for i in lst:
                        if i > mi + SCHED_WINDOW:
                            continue
                        st = rtime[i - lo]
                        if st < fa:
                            st = fa
                        cands.append((st, i))
                        if mst is None or st < mst:
                            mst = st
                    bi = None
                    for st, i in cands:
                        if st <= mst + PRIO_SLACK:
                            if bi is None or blev[i - lo] > blev[bi[1] - lo]:
                                bi = (st, i)
                    if best is None or (bi[0], bi[1]) < best[0]:
                        best = ((bi[0], bi[1]), s, bi[1])
                    continue
                for i in lst:
                    if i > mi + SCHED_WINDOW:
                        continue
                    st = rtime[i - lo]
                    if st < fa:
                        st = fa
                    if best is None or (st, i) < best[0]:
                        best = ((st, i), s, i)
            (st, _), s, i = best
            o = ops[i]
            if o.is_dma:
                free_at[s] = st + 0.07
            else:
                free_at[s] = st + o.cost
            fin = st + o.cost
            if fin > tmax:
                tmax = fin
            ready[s].remove(i)
            order[s].append(o)
            remaining -= 1
            for u in users[i - lo]:
                k = u - lo
                lat = 0.0 if (ops[u].stream == s == "pe") else SEM_LAT
                if fin + lat > rtime[k]:
                    rtime[k] = fin + lat
                indeg[k] -= 1
                if indeg[k] == 0:
                    ready[ops[u].stream].append(u)
        self.sim_time.append(tmax)
        return order

    def _assign(self, order):
        ops = self.ops
        for s in self.STREAMS:
            base = len(self.order[s])
            for k, o in enumerate(order[s]):
                o.pos = base + k
        for s in self.STREAMS:
            for o in order[s]:
                latest = {}
                for d in o.deps:
                    p = ops[d]
                    if p.is_dma:
                        p.needed = True
                        continue
                    if p.fn is None:
                        continue
                    if p.stream == o.stream and p.stream in ("pe", "sp"):
                        continue
                    q = latest.get(p.stream)
                    if q is None or p.pos > q.pos:
                        latest[p.stream] = p
                for p in latest.values():
                    p.needed = True
        for s in self.STREAMS:
            for o in order[s]:
                if not o.needed:
                    continue
                if o.is_dma:
                    b = o.dbuf
                    if b.dsem is None or b.dcount + 16 > SEM_LIMIT:
                        b.dsem = self.new_sem("d")
                        b.dcount = 0
                    b.dcount += 16
                    o.signal = (b.dsem, b.dcount, 16)
                else:
                    c = self.cur[s]
                    if c[0] is None or c[1] + 1 > SEM_LIMIT:
                        c[0] = self.new_sem("e" + s)
                        c[1] = 0
                    c[1] += 1
                    o.signal = (c[0], c[1], 1)
            self.order[s].extend(order[s])

    def _emit_stream(self, stream, eng):
        ops = self.ops
        waited = self.waited[stream]
        lst = self.order[stream]
        for o in lst[self.cursor[stream]:]:
            need = {}
            for d in o.deps:
                p = ops[d]
                if p.signal is None:
                    continue
                sem, val, _ = p.signal
                k = id(sem)
                if waited.get(k, 0) >= val:
                    continue
                if k not in need or need[k][1] < val:
                    need[k] = (sem, val)
            for k, (sem, val) in need.items():
                eng.wait_ge(sem, val)
                waited[k] = val
            if o.fn is None:
                continue
            ins = o.fn(eng)
            if o.signal is not None:
                ins.then_inc(o.signal[0], o.signal[2])
        self.cursor[stream] = len(lst)

    def flush(self):
        order = self._schedule(self.flushed, len(self.ops))
        self.flushed = len(self.ops)
        self._assign(order)
        S = self
        with self.nc.Block() as block:
            @block.sync
            def _(e):
                S._emit_stream("sp", e)

            @block.tensor
            def _(e):
                S._emit_stream("pe", e)

            @block.scalar
            def _(e):
                S._emit_stream("act", e)

            @block.vector
            def _(e):
                S._emit_stream("dve", e)

            @block.gpsimd
            def _(e):
                S._emit_stream("pool", e)


class T:
    def __init__(self, t, name):
        self.t = t
        self.b = Buf(name)


class Ring:
    def __init__(self, tiles):
        self.tiles = tiles
        self.i = 0

    def next(self):
        t = self.tiles[self.i % len(self.tiles)]
        self.i += 1
        return t


def _fs(ap):
    try:
        return float(ap.free_size())
    except Exception:
        return 256.0


def f_mm(out, lhsT, rhs, start, stop):
    n = _fs(out)
    c = max(n, 64.0) / 2400.0 + 0.012
    if lhsT.dtype == F32:
        c *= 4.0
    return Fn(lambda e: e.matmul(out, lhsT=lhsT, rhs=rhs, start=start, stop=stop), c)


def f_tr(out, in_, ident):
    return Fn(lambda e: e.transpose(out=out, in_=in_, identity=ident), 0.09)


def f_act(out, in_, func, bias=None, scale=None, accum=None):
    kw = {}
    if bias is not None:
        kw["bias"] = bias
    if scale is not None:
        kw["scale"] = scale
    if accum is not None:
        kw["accum_out"] = accum
    return Fn(lambda e: e.activation(out=out, in_=in_, func=func, **kw), 0.2 + _fs(in_) / 1400.0)


def f_tt(out, in0, in1, op):
    return Fn(lambda e: e.tensor_tensor(out=out, in0=in0, in1=in1, op=op), 0.07 + _fs(out) / 960.0)


def f_ts(out, in0, s1, s2, op0, op1=None):
    c = 0.07 + _fs(out) / 960.0
    if op1 is None:
        return Fn(lambda e: e.tensor_scalar(out=out, in0=in0, scalar1=s1, scalar2=None, op0=op0), c)
    return Fn(lambda e: e.tensor_scalar(out=out, in0=in0, scalar1=s1, scalar2=s2, op0=op0, op1=op1), c)


def f_stt(out, in0, scalar, in1, op0, op1):
    return Fn(lambda e: e.scalar_tensor_tensor(out=out, in0=in0, scalar=scalar, in1=in1, op0=op0, op1=op1),
              0.07 + _fs(out) / 960.0)


def f_cp(out, in_):
    return Fn(lambda e: e.tensor_copy(out=out, in_=in_), 0.07 + _fs(out) / 960.0)


def f_rcp(out, in_):
    return Fn(lambda e: e.reciprocal(out=out, in_=in_), 0.07 + _fs(out) / 960.0)


def f_dma(out, in_):
    try:
        nb = float(out.nbytes())
    except Exception:
        nb = 65536.0
    return Fn(lambda e: e.dma_start(out=out, in_=in_), 2.0 + nb / 150e3)


def f_memset(ap, val):
    return Fn(lambda e: e.memset(ap, val), 0.07 + _fs(ap) / 960.0)


def build_program(stop=99, dbg=False, p2_blocks=99, p2_conv=True, p2_sub=99):
    nc = bass.Bass("TRN2", target_bir_lowering=False)

    def di(name, shape):
        return nc.dram_tensor(name, shape, F32, kind="ExternalInput").ap()

    x_seq = di("x_seq", [SEQ, D])
    ctx_seq = di("ctx_seq", [CTX, D])
    c_fm = di("c_fm", [128, 8])
    cctx_fm = di("cctx_fm", [128, 8])
    w_mod = di("w_mod", [D, 6 * D])
    b_mod = di("b_mod", [6 * D])
    norm1_g = di("norm1_g", [D])
    w_in = di("w_in", [D, DIN])
    cw_fm = di("cw_fm", [128, 4, 31])
    cvec_fm = di("cvec_fm", [128, 3, 4])
    wgF = di("wgF_aug", [33, 256])
    wgB = di("wgB_aug", [33, 256])
    gla_g = di("gla_norm_fm", [128, 4])
    w_out = di("w_out", [D, D])
    norm2_g = di("norm2_g", [D])
    w_up = di("w_up", [D, 2 * HID])
    fdw_fm = di("fdw_fm", [128, NH, 4])
    w_down = di("w_down", [HID, D])
    final_g = di("final_g", [D])
    ident_in = di("ident", [128, 128])
    masks_in = di("masks", [128, 4, 128])
    ind_in = di("ind", [128, 2])
    y_out = nc.dram_tensor("y_out", [SEQ // 2, D], F32, kind="ExternalOutput").ap()
    x1s = nc.dram_tensor("x1_scratch", [EXT * 128, D], F32, kind="ExternalOutput" if dbg else "Internal").ap()
    if dbg:
        d_mod = nc.dram_tensor("d_mod", [4, 128, D], F32, kind="ExternalOutput").ap()
        d_st = nc.dram_tensor("d_st", [4, 128, 256], F32, kind="ExternalOutput").ap()
        d_gcol = nc.dram_tensor("d_gcol", [128, 2 * (15 + 2 * HALO_T) * 64], BF16, kind="ExternalOutput").ap()
        d_sbp = nc.dram_tensor("d_sbp", [128, 2 * EXT * 256], BF16, kind="ExternalOutput").ap()

    modsave = nc.dram_tensor("mod_scratch", [4, D], F32, kind="Internal").ap()
    sbp_dram = nc.dram_tensor("sbp_scratch", [2 * EXT, 128, 256], BF16, kind="Internal").ap()
    w_up_bf = nc.dram_tensor("w_up_bf16", [D, 2 * HID], BF16, kind="Internal").ap()
    kt_d = nc.dram_tensor("kt_scratch", [EXT, 128, 256], F32, kind="Internal").ap()
    v_d = nc.dram_tensor("v_scratch", [EXT, 128, 512], BF16, kind="Internal").ap()
    z_d = nc.dram_tensor("z_scratch", [EXT, 32, 128], BF16, kind="Internal").ap()
    w_dn_bf = nc.dram_tensor("w_dn_bf16", [HID, D], BF16, kind="Internal").ap()

    with ExitStack() as top:
        S = Sched(nc, top)

        def sb(stack, name, shape, dt):
            return T(stack.enter_context(nc.sbuf_tensor("sb_" + name, shape, dt)), name)

        def ps(stack, name, shape, dt):
            t_ = T(stack.enter_context(nc.psum_tensor("ps_" + name, shape, dt)), name)
            t_.b.psum = True
            return t_

        ident_f = sb(top, "ident_f", [128, 128], F32)
        ident_b = sb(top, "ident_b", [128, 128], BF16)
        masks = sb(top, "masks", [128, 4, 128], F32)
        ind = sb(top, "ind", [128, 2], F32)
        masks_b = sb(top, "masks_b", [128, 4, 128], BF16)
        ind_b = sb(top, "ind_b", [128, 2], BF16)
        ones_f = sb(top, "ones_f", [128, 128], F32)
        onesM = sb(top, "onesM", [128, 128], BF16)
        junk = sb(top, "junk", [128, 1024], BF16)
        M_LI, M_UI, M_LS, M_US = 0, 1, 2, 3

        pT = ps(top, "pT", [128, 512], F32)
        pbank = [ps(top, f"p{i}", [128, 512], F32) for i in range(1, 8)]
        p1, p2, p3, p4, p5, p6, p7 = pbank
        p3r = p3.b

        S.op("sp", f_dma(ident_f.t[:], ident_in[:, :]), writes=[ident_f.b], dma=ident_f.b)
        S.op("sp", f_dma(masks.t[:], masks_in[:, :, :]), writes=[masks.b], dma=masks.b)
        S.op("sp", f_dma(ind.t[:], ind_in[:, :]), writes=[ind.b], dma=ind.b)
        S.op("dve", f_cp(ident_b.t[:], ident_f.t[:]), reads=[ident_f.b], writes=[ident_b.b])
        S.op("dve", f_cp(masks_b.t[:], masks.t[:]), reads=[masks.b], writes=[masks_b.b])
        S.op("dve", f_cp(ind_b.t[:], ind.t[:]), reads=[ind.b], writes=[ind_b.b])
        S.op("dve", f_memset(ones_f.t[:], 1.0), writes=[ones_f.b])
        S.op("dve", f_memset(onesM.t[:], 1.0 / 512.0), writes=[onesM.b])

        def front(fr, rows_ap, n, sbt_, bbt_, dst=None):
            xt = fr["x"].next()
            st = fr["st"].next()
            hm1 = fr["hm1"].next()
            hm = fr["hm"].next()
            hT = fr["hT"].next() if dst is None else dst[0]
            hT_ap = hT.t[:, :, 0:n] if dst is None else dst[1]
            fr["last_x"] = S.op("sp", f_dma(xt.t[0:n, :], rows_ap), writes=[xt.b], dma=xt.b)
            S.op("act", f_act(hm1.t[0:n, :], xt.t[0:n, :], AF.Square, accum=st.t[0:n, 0:1]),
                 reads=[xt.b], writes=[st.b, hm1.b])
            S.op("act", f_act(st.t[0:n, 1:2], st.t[0:n, 0:1], AF.Ln, bias=EPS, scale=1.0 / D),
                 reads=[st.b], writes=[st.b])
            S.op("act", f_act(st.t[0:n, 2:3], st.t[0:n, 1:2], AF.Exp, scale=-0.5),
                 reads=[st.b], writes=[st.b])
            S.op("dve", f_stt(hm1.t[0:n, :], xt.t[0:n, :], st.t[0:n, 2:3], sbt_.t[0:n, :], ALU.mult, ALU.mult),
                 reads=[xt.b, st.b, sbt_.b], writes=[hm1.b])
            S.op(CFG["add_eng"], f_tt(hm.t[0:n, :], hm1.t[0:n, :], bbt_.t[0:n, :], ALU.add),
                 reads=[hm1.b, bbt_.b], writes=[hm.b])
            pv = pT.t[:, :].rearrange("p (a b) -> p a b", a=4)
            for half in range(2):
                for q in range(4):
                    kc = half * 4 + q
                    S.op("pe", f_mm(pT.t[:, q * 128:q * 128 + n], hm.t[0:n, kc * 128:(kc + 1) * 128],
                                    ident_b.t[0:n, 0:n], True, True),
                         reads=[hm.b, ident_b.b], writes=[pT.b])
                if CFG["hT_dve"] and half == 1:
                    S.op("dve", f_cp(hT_ap[:, half * 4:half * 4 + 4, :], pv[:, :, 0:n]), reads=[pT.b], writes=[hT.b])
                else:
                    S.op("act", f_act(hT_ap[:, half * 4:half * 4 + 4, :], pv[:, :, 0:n], AF.Copy), reads=[pT.b], writes=[hT.b])
            return hT

        def act_sigmoid(dst, src, dst_b, src_b):
            S.op("act", f_act(dst, src, AF.Exp, scale=-1.0), reads=[src_b], writes=[dst_b])
            S.op("act", f_act(dst, dst, AF.Ln, bias=1.0), reads=[dst_b], writes=[dst_b])
            S.op("act", f_act(dst, dst, AF.Exp, scale=-1.0), reads=[dst_b], writes=[dst_b])

        def make_front(stack, pre, nx=2, nh=2, nm=1):
            return {
                "x": Ring([sb(stack, f"{pre}x{i}", [128, D], F32) for i in range(nx)]),
                "st": Ring([sb(stack, f"{pre}st{i}", [128, 4], F32) for i in range(4)]),
                "hm1": Ring([sb(stack, f"{pre}hm1_{i}", [128, D], BF16) for i in range(nm)]),
                "hm": Ring([sb(stack, f"{pre}hm_{i}", [128, D], BF16) for i in range(nm)]),
                "hT": Ring([sb(stack, f"{pre}hT{i}", [128, 8, 128], BF16) for i in range(nh)]),
            }

        with ExitStack() as mix:
            w_in_sb = sb(mix, "w_in_sb", [128, 8, DIN], BF16)
            w_out_sb = sb(mix, "w_out_sb", [128, 8, D], BF16)
            s1b = sb(mix, "s1b", [128, D], F32)
            b1b = sb(mix, "b1b", [128, D], F32)
            cw = sb(mix, "cw", [128, 4, 31], F32)
            cvec = sb(mix, "cvec", [128, 3, 4], F32)
            wgF_sb = sb(mix, "wgF_sb", [33, 256], BF16)
            wgB_sb = sb(mix, "wgB_sb", [33, 256], BF16)
            gng_sb = sb(mix, "gng_sb", [128, 4], F32)
            SF = sb(mix, "SF", [128, 2, 128], F32)
            SBs = sb(mix, "SBs", [128, 2, 128], F32)
            zTa_r = Ring([sb(mix, f"zTa{i}", [33, 128], BF16) for i in range(2)])
            zcur = {"t": zTa_r.tiles[0]}

            w_in_v = w_in.rearrange("(kc p) n -> p kc n", p=128)
            wgrp = Buf("wgrp")
            crit_cols = [(C_K, C_K + 768), (C_Z, C_Z + 32), (C_CU + 256, C_CU + 512), (C_CG + 256, C_CG + 512)]
            rest_cols = [(C_CU, C_CU + 256), (C_CG, C_CG + 256), (C_Q, C_Q + 256), (C_OG, C_OG + 512)]
            w_all_ops = [S.op("pool", f_dma(w_in_sb.t[:, :, a:b_], w_in_v[:, :, a:b_]), dma=wgrp) for (a, b_) in crit_cols]
            w_out_v = w_out.rearrange("(kc p) n -> p kc n", p=128)
            w_out_regs = [Buf(f"wout{kc}") for kc in range(8)]
            S.op("sp", f_dma(cw.t[:], cw_fm[:, :, :]), writes=[cw.b], dma=cw.b)
            S.op("sp", f_dma(cvec.t[:], cvec_fm[:, :, :]), writes=[cvec.b], dma=cvec.b)
            S.op("pool", f_dma(wgF_sb.t[:], wgF[:, :]), writes=[wgF_sb.b], dma=wgF_sb.b)
            S.op("pool", f_dma(wgB_sb.t[:], wgB[:, :]), writes=[wgB_sb.b], dma=wgB_sb.b)
            S.op("sp", f_dma(gng_sb.t[:], gla_g[:, :]), writes=[gng_sb.b], dma=gng_sb.b)
            for zt in zTa_r.tiles:
                S.op("dve", f_memset(zt.t[32:33, :], 1.0), writes=[zt.b])
            SF.regs = [Buf(f"SF{i}") for i in range(4)]
            SBs.regs = [Buf(f"SB{i}") for i in range(4)]
            S.op("dve", f_memset(SF.t[:], 0.0), writes=SF.regs)
            S.op("dve", f_memset(SBs.t[:], 0.0), writes=SBs.regs)

            def gate(X, g):
                wg = wgF_sb if X == "F" else wgB_sb
                gn = g["gn" + X]
                zTa = zcur["t"]
                S.op("pe", f_mm(p6.t[:, 0:256], zTa.t[0:33, :], wg.t[0:33, :], True, True),
                     reads=[zTa.b, wg.b], writes=[p6.b])
                S.op("act", f_act(g["eg"].t[:], p6.t[:, 0:256], AF.Exp, scale=-1.0), reads=[p6.b], writes=[g["eg"].b])
                S.op("act", f_act(gn.t[:], g["eg"].t[:], AF.Ln, bias=1.0), reads=[g["eg"].b], writes=[gn.b])
                return gn

            def state_stage(X, g, gn, ktok, vbf, save_chunks):
                S_ = SF if X == "F" else SBs
                ms = M_US if X == "F" else M_LS
                Et, ktail, dec = g["Et"], g["ktail"], g["dec"]
                S.op("pe", f_mm(p6.t[:, 256:512], masks_b.t[:, ms, :], gn.t[:], True, True),
                     reads=[masks_b.b, gn.b], writes=[p6.b])
                for j in range(2):
                    S.op("pe", f_mm(p7.t[:, 2 * j:2 * j + 2], gn.t[:, j * 128:(j + 1) * 128], ind_b.t[:], True, True),
                         reads=[gn.b, ind_b.b], writes=[p7.b])
                S.op("act", f_act(Et.t[:], p6.t[:, 256:512], AF.Exp, scale=-1.0 / 16), reads=[p6.b], writes=[Et.b])
                S.op("act", f_act(dec.t[:], p7.t[:, 0:4], AF.Exp, scale=-1.0 / 16), reads=[p7.b], writes=[dec.b])
                S.op("dve", f_tt(ktail.t[:], ktok.t[:], Et.t[:], ALU.mult), reads=[ktok.b, Et.b], writes=[ktail.b])
                order = (0, 1) if X == "F" else (1, 0)
                kvp = {0: (p2 if CFG["kv_p2"] else p3), 1: p5}
                for lc in order:
                    if save_chunks is not None:
                        spt = sps.next()
                        S.op("act", f_act(spt.t[:], S_.t[:], AF.Copy), reads=S_.regs, writes=[spt.b])
                        S.op("sp", f_dma(sbp_dram[save_chunks[lc]], spt.t[:].rearrange("p a b -> p (a b)")),
                             reads=[spt.b], dma=spt.b)
                    pk = kvp[lc]
                    for j in range(2):
                        S.op("pe", f_mm(pk.t[:, j * 256:(j + 1) * 256],
                                        ktail.t[lc * 64:(lc + 1) * 64, j * 128:(j + 1) * 128],
                                        vbf.t[lc * 64:(lc + 1) * 64, j * 256:(j + 1) * 256], True, True),
                             reads=[ktail.b, vbf.b], writes=[pk.b])
                    for j in range(2):
                        for e_ in range(2):
                            sl = slice(e_ * 64, (e_ + 1) * 64)
                            S.op("dve", f_stt(S_.t[sl, j, :], S_.t[sl, j, :], dec.t[sl, 2 * j + lc:2 * j + lc + 1],
                                              pk.t[sl, j * 256 + e_ * 128:j * 256 + (e_ + 1) * 128],
                                              ALU.mult, ALU.add),
                                 reads=[S_.regs[2 * j + e_], dec.b, pk.b], writes=[S_.regs[2 * j + e_]])

            def make_gate_tiles(stack, pre):
                d_ = {
                    "gnF": sb(stack, pre + "gnF", [128, 256], BF16),
                    "gnB": sb(stack, pre + "gnB", [128, 256], BF16),
                    "Et": sb(stack, pre + "Et", [128, 256], F32),
                    "ktail": sb(stack, pre + "ktail", [128, 256], BF16),
                    "dec": sb(stack, pre + "dec", [128, 4], F32),
                    "ktok": sb(stack, pre + "ktok", [128, 256], F32),
                    "vbf": sb(stack, pre + "vbf", [128, 512], BF16),
                }
                d_["eg"] = d_["Et"]
                return d_

            def kvz_proj(hT, g, save_t=None):
                for kc in range(8):
                    S.op("pe", f_mm(p3.t[:, 0:256], hT.t[:, kc, :], w_in_sb.t[:, kc, C_K:C_K + 256], kc == 0, kc == 7),
                         reads=[hT.b], extra=w_all_ops, writes=[p3.b])
                for kc in range(8):
                    S.op("pe", f_mm(p4.t[:, :], hT.t[:, kc, :], w_in_sb.t[:, kc, C_V:C_V + 512], kc == 0, kc == 7),
                         reads=[hT.b], extra=w_all_ops, writes=[p4.b])
                for kc in range(8):
                    S.op("pe", f_mm(p3.t[0:32, 256:384], w_in_sb.t[:, kc, C_Z:C_Z + 32], hT.t[:, kc, :], kc == 0, kc == 7),
                         reads=[hT.b], extra=w_all_ops, writes=[p3r])
                S.op("act", f_act(g["ktok"].t[:], p3.t[:, 0:256], AF.Copy), reads=[p3.b], writes=[g["ktok"].b])
                S.op("dve", f_cp(g["vbf"].t[:], p4.t[:, :]), reads=[p4.b], writes=[g["vbf"].b])
                zTa = zTa_r.next()
                zcur["t"] = zTa
                S.op("act", f_act(zTa.t[0:32, :], p3.t[0:32, 256:384], AF.Copy), reads=[p3r], writes=[zTa.b])
                if save_t is not None:
                    S.op("sp", f_dma(kt_d[save_t], g["ktok"].t[:]), reads=[g["ktok"].b], dma=g["ktok"].b)
                    S.op("sp", f_dma(v_d[save_t], g["vbf"].t[:]), reads=[g["vbf"].b], dma=g["vbf"].b)
                    S.op("sp", f_dma(z_d[save_t], zTa.t[0:32, :]), reads=[zTa.b], dma=zTa.b)

            def adaln(ph, pre, chunks, with_ctx, dst, banks):
                nv = 16 if with_ctx else 8
                c_sb = sb(ph, pre + "c_sb", [128, nv], F32)
                e_c = sb(ph, pre + "e_c", [128, nv], F32)
                screp = sb(ph, pre + "screp", [128, nv, 128], F32)
                wm = Ring([sb(ph, f"{pre}wm{i}", [128, 8, 256], F32) for i in range(2)])
                bm = Ring([sb(ph, f"{pre}bm{i}", [128, 256], F32) for i in range(2)])
                ngb = Ring([sb(ph, f"{pre}ngb{i}", [128, 256], F32) for i in range(2)])
                tmpm = Ring([sb(ph, f"{pre}tmpm{i}", [128, 256], F32) for i in range(2)])
                S.op("sp", f_dma(c_sb.t[:, 0:8], c_fm[:, :]), writes=[c_sb.b], dma=Buf(pre + "c0"))
                if with_ctx:
                    S.op("sp", f_dma(c_sb.t[:, 8:16], cctx_fm[:, :]), writes=[c_sb.b], dma=Buf(pre + "c1"))
                S.op("act", f_act(e_c.t[:], c_sb.t[:], AF.Exp, scale=-1.0), reads=[c_sb.b], writes=[e_c.b])
                S.op("dve", f_ts(e_c.t[:], e_c.t[:], 1.0, None, ALU.add), reads=[e_c.b], writes=[e_c.b])
                S.op("dve", f_rcp(e_c.t[:], e_c.t[:]), reads=[e_c.b], writes=[e_c.b])
                S.op("dve", f_tt(c_sb.t[:], c_sb.t[:], e_c.t[:], ALU.mult), reads=[c_sb.b, e_c.b], writes=[c_sb.b])
                for i in range(nv):
                    S.op("dve", f_ts(screp.t[:, i, :], ones_f.t[:], c_sb.t[:, i:i + 1], None, ALU.mult),
                         reads=[ones_f.b, c_sb.b], writes=[screp.b])
                w_mod_v = w_mod.rearrange("(kc p) n -> p kc n", p=128)
                for ci, nci in enumerate(chunks):
                    n0 = nci * 256
                    wmt = wm.next()
                    bmt = bm.next()
                    q_ = ("sp", "act")[ci % 2] if with_ctx else "sp"
                    S.op(q_, f_dma(wmt.t[:, :, :], w_mod_v[:, :, n0:n0 + 256]), writes=[wmt.b], dma=wmt.b)
                    S.op("sp", f_dma(bmt.t[:], b_mod[n0:n0 + 256].partition_broadcast(128)), writes=[bmt.b], dma=bmt.b)
                    which = nci // 4
                    c0_ = (nci % 4) * 256
                    half = slice(c0_, c0_ + 256)
                    if which in (1, 4):
                        ngt = ngb.next()
                        gsrc = norm1_g if which == 1 else norm2_g
                        S.op("sp", f_dma(ngt.t[:], gsrc[c0_:c0_ + 256].partition_broadcast(128)), writes=[ngt.b], dma=ngt.b)
                    variants = [(0, banks[0])] + ([(8, banks[1])] if (which < 2 and with_ctx) else [])
                    for off, pp_ in variants:
                        pp = T(pp_.t[:, 0:256], pp_.b.name)
                        pp.b = pp_.b
                        for kc in range(8):
                            S.op("pe", f_mm(pp.t, screp.t[:, off + kc, :], wmt.t[:, kc, :], kc == 0, kc == 7),
                                 reads=[screp.b, wmt.b], writes=[pp.b])
                        key = (which, off)
                        if which in (0, 2):
                            d_ = dst[key]
                            S.op("dve", f_tt(d_.t[:, half], pp.t, bmt.t[:], ALU.add), reads=[pp.b, bmt.b], writes=[d_.b])
                        elif which == 1:
                            d_ = dst[key]
                            tm = tmpm.next()
                            S.op("dve", f_tt(tm.t[:], pp.t, bmt.t[:], ALU.add), reads=[pp.b, bmt.b], writes=[tm.b])
                            S.op("dve", f_stt(d_.t[:, half], tm.t[:], 1.0, ngt.t[:], ALU.add, ALU.mult),
                                 reads=[tm.b, ngt.b], writes=[d_.b])
                        else:
                            row = {3: 1, 4: 0, 5: 2}[which]
                            tm = tmpm.next()
                            S.op("dve", f_tt(tm.t[:], pp.t, bmt.t[:], ALU.add), reads=[pp.b, bmt.b], writes=[tm.b])
                            if which == 4:
                                S.op("dve", f_stt(tm.t[:], tm.t[:], 1.0, ngt.t[:], ALU.add, ALU.mult),
                                     reads=[tm.b, ngt.b], writes=[tm.b])
                            S.op("sp", f_dma(modsave[row:row + 1, c0_:c0_ + 256], tm.t[0:1, :]), reads=[tm.b], dma=tm.b)

            with ExitStack() as ph:
                s1c = sb(ph, "s1c", [128, D], F32)
                b1c = sb(ph, "b1c", [128, D], F32)
                fr0 = make_front(ph, "f0", nx=2, nh=2)
                g0 = {"F": make_gate_tiles(ph, "g0F"), "B": make_gate_tiles(ph, "g0B")}
                adaln(ph, "a0", list(range(8)), True,
                      {(0, 0): b1b, (0, 8): b1c, (1, 0): s1b, (1, 8): s1c}, (p1, p2))

                for X, tiles in (("B", (1, 0)), ("F", (0, 1))):
                    for t in tiles:
                        hT = front(fr0, ctx_seq[t * 128:(t + 1) * 128, :], 128, s1c, b1c)
                        kvz_proj(hT, g0[X])
                        gn = gate(X, g0[X])
                        state_stage(X, g0[X], gn, g0[X]["ktok"], g0[X]["vbf"], None)
                if dbg:
                    S.op("sp", f_dma(d_mod[0], s1b.t[:]), reads=[s1b.b], dma=Buf("dd0"))
                    S.op("sp", f_dma(d_mod[1], b1b.t[:]), reads=[b1b.b], dma=Buf("dd1"))
                    S.op("sp", f_dma(d_mod[2], s1c.t[:]), reads=[s1c.b], dma=Buf("dd2"))
                    S.op("sp", f_dma(d_mod[3], b1c.t[:]), reads=[b1c.b], dma=Buf("dd3"))
                    S.op("sp", f_dma(d_st[0], SF.t[:].rearrange("p a b -> p (a b)")), reads=SF.regs, dma=Buf("dd4"))
                    S.op("sp", f_dma(d_st[1], SBs.t[:].rearrange("p a b -> p (a b)")), reads=SBs.regs, dma=Buf("dd5"))
                S.barrier()
                S.flush()
            if stop <= 0:
                return nc

            sps = Ring([sb(mix, f"sps{i}", [128, 2, 128], BF16) for i in range(2)])
            gcol = sb(mix, "gcol", [128, 2, (15 + 2 * HALO_T) * 64], BF16)
            diagT = sb(mix, "diagT", [128, 4 * 31, 128], BF16)
            S.op("pool", f_memset(gcol.t[:, :, 0:15 * 64], 0.0), writes=[gcol.b])
            for c in range(4):
                for k in range(31):
                    S.op("dve", f_ts(diagT.t[:, c * 31 + k, :], ident_b.t[:], cw.t[:, c, k:k + 1], None, ALU.mult),
                         reads=[ident_b.b, cw.b], writes=[diagT.b])

            with ExitStack() as ph:
                pcg = Buf("precast")
                bg = []

                def _bg_dma(out_ap, in_ap, grp, lst=None):
                    def go(dep):
                        i = S.op("pool", f_dma(out_ap, in_ap), dma=grp, extra=[dep])
                        if lst is not None:
                            lst.append(i)
                    return go
                g1b = sb(ph, "g1b", [128, D], F32)
                adaln(ph, "a1", list(range(8, 24)), False, {(2, 0): g1b}, (p7,))
                wgrp1 = Buf("wgrp1")
                w_rest_ops = []
                for kc in (0, 4):
                    bg.append(_bg_dma(w_out_sb.t[:, kc:kc + 4, :], w_out_v[:, kc:kc + 4, :], wgrp1, w_rest_ops))
                for (a, b_) in rest_cols:
                    bg.append(_bg_dma(w_in_sb.t[:, :, a:b_], w_in_v[:, :, a:b_], wgrp1, w_rest_ops))
                for kc in range(8):
                    bg.append(_bg_dma(w_up_bf[kc * 128:(kc + 1) * 128, :], w_up[kc * 128:(kc + 1) * 128, :], pcg))
                for j0 in range(0, NH, 2):
                    bg.append(_bg_dma(w_dn_bf[j0 * 128:(j0 + 2) * 128, :], w_down[j0 * 128:(j0 + 2) * 128, :], pcg))

                def fold_w_out():
                    for kc in range(8):
                        if kc < 4:
                            S.op("dve", f_tt(w_out_sb.t[:, kc, :], w_out_sb.t[:, kc, :], g1b.t[:], ALU.mult),
                                 reads=[g1b.b], writes=[w_out_regs[kc]], extra=w_rest_ops)
                        else:
                            S.op("dve", f_stt(w_out_sb.t[:, kc, :], w_out_sb.t[:, kc, :], gng_sb.t[:, kc - 4:kc - 3], g1b.t[:],
                                              ALU.mult, ALU.mult),
                                 reads=[g1b.b, gng_sb.b], writes=[w_out_regs[kc]], extra=w_rest_ops)
                fold_done = []
                fr1 = make_front(ph, "f1", nx=3, nh=2, nm=CFG["p1_hm"])
                g1r = Ring([make_gate_tiles(ph, f"g1{i}") for i in range(CFG["p1_gate"])])
                ecg = sb(ph, "ecg1", [128, 2, 128], F32)
                for t in range(NT - 1, -1, -1):
                    g1 = g1r.next()
                    hT = front(fr1, x_seq[t * 128:(t + 1) * 128, :], 128, s1b, b1b)
                    if bg and t % 2 == 0:
                        bg.pop(0)(fr1["last_x"])
                        if len(w_rest_ops) == 6 and not fold_done:
                            fold_w_out()
                            fold_done.append(1)
                    kvz_proj(hT, g1, save_t=t if t < EXT else None)
                    if t < HALO_T:
                        for m in range(4):
                            col = (C_CU + 256 + m * 128) if m < 2 else (C_CG + 256 + (m - 2) * 128)
                            for kc in range(8):
                                S.op("pe", f_mm(p1.t[:, m * 128:(m + 1) * 128], w_in_sb.t[:, kc, col:col + 128],
                                                hT.t[:, kc, :], kc == 0, kc == 7),
                                     reads=[hT.b], extra=w_all_ops, writes=[p1.b])
                        p1v = p1.t[:, :].rearrange("p (a b) -> p a b", a=4)
                        act_sigmoid(ecg.t[:], p1v[:, 2:4, :], ecg.b, p1.b)
                        pos = (15 + 2 * t) * 64
                        S.op("dve", f_tt(gcol.t[:, :, pos:pos + 128], p1v[:, 0:2, :], ecg.t[:], ALU.mult),
                             reads=[p1.b, ecg.b], writes=[gcol.b])
                    gn = gate("B", g1)
                    state_stage("B", g1, gn, g1["ktok"], g1["vbf"], (2 * t, 2 * t + 1) if t < EXT else None)
                while bg:
                    bg.pop(0)(fr1["last_x"])
                if not fold_done:
                    fold_w_out()
                if dbg:
                    S.op("sp", f_dma(d_st[2], SBs.t[:].rearrange("p a b -> p (a b)")), reads=SBs.regs, dma=Buf("dd6"))
                    S.op("sp", f_dma(d_gcol[:, :], gcol.t[:].rearrange("p a b -> p (a b)")), reads=[gcol.b], dma=Buf("dd7"))
                S.barrier()
                S.flush()
            if stop <= 1:
                return nc

            with ExitStack() as ph:
                fr2 = make_front(ph, "f2", nx=2, nh=2, nm=CFG["p2_hm"])
                g2r = Ring([make_gate_tiles(ph, f"g2{i}") for i in range(CFG["p2_gate"])])
                qk_r = Ring([sb(ph, f"qk_s{i}", [128, 4, 128], F32) for i in range(CFG["p2_qk"])])
                eog_r = Ring([sb(ph, f"eog{i}", [128, 512], F32) for i in range(CFG["p2_og"])])
                sog_r = Ring([sb(ph, f"sog{i}", [128, 512], F32) for i in range(CFG["p2_og"])])
                EEp = sb(ph, "EEp", [128, 2, 128], F32)
                EEn = sb(ph, "EEn", [128, 2, 128], F32)
                EE = {"Fp": EEp, "Bp": EEp, "Fn": EEn, "Bn": EEn}
                QK = {X + s: sb(ph, "QK" + X + s, [128, 2, 128], BF16) for X in "FB" for s in "qk"}
                scm = sb(ph, "scm", [128, 8, 128], BF16)
                SFbf = Ring([sb(ph, f"SFbf{i}", [128, 2, 128], BF16) for i in range(2)])
                sbl = Ring([sb(ph, f"sbl{i}", [128, 2, 256], BF16) for i in range(2)])
                ost = sb(ph, "ost", [128, 12], F32)
                o_g = sb(ph, "o_g", [128, 512], BF16)
                ecg2 = sb(ph, "ecg2", [128, 2, 128], F32)
                growp = Ring([sb(ph, f"growp{i}", [128, 2, 4, 94], BF16) for i in range(2)])
                mixT_r = Ring([sb(ph, f"mixT{i}", [128, 8, 256], BF16) for i in range(CFG["p2_mixT"])])
                y32 = sb(ph, "y32", [128, 4, 256], F32)
                yb = sb(ph, "yb", [128, 4, 256], BF16)
                ysq = sb(ph, "ysq", [128, 4, 256], BF16)
                lnm = sb(ph, "lnm", [128, 256], F32)
                lnv = sb(ph, "lnv", [128, 256], F32)
                lnr = sb(ph, "lnr", [128, 256], F32)
                ynt = Ring([sb(ph, f"ynt{i}", [128, 256], F32) for i in range(1)])
                eyn = Ring([sb(ph, f"eyn{i}", [128, 256], F32) for i in range(1)])
                xr = Ring([sb(ph, f"xr{i}", [128, D], F32) for i in range(CFG["p2_xr"])])
                for gt in growp.tiles:
                    S.op("pool", f_memset(gt.t[:], 0.0), writes=[gt.b])

                def mixer_tile(t, lt, grow, mixT):
                    g2 = g2r.next()
                    qk_s = qk_r.next()
                    eog = eog_r.next()
                    sog = sog_r.next()
                    sbt2 = sbl.next()
                    S.op("sp", f_dma(sbt2.t[:], sbp_dram[2 * t:2 * t + 2].rearrange("c p f -> p c f")),
                         writes=[sbt2.b], dma=sbt2.b)
                    hT = front(fr2, x_seq[t * 128:(t + 1) * 128, :], 128, s1b, b1b)
                    zTa = zTa_r.next()
                    zcur["t"] = zTa
                    S.op("sp", f_dma(g2["ktok"].t[:], kt_d[t]), writes=[g2["ktok"].b], dma=g2["ktok"].b)
                    S.op("sp", f_dma(g2["vbf"].t[:], v_d[t]), writes=[g2["vbf"].b], dma=g2["vbf"].b)
                    S.op("sp", f_dma(zTa.t[0:32, :], z_d[t]), writes=[zTa.b], dma=zTa.b)
                    if p2_sub <= 1:
                        return
                    for m in range(4):
                        col = (C_CU + m * 128) if m < 2 else (C_CG + (m - 2) * 128)
                        for kc in range(8):
                            S.op("pe", f_mm(p1.t[:, m * 128:(m + 1) * 128], w_in_sb.t[:, kc, col:col + 128],
                                            hT.t[:, kc, :], kc == 0, kc == 7),
                                 reads=[hT.b], extra=w_all_ops, writes=[p1.b])
                    for m in range(4):
                        col = C_Q + m * 128
                        for kc in range(8):
                            S.op("pe", f_mm(p2.t[:, m * 128:(m + 1) * 128], w_in_sb.t[:, kc, col:col + 128],
                                            hT.t[:, kc, :], kc == 0, kc == 7),
                                 reads=[hT.b], extra=w_all_ops, writes=[p2.b])
                    for kc in range(8):
                        S.op("pe", f_mm(p3.t[:, :], hT.t[:, kc, :], w_in_sb.t[:, kc, C_OG:C_OG + 512], kc == 0, kc == 7),
                             reads=[hT.b], extra=w_all_ops, writes=[p3.b])
                    if p2_sub <= 2:
                        return
                    p1v = p1.t[:, :].rearrange("p (a b) -> p a b", a=4)
                    act_sigmoid(ecg2.t[:], p1v[:, 2:4, :], ecg2.b, p1.b)
                    for c in range(2):
                        S.op("dve", f_tt(grow.t[:, c, 2 * lt:2 * lt + 2, 15:79],
                                         p1v[:, c, :].rearrange("p (r w) -> p r w", r=2),
                                         ecg2.t[:, c, :].rearrange("p (r w) -> p r w", r=2), ALU.mult),
                             reads=[p1.b, ecg2.b], writes=[grow.b])
                    if p2_sub <= 3:
                        return
                    S.op("act", f_act(qk_s.t[:], p2.t[:, :].rearrange("p (a b) -> p a b", a=4), AF.Copy),
                         reads=[p2.b], writes=[qk_s.b])
                    act_sigmoid(eog.t[:], p3.t[:, :], eog.b, p3.b)
                    S.op("dve", f_tt(sog.t[:], p3.t[:, :], eog.t[:], ALU.mult), reads=[p3.b, eog.b], writes=[sog.b])
                    if p2_sub <= 4:
                        return
                    gnF = gate("F", g2)
                    gnB = gate("B", g2)
                    Et, ktail, dec = g2["Et"], g2["ktail"], g2["dec"]
                    S.op("pe", f_mm(p6.t[:, 256:512], masks_b.t[:, M_US, :], gnF.t[:], True, True),
                         reads=[masks_b.b, gnF.b], writes=[p6.b])
                    for j in range(2):
                        S.op("pe", f_mm(p6.t[:, 2 * j:2 * j + 2], gnF.t[:, j * 128:(j + 1) * 128], ind_b.t[:], True, True),
                             reads=[gnF.b, ind_b.b], writes=[p6.b])
                    S.op("act", f_act(Et.t[:], p6.t[:, 256:512], AF.Exp, scale=-1.0 / 16), reads=[p6.b], writes=[Et.b])
                    S.op("act", f_act(dec.t[:], p6.t[:, 0:4], AF.Exp, scale=-1.0 / 16), reads=[p6.b], writes=[dec.b])
                    S.op("dve", f_tt(ktail.t[:], g2["ktok"].t[:], Et.t[:], ALU.mult),
                         reads=[g2["ktok"].b, Et.b], writes=[ktail.b])
                    for xi, (X, gn, mk) in enumerate((("F", gnF, M_LI), ("B", gnB, M_UI))):
                        for j in range(2):
                            S.op("pe", f_mm(p7.t[:, xi * 256 + j * 128:xi * 256 + (j + 1) * 128],
                                            gn.t[:, j * 128:(j + 1) * 128], masks_b.t[:, mk, :], True, True),
                                 reads=[gn.b, masks_b.b], writes=[p7.b])
                    for xi, X in enumerate("FB"):
                        src = p7.t[:, xi * 256:(xi + 1) * 256].rearrange("p (a b) -> p a b", a=2)
                        S.op("act", f_act(EE[X + "p"].t[:], src, AF.Exp, scale=-1.0 / 16, bias=float(np.log(0.125))),
                             reads=[p7.b], writes=[EE[X + "p"].b])
                        S.op("act", f_act(EE[X + "n"].t[:], src, AF.Exp, scale=1.0 / 16),
                             reads=[p7.b], writes=[EE[X + "n"].b])
                        S.op("dve", f_tt(QK[X + "q"].t[:], qk_s.t[:, 0:2, :], EE[X + "p"].t[:], ALU.mult),
                             reads=[qk_s.b, EE[X + "p"].b], writes=[QK[X + "q"].b])
                        S.op("dve", f_tt(QK[X + "k"].t[:], qk_s.t[:, 2:4, :], EE[X + "n"].t[:], ALU.mult),
                             reads=[qk_s.b, EE[X + "n"].b], writes=[QK[X + "k"].b])
                    if p2_sub <= 5:
                        return
                    for e_ in range(2):
                        pp = p6 if e_ == 0 else p7
                        sl = slice(e_ * 64, (e_ + 1) * 64)
                        for xi, X in enumerate("FB"):
                            for j in range(2):
                                slot = xi * 2 + j
                                S.op("pe", f_mm(pp.t[:, slot * 128:(slot + 1) * 128], QK[X + "k"].t[sl, j, :],
                                                QK[X + "q"].t[sl, j, :], True, True),
                                     reads=[QK[X + "k"].b, QK[X + "q"].b], writes=[pp.b])
                    for e_ in range(2):
                        pp = p6 if e_ == 0 else p7
                        for xi, X in enumerate("FB"):
                            mk = M_LI if X == "F" else M_UI
                            for j in range(2):
                                slot = xi * 2 + j
                                h = 2 * j + e_
                                S.op("dve", f_tt(scm.t[:, xi * 4 + h, :], pp.t[:, slot * 128:(slot + 1) * 128],
                                                 masks.t[:, mk, :], ALU.mult),
                                     reads=[pp.b, masks.b], writes=[scm.b])
                    if p2_sub <= 6:
                        return
                    S_prev_tiles = {}
                    vbf = g2["vbf"]
                    for lc in range(2):
                        sf = SFbf.next()
                        S.op("act", f_act(sf.t[:], SF.t[:], AF.Copy), reads=SF.regs, writes=[sf.b])
                        S_prev_tiles[lc] = sf
                        for j in range(2):
                            S.op("pe", f_mm(p4.t[:, j * 256:(j + 1) * 256],
                                            ktail.t[lc * 64:(lc + 1) * 64, j * 128:(j + 1) * 128],
                                            vbf.t[lc * 64:(lc + 1) * 64, j * 256:(j + 1) * 256], True, True),
                                 reads=[ktail.b, vbf.b], writes=[p4.b])
                        for j in range(2):
                            for e_ in range(2):
                                sl = slice(e_ * 64, (e_ + 1) * 64)
                                S.op("dve", f_stt(SF.t[sl, j, :], SF.t[sl, j, :], dec.t[sl, 2 * j + lc:2 * j + lc + 1],
                                                  p4.t[sl, j * 256 + e_ * 128:j * 256 + (e_ + 1) * 128],
                                                  ALU.mult, ALU.add),
                                     reads=[SF.regs[2 * j + e_], dec.b, p4.b], writes=[SF.regs[2 * j + e_]])
                    if p2_sub <= 7:
                        return
                    for h in range(4):
                        j, e_ = h // 2, h % 2
                        sl = slice(e_ * 64, (e_ + 1) * 64)
                        oc = slice(h * 128, (h + 1) * 128)
                        S.op("pe", f_mm(p5.t[:, oc], scm.t[:, h, :], vbf.t[:, oc], True, False),
                             reads=[scm.b, vbf.b], writes=[p5.b])
                        S.op("pe", f_mm(p5.t[:, oc], scm.t[:, 4 + h, :], vbf.t[:, oc], False, False),
                             reads=[scm.b, vbf.b], writes=[p5.b])
                        for lc in range(2):
                            tl = slice(lc * 64, (lc + 1) * 64)
                            S.op("pe", f_mm(p5.t[tl, oc], QK["Fq"].t[sl, j, tl], S_prev_tiles[lc].t[sl, j, :], False, False),
                                 reads=[QK["Fq"].b, S_prev_tiles[lc].b], writes=[p5.b])
                            S.op("pe", f_mm(p5.t[tl, oc], QK["Bq"].t[sl, j, tl], sbt2.t[sl, lc, j * 128:(j + 1) * 128], False, True),
                                 reads=[QK["Bq"].b, sbt2.b], writes=[p5.b])
                    if p2_sub <= 8:
                        return
                    for h in range(4):
                        S.op("act", f_act(junk.t[:, h * 128:(h + 1) * 128], p5.t[:, h * 128:(h + 1) * 128], AF.Square,
                                          accum=ost.t[:, h:h + 1]),
                             reads=[p5.b], writes=[ost.b, junk.b])
                    S.op("act", f_act(ost.t[:, 4:8], ost.t[:, 0:4], AF.Ln, bias=EPS, scale=1.0 / 128), reads=[ost.b], writes=[ost.b])
                    S.op("act", f_act(ost.t[:, 8:12], ost.t[:, 4:8], AF.Exp, scale=-0.5), reads=[ost.b], writes=[ost.b])
                    for h in range(4):
                        oc = slice(h * 128, (h + 1) * 128)
                        S.op("dve", f_stt(o_g.t[:, oc], p5.t[:, oc], ost.t[:, 8 + h:9 + h], sog.t[:, oc], ALU.mult, ALU.mult),
                             reads=[p5.b, ost.b, sog.b], writes=[o_g.b])
                    if p2_sub <= 9:
                        return
                    for h in range(4):
                        S.op("pe", f_mm(p7.t[:, h * 128:(h + 1) * 128], o_g.t[:, h * 128:(h + 1) * 128], ident_b.t[:, :], True, True),
                             reads=[o_g.b, ident_b.b], writes=[p7.b])
                    S.op("act", f_act(mixT.t[:, 4:8, lt * 128:(lt + 1) * 128],
                                      p7.t[:, 0:512].rearrange("p (a b) -> p a b", a=4), AF.Copy),
                         reads=[p7.b], writes=[mixT.b])

                def conv_block(t0, ntl, grow, mixT):
                    n = ntl * 128
                    nr = 2 * ntl
                    r0 = 2 * t0
                    cps = {0: p1, 1: p1, 2: p2, 3: p2}
                    for c in range(4):
                        pp = cps[c]
                        oc = slice((c % 2) * 256, (c % 2) * 256 + n)
                        for k in range(31):
                            if c < 2:
                                rhs = grow.t[:, c, 0:nr, k:k + 64]
                            else:
                                rhs = gcol.t[:, c - 2, (r0 + k) * 64:(r0 + k) * 64 + n]
                            S.op("pe", f_mm(pp.t[:, oc], diagT.t[:, c * 31 + k, :], rhs, k == 0, k == 30),
                                 reads=[diagT.b, grow.b if c < 2 else gcol.b], writes=[pp.b])
                    for c in range(4):
                        pp = cps[c]
                        oc = slice((c % 2) * 256, (c % 2) * 256 + n)
                        S.op("act", f_act(y32.t[:, c, 0:n], pp.t[:, oc], AF.Identity, bias=cvec.t[:, 0, c:c + 1]),
                             reads=[pp.b, cvec.b], writes=[y32.b])
                        S.op("act", f_act(ysq.t[:, c, 0:n], pp.t[:, oc], AF.Square, bias=cvec.t[:, 0, c:c + 1]),
                             reads=[pp.b, cvec.b], writes=[ysq.b])
                        S.op("dve", f_cp(yb.t[:, c, 0:n], y32.t[:, c, 0:n]), reads=[y32.b], writes=[yb.b])
                    for c in range(4):
                        S.op("pe", f_mm(p6.t[:, 0:n], onesM.t[:], yb.t[:, c, 0:n], c == 0, c == 3),
                             reads=[onesM.b, yb.b], writes=[p6.b])
                    for c in range(4):
                        S.op("pe", f_mm(p6.t[:, 256:256 + n], onesM.t[:], ysq.t[:, c, 0:n], c == 0, c == 3),
                             reads=[onesM.b, ysq.b], writes=[p6.b])
                    S.op("act", f_act(lnm.t[:, 0:n], p6.t[:, 0:n], AF.Copy), reads=[p6.b], writes=[lnm.b])
                    S.op("dve", f_tt(lnv.t[:, 0:n], lnm.t[:, 0:n], lnm.t[:, 0:n], ALU.mult), reads=[lnm.b], writes=[lnv.b])
                    S.op("dve", f_tt(lnv.t[:, 0:n], p6.t[:, 256:256 + n], lnv.t[:, 0:n], ALU.subtract),
                         reads=[p6.b, lnv.b], writes=[lnv.b])
                    S.op("act", f_act(lnr.t[:, 0:n], lnv.t[:, 0:n], AF.Ln, bias=EPS), reads=[lnv.b], writes=[lnr.b])
                    S.op("act", f_act(lnr.t[:, 0:n], lnr.t[:, 0:n], AF.Exp, scale=-0.5), reads=[lnr.b], writes=[lnr.b])
                    for c in range(4):
                        yn = ynt.next()
                        ey = eyn.next()
                        S.op("dve", f_tt(yn.t[:, 0:n], y32.t[:, c, 0:n], lnm.t[:, 0:n], ALU.subtract),
                             reads=[y32.b, lnm.b], writes=[yn.b])
                        S.op("dve", f_tt(yn.t[:, 0:n], yn.t[:, 0:n], lnr.t[:, 0:n], ALU.mult), reads=[yn.b, lnr.b], writes=[yn.b])
                        S.op("dve", f_ts(yn.t[:, 0:n], yn.t[:, 0:n], cvec.t[:, 1, c:c + 1], cvec.t[:, 2, c:c + 1], ALU.mult, ALU.add),
                             reads=[yn.b, cvec.b], writes=[yn.b])
                        act_sigmoid(ey.t[:, 0:n], yn.t[:, 0:n], ey.b, yn.b)
                        S.op("dve", f_tt(mixT.t[:, c, 0:n], yn.t[:, 0:n], ey.t[:, 0:n], ALU.mult),
                             reads=[yn.b, ey.b], writes=[mixT.b])
                    for lt in range(ntl):
                        t = t0 + lt
                        xrt = xr.next()
                        S.op("sp", f_dma(xrt.t[:], x_seq[t * 128:(t + 1) * 128, :]), writes=[xrt.b], dma=xrt.b)
                        for half, pp in ((0, p3), (1, p5)):
                            for kc in range(8):
                                S.op("pe", f_mm(pp.t[:, :], mixT.t[:, kc, lt * 128:(lt + 1) * 128],
                                                w_out_sb.t[:, kc, half * 512:(half + 1) * 512], kc == 0, kc == 7),
                                     reads=[mixT.b], writes=[pp.b])
                        for half, pp in ((0, p3), (1, p5)):
                            hs = slice(half * 512, (half + 1) * 512)
                            S.op("dve", f_tt(xrt.t[:, hs], pp.t[:, :], xrt.t[:, hs], ALU.add),
                                 reads=[pp.b, xrt.b], writes=[xrt.b])
                        S.op("sp", f_dma(x1s[t * 128:(t + 1) * 128, :], xrt.t[:]), reads=[xrt.b], dma=xrt.b)

                t = 0
                while t < EXT:
                    ntl = min(2, EXT - t)
                    grow = growp.next()
                    mixT = mixT_r.next()
                    for lt in range(ntl):
                        mixer_tile(t + lt, lt, grow, mixT)
                    if p2_conv:
                        conv_block(t, ntl, grow, mixT)
                    t += ntl
                    if t >= 2 * p2_blocks:
                        break
                if dbg:
                    S.op("sp", f_dma(d_st[3], SF.t[:].rearrange("p a b -> p (a b)")), reads=SF.regs, dma=Buf("dd9"))
                S.barrier()
                S.flush()
        if stop <= 2:
            return nc

        with ExitStack() as ffn:
            w_up_sb = sb(ffn, "w_up_sb", [128, 8, 2 * HID], BF16)
            w_dn_sb = sb(ffn, "w_dn_sb", [128, NH, D], BF16)
            g2b = sb(ffn, "g2b", [128, D], F32)
            w_up_v = w_up_bf.rearrange("(kc p) n -> p kc n", p=128)
            NG = 8
            gsz = [3, 3, 3, 3, 3, 3, 2, 2]
            gst = [sum(gsz[:g]) for g in range(NG)]
            w_up_grp = {}
            for g in range(NG):
                gb = Buf(f"wupg{g}")
                ops_ = []
                for hh in range(2):
                    c0_ = hh * HID + gst[g] * 128
                    c1_ = c0_ + gsz[g] * 128
                    ops_.append(S.op(("sp", "act")[g % 2], f_dma(w_up_sb.t[:, :, c0_:c1_], w_up_v[:, :, c0_:c1_]), dma=gb))
                for jj in range(gst[g], gst[g] + gsz[g]):
                    w_up_grp[jj] = ops_
            w_dn_v = w_dn_bf.rearrange("(j p) n -> p j n", p=128)
            S.op("sp", f_dma(g2b.t[:], modsave[2, :].partition_broadcast(128)), writes=[g2b.b], dma=g2b.b)
            w_dn_regs = {}
            for j0 in range(0, NH, 2):
                rb = Buf(f"wdn{j0}")
                S.op(("sp", "act")[(j0 // 2) % 2], f_dma(w_dn_sb.t[:, j0:j0 + 2, :], w_dn_v[:, j0:j0 + 2, :]), writes=[rb], dma=rb)
                for j in (j0, j0 + 1):
                    S.op("dve", f_tt(w_dn_sb.t[:, j, :], w_dn_sb.t[:, j, :], g2b.t[:], ALU.mult),
                         reads=[rb, g2b.b], writes=[rb])
                    w_dn_regs[j] = rb

            s2b = sb(ffn, "s2b", [128, D], F32)
            b2b = sb(ffn, "b2b", [128, D], F32)
            fgb = sb(ffn, "fgb", [128, D], F32)
            fdw = sb(ffn, "fdw", [128, NH, 4], F32)
            fr3 = make_front(ffn, "f3", nx=CFG["f3_nx"], nh=1)
            h2Tr = Ring([sb(ffn, f"h2T{i}", [128, 8, FB], BF16) for i in range(CFG["h2T"])])
            abuf = Ring([sb(ffn, f"abuf{i}", [128, FB + 2], F32) for i in range(CFG["ffn_ring"])])
            vbuf = Ring([sb(ffn, f"vbuf{i}", [128, FB + 1], F32) for i in range(CFG["ffn_ring"])])
            tcv = Ring([sb(ffn, f"tcv{i}", [128, FB], F32) for i in range(CFG["ffn_ring"])])
            esg = Ring([sb(ffn, f"esg{i}", [128, FB], F32) for i in range(CFG["ffn_ring"])])
            car = sb(ffn, "car", [128, NH, 3], F32)
            hidT = sb(ffn, "hidT", [128, NH, FB], BF16)
            hidT.regs = [Buf(f"hid{j}") for j in range(NH)]
            xrf = Ring([sb(ffn, f"xrf{i}", [128, D], F32) for i in range(2)])
            fst = Ring([sb(ffn, f"fst{i}", [128, 4], F32) for i in range(2)])

            S.op("sp", f_dma(s2b.t[:], modsave[0, :].partition_broadcast(128)), writes=[s2b.b], dma=s2b.b)
            S.op("sp", f_dma(b2b.t[:], modsave[1, :].partition_broadcast(128)), writes=[b2b.b], dma=b2b.b)
            S.op("sp", f_dma(fgb.t[:], final_g.partition_broadcast(128)), writes=[fgb.b], dma=fgb.b)
            S.op("sp", f_dma(fdw.t[:], fdw_fm[:, :, :]), writes=[fdw.b], dma=fdw.b)
            S.op("dve", f_memset(car.t[:], 0.0), writes=[car.b])
            for xt_ in xrf.tiles:
                S.op("pool", f_memset(xt_.t[:], 0.0), writes=[xt_.b])

            blocks = [(b * FB, FB) for b in range(SEQ // 2 // FB)] + [(SEQ // 2, 1)]
            for (t0, n) in blocks:
                h2T = h2Tr.next()
                for m in range((n + 127) // 128):
                    r0 = t0 + m * 128
                    nn = min(128, n - m * 128)
                    front(fr3, x1s[r0:r0 + nn, :], nn, s2b, b2b, dst=(h2T, h2T.t[:, :, m * 128:m * 128 + nn]))
                for j in range(NH):
                    pp = pbank[j % 4]
                    ab = abuf.next()
                    vb = vbuf.next()
                    for half in range(2):
                        col = half * HID + j * 128
                        for kc in range(8):
                            S.op("pe", f_mm(pp.t[:, half * 256:half * 256 + n], w_up_sb.t[:, kc, col:col + 128],
                                            h2T.t[:, kc, 0:n], kc == 0, kc == 7),
                                 reads=[h2T.b], extra=w_up_grp[j], writes=[pp.b])
                    S.op("pool", f_cp(ab.t[:, 0:2], car.t[:, j, 0:2]), reads=[car.b], writes=[ab.b])
                    S.op("pool", f_cp(vb.t[:, 0:1], car.t[:, j, 2:3]), reads=[car.b], writes=[vb.b])
                    S.op("act", f_act(ab.t[:, 2:2 + n], pp.t[:, 0:n], AF.Copy), reads=[pp.b], writes=[ab.b])
                    if CFG["ffn_bal"]:
                        S.op("dve", f_cp(vb.t[:, 1:1 + n], pp.t[:, 256:256 + n]), reads=[pp.b], writes=[vb.b])
                    else:
                        S.op("act", f_act(vb.t[:, 1:1 + n], pp.t[:, 256:256 + n], AF.Copy), reads=[pp.b], writes=[vb.b])
                    S.op("pool", f_cp(car.t[:, j, 0:2], ab.t[:, n:n + 2]), reads=[ab.b], writes=[car.b])
                    S.op("pool", f_cp(car.t[:, j, 2:3], vb.t[:, n:n + 1]), reads=[vb.b], writes=[car.b])
                    tc_ = tcv.next()
                    es = esg.next()
                    S.op("dve", f_ts(tc_.t[:, 0:n], ab.t[:, 0:n], fdw.t[:, j, 0:1], fdw.t[:, j, 3:4], ALU.mult, ALU.add),
                         reads=[ab.b, fdw.b], writes=[tc_.b])
                    S.op("dve", f_stt(tc_.t[:, 0:n], ab.t[:, 1:n + 1], fdw.t[:, j, 1:2], tc_.t[:, 0:n], ALU.mult, ALU.add),
                         reads=[ab.b, fdw.b, tc_.b], writes=[tc_.b])
                    S.op("dve", f_stt(tc_.t[:, 0:n], ab.t[:, 2:n + 2], fdw.t[:, j, 2:3], tc_.t[:, 0:n], ALU.mult, ALU.add),
                         reads=[ab.b, fdw.b, tc_.b], writes=[tc_.b])
                    act_sigmoid(es.t[:, 0:n], tc_.t[:, 0:n], es.b, tc_.b)
                    S.op("pool" if CFG["ffn_bal"] == 1 else "dve", f_tt(tc_.t[:, 0:n], tc_.t[:, 0:n], vb.t[:, 0:n], ALU.mult),
                         reads=[tc_.b, vb.b], writes=[tc_.b])
                    S.op("dve", f_tt(hidT.t[:, j, 0:n], tc_.t[:, 0:n], es.t[:, 0:n], ALU.mult),
                         reads=[tc_.b, es.b], writes=[hidT.b])
                for m in range((n + 127) // 128):
                    tk0 = t0 - 1 + m * 128
                    lo = max(0, -tk0)
                    hi = min(128, SEQ // 2 - tk0, n - m * 128)
                    if hi <= lo:
                        continue
                    xt_ = xrf.next()
                    st = fst.next()
                    S.op("sp", f_dma(xt_.t[lo:hi, :], x1s[tk0 + lo:tk0 + hi, :]), writes=[xt_.b], dma=xt_.b)
                    for half, pp in ((0, p5), (1, p6)):
                        for j in range(NH):
                            S.op("pe", f_mm(pp.t[0:hi, :], hidT.t[:, j, m * 128:m * 128 + hi],
                                            w_dn_sb.t[:, j, half * 512:(half + 1) * 512], j == 0, j == NH - 1),
                                 reads=[hidT.b, w_dn_regs[j]], writes=[pp.b])
                    for half, pp in ((0, p5), (1, p6)):
                        hs = slice(half * 512, (half + 1) * 512)
                        S.op("dve", f_tt(xt_.t[0:hi, hs], pp.t[0:hi, :], xt_.t[0:hi, hs], ALU.add), reads=[pp.b, xt_.b], writes=[xt_.b])
                    S.op("act", f_act(junk.t[0:hi, :], xt_.t[0:hi, :], AF.Square, accum=st.t[0:hi, 0:1]), reads=[xt_.b], writes=[st.b, junk.b])
                    S.op("act", f_act(st.t[0:hi, 1:2], st.t[0:hi, 0:1], AF.Ln, bias=EPS, scale=1.0 / D), reads=[st.b], writes=[st.b])
                    S.op("act", f_act(st.t[0:hi, 2:3], st.t[0:hi, 1:2], AF.Exp, scale=-0.5), reads=[st.b], writes=[st.b])
                    S.op("dve", f_stt(xt_.t[0:hi, :], xt_.t[0:hi, :], st.t[0:hi, 2:3], fgb.t[0:hi, :], ALU.mult, ALU.mult),
                         reads=[xt_.b, st.b, fgb.b], writes=[xt_.b])
                    S.op("sp", f_dma(y_out[tk0 + lo:tk0 + hi, :], xt_.t[lo:hi, :]), reads=[xt_.b], dma=xt_.b)
            S.barrier()
            S.flush()
    return nc


def _consts():
    idx = np.arange(128)
    same = (idx[:, None] // 64) == (idx[None, :] // 64)
    a, b = idx[:, None], idx[None, :]
    masks = np.stack([same & (a <= b), same & (a >= b), same & (a < b), same & (a > b)], axis=1).astype(np.float32)
    ind = ((idx[:, None] // 64) == np.arange(2)[None, :]).astype(np.float32)
    return np.eye(128, dtype=np.float32), np.ascontiguousarray(masks), ind


def _fm(v, nchunk):
    return np.ascontiguousarray(np.asarray(v, np.float32).reshape(nchunk, 128).T)


def make_in_maps(x, c, ctx, c_ctx, w_mod, b_mod, norm1_g, w_in, conv_dw, conv_b, conv_ln_g, conv_ln_b,
                 w_gf, b_gf, w_gb, b_gb, gla_norm_g, w_out, norm2_g, w_up, ffn_dw, ffn_dw_b, w_down, final_g):
    f = lambda a: np.asarray(a, dtype=np.float32)
    x, c, ctx, c_ctx = f(x), f(c), f(ctx), f(c_ctx)
    ident, masks, ind = _consts()
    w_in0 = f(w_in)[0]
    w_in_rev = np.ascontiguousarray(np.concatenate([w_in0[:, :C_Z], w_in0[:, C_Z + 16:C_Z + 32], w_in0[:, C_Z:C_Z + 16]], axis=1))
    cdw = f(conv_dw)[0]
    fdw_ = f(ffn_dw)[0]
    zeros16 = np.zeros((16, 256), np.float32)
    gate = {"f": (f(w_gf)[0], f(b_gf)[0]), "b": (f(w_gb)[0], f(b_gb)[0])}
    cvec = np.stack([_fm(f(conv_b)[0], 4), _fm(f(conv_ln_g)[0], 4), _fm(f(conv_ln_b)[0], 4)], axis=1)
    common = dict(
        cctx_fm=_fm(c_ctx, 8), w_mod=f(w_mod)[0], b_mod=f(b_mod)[0], norm1_g=f(norm1_g)[0],
        cvec_fm=np.ascontiguousarray(cvec), gla_norm_fm=_fm(f(gla_norm_g)[0], 4), w_out=f(w_out)[0], norm2_g=f(norm2_g)[0],
        w_up=f(w_up)[0], w_down=f(w_down)[0], final_g=f(final_g), ident=ident, masks=masks, ind=ind,
    )
    in_maps = []
    for core in range(8):
        b, rev = core // 2, core % 2
        xs = x[b][::-1] if rev else x[b]
        cs = ctx[b][::-1] if rev else ctx[b]
        cd = cdw[::-1] if rev else cdw
        fd = fdw_[::-1] if rev else fdw_
        pF, pB = ("b", "f") if rev else ("f", "b")
        wgF_aug = np.concatenate([gate[pF][0], zeros16, gate[pF][1][None, :]], axis=0)
        wgB_aug = np.concatenate([zeros16, gate[pB][0], gate[pB][1][None, :]], axis=0)
        cw_fm = np.ascontiguousarray(cd.T.reshape(4, 128, 31).transpose(1, 0, 2))
        fdw_fm = np.concatenate([fd.T.reshape(NH, 128, 3).transpose(1, 0, 2),
                                 f(ffn_dw_b)[0].reshape(NH, 128).T[:, :, None]], axis=2)
        m = dict(common)
        m.update(
            x_seq=np.ascontiguousarray(xs), ctx_seq=np.ascontiguousarray(cs), c_fm=_fm(c[b], 8),
            w_in=w_in_rev if rev else w_in0, cw_fm=cw_fm, wgF_aug=np.ascontiguousarray(wgF_aug),
            wgB_aug=np.ascontiguousarray(wgB_aug), fdw_fm=np.ascontiguousarray(fdw_fm),
        )
        in_maps.append(m)
    return in_maps


def kernel(**inputs):
    in_maps = make_in_maps(**inputs)
    nc = build_program()
    res = run_bass_kernel_spmd(nc, in_maps, core_ids=list(range(8)))
    out = np.empty((4, SEQ, D), np.float32)
    for core in range(8):
        b, rev = core // 2, core % 2
        y = np.asarray(res.results[core]["y_out"], np.float32)
        if rev:
            out[b, SEQ // 2:] = y[::-1]
        else:
            out[b, :SEQ // 2] = y
    return out
```
